# Optimizing a Trainium2 kernel written in Bass

```python
import jax
import jax.numpy as jnp
from jax import lax
import numpy as np


D_MODEL = 2048
BATCH = 2
SEQ = 16384
DEPTH = 4

PLE_DIM = 256
D_FF = 4 * D_MODEL
N_EVEN = (DEPTH + 1) // 2
N_ODD = DEPTH // 2
NORM_EPS = 1e-6

POOL_WINDOWS = (2, 4, 8, 16)
POOL_GROUP = D_MODEL // 16
POOL_WIDTH = POOL_GROUP * len(POOL_WINDOWS)

GLA_HEADS = 4
GLA_WIDTH = D_MODEL - POOL_WIDTH
GLA_DV = GLA_WIDTH // GLA_HEADS
GLA_DK = GLA_DV // 2
GLA_QK = GLA_HEADS * GLA_DK
GLA_GATE_RANK = 16
GLA_TAU = 16.0
GLA_CHUNK = 64
EVEN_SPLITS = (POOL_WIDTH, POOL_WIDTH + GLA_QK, POOL_WIDTH + 2 * GLA_QK,
               POOL_WIDTH + 2 * GLA_QK + GLA_WIDTH, POOL_WIDTH + 2 * GLA_QK + 2 * GLA_WIDTH)
EVEN_IN = POOL_WIDTH + 2 * GLA_QK + 2 * GLA_WIDTH + GLA_GATE_RANK
EVEN_MIX = POOL_WIDTH + GLA_WIDTH

RWKV_HEAD = 64
RWKV_WIDTH = D_MODEL // 2
RWKV_HEADS = RWKV_WIDTH // RWKV_HEAD
RWKV_DECAY_RANK = 64
RWKV_A_RANK = 64
RWKV_GATE_RANK = 160
RWKV_SPLITS = (RWKV_WIDTH, 2 * RWKV_WIDTH, 3 * RWKV_WIDTH, 3 * RWKV_WIDTH + RWKV_DECAY_RANK,
               3 * RWKV_WIDTH + RWKV_DECAY_RANK + RWKV_A_RANK)
RWKV_IN = 3 * RWKV_WIDTH + RWKV_DECAY_RANK + RWKV_A_RANK + RWKV_GATE_RANK
RWKV_LN_EPS = 64e-5

DIL_WIDTH = D_MODEL - RWKV_WIDTH
DIL_HEAD = 128
DIL_HEADS = DIL_WIDTH // DIL_HEAD
DIL_PATTERNS = ((128, 1), (512, 4), (2048, 16))
DIL_BLOCK = 128
ROPE_THETA = 10000.0
ODD_IN = RWKV_IN + 3 * DIL_WIDTH
ODD_MIX = RWKV_WIDTH + DIL_WIDTH

kernel_name = 'hybrid_pool_gla_rwkv7_dilated_trunk'


def rmsnorm(x, g):
    xf = x.astype(jnp.float32)
    y = xf * lax.rsqrt(jnp.mean(xf * xf, axis=-1, keepdims=True) + NORM_EPS)
    return (y * g.astype(jnp.float32)).astype(x.dtype)


def shift_prev(h):
    return jnp.pad(h[:, :-1], ((0, 0), (1, 0), (0, 0)))


def pool_mixer(u, pool_w, pool_scale):
    S = u.shape[1]
    uf = u.astype(jnp.float32)
    cs = jnp.cumsum(uf, axis=1)
    outs = []
    for gi, w in enumerate(POOL_WINDOWS):
        lo, hi = gi * POOL_GROUP, (gi + 1) * POOL_GROUP
        c = cs[:, :, lo:hi]
        c_lag = jnp.pad(c, ((0, 0), (w, 0), (0, 0)))[:, :S]
        cnt = jnp.minimum(jnp.arange(1, S + 1), w).astype(jnp.float32)[None, :, None]
        pooled = (c - c_lag) / cnt - uf[:, :, lo:hi]
        outs.append(jnp.einsum('bsc,cd->bsd', pooled.astype(u.dtype), pool_w[gi]))
    return jnp.concatenate(outs, axis=-1) * pool_scale


def gla_chunked(q, k, v, gk):
    B, S, H, DK = q.shape
    DV = v.shape[-1]
    C = GLA_CHUNK
    n = S // C

    def to_chunks(t):
        return t.astype(jnp.float32).reshape(B, n, C, H, t.shape[-1]).transpose(1, 0, 3, 2, 4)

    qc = to_chunks(q) * (DK ** -0.5)
    kc, vc, gc = to_chunks(k), to_chunks(v), to_chunks(gk)
    causal = jnp.tril(jnp.ones((C, C), dtype=bool))

    def step(state, inp):
        qi, ki, vi, gi = inp
        b = jnp.cumsum(gi, axis=2)
        b_last = b[:, :, -1:, :]
        o_inter = jnp.einsum('bhcd,bhde->bhce', qi * jnp.exp(b), state)
        diff = jnp.where(causal[:, :, None], b[:, :, :, None, :] - b[:, :, None, :, :], -jnp.inf)
        att = jnp.einsum('bhid,bhjd,bhijd->bhij', qi, ki, jnp.exp(diff))
        o = o_inter + jnp.einsum('bhij,bhje->bhie', att, vi)
        state = state * jnp.exp(b_last)[:, :, 0, :, None] + jnp.einsum(
            'bhcd,bhce->bhde', ki * jnp.exp(b_last - b), vi)
        return state, o

    s0 = jnp.zeros((B, H, DK, DV), jnp.float32)
    _, o = lax.scan(step, s0, (qc, kc, vc, gc))
    return o.transpose(1, 0, 3, 2, 4).reshape(B, S, H, DV)


def rwkv7_time_mix(hc, mu, w0, w2, a0, a2, g2, k_k, k_a, r_k, ln_w, ln_b):
    B, S, _ = hc.shape
    f32 = jnp.float32
    hs = hc + (shift_prev(hc) - hc) * mu
    r, k, v, hw, ha, hg = jnp.split(hs, RWKV_SPLITS, axis=-1)
    w_log = -jax.nn.softplus(-(w0 + jnp.tanh(hw) @ w2).astype(f32)) - 0.5
    decay = jnp.exp(-jnp.exp(w_log))
    a = jax.nn.sigmoid((a0 + ha @ a2).astype(f32))
    g = jax.nn.sigmoid(hg) @ g2

    def heads(t):
        return t.astype(f32).reshape(B, S, RWKV_HEADS, RWKV_HEAD)

    kk = heads(k * k_k)
    kk = kk / jnp.maximum(jnp.linalg.norm(kk, axis=-1, keepdims=True), 1e-12)
    k_mod = k.astype(f32) * (1.0 + (a - 1.0) * k_a)
    rh, kh, vh, wh, ah = heads(r), heads(k_mod), heads(v), heads(decay), heads(a)
    a_vec = -kk
    b_vec = kk * ah

    def tm(t):
        return jnp.moveaxis(t, 1, 0)

    def step(state, inp):
        r_t, w_t, k_t, v_t, a_t, b_t = inp
        sa = jnp.einsum('bhvk,bhk->bhv', state, a_t)
        state = (state * w_t[:, :, None, :] + sa[..., None] * b_t[:, :, None, :]
                 + v_t[..., None] * k_t[:, :, None, :])
        y = jnp.einsum('bhvk,bhk->bhv', state, r_t)
        return state, y

    s0 = jnp.zeros((B, RWKV_HEADS, RWKV_HEAD, RWKV_HEAD), f32)
    _, y = lax.scan(step, s0, (tm(rh), tm(wh), tm(kh), tm(vh), tm(a_vec), tm(b_vec)))
    y = jnp.moveaxis(y, 0, 1)
    mean = jnp.mean(y, axis=-1, keepdims=True)
    var = jnp.mean(jnp.square(y - mean), axis=-1, keepdims=True)
    yn = ((y - mean) * lax.rsqrt(var + RWKV_LN_EPS)).reshape(B, S, RWKV_WIDTH) * ln_w + ln_b
    bonus = (jnp.sum(rh * kh * r_k, axis=-1, keepdims=True) * vh).reshape(B, S, RWKV_WIDTH)
    return ((yn + bonus) * g).astype(hc.dtype)


def rope(x, pos):
    half = x.shape[-1] // 2
    inv = ROPE_THETA ** (-jnp.arange(half, dtype=jnp.float32) / half)
    ang = pos.astype(jnp.float32)[:, None] * inv[None, :]
    cos = jnp.cos(ang)[None, :, None, :]
    sin = jnp.sin(ang)[None, :, None, :]
    xf = x.astype(jnp.float32)
    x1, x2 = xf[..., :half], xf[..., half:]
    return jnp.concatenate([x1 * cos - x2 * sin, x2 * cos + x1 * sin], axis=-1).astype(x.dtype)


def dilated_branch(q, k, v, window, dil):
    B, H, S, Dh = q.shape
    L = S // dil
    nb = -(-L // DIL_BLOCK)
    Lp = nb * DIL_BLOCK
    n_back = window // dil

    def sub(t):
        t = t.astype(jnp.float32).reshape(B, H, L, dil, Dh).transpose(0, 1, 3, 2, 4)
        t = jnp.pad(t, ((0, 0), (0, 0), (0, 0), (0, Lp - L), (0, 0)))
        return t.reshape(B, H, dil, nb, DIL_BLOCK, Dh)

    def with_prev(t):
        prev = jnp.pad(t, ((0, 0), (0, 0), (0, 0), (1, 0), (0, 0), (0, 0)))[:, :, :, :-1]
        return jnp.concatenate([prev, t], axis=4)

    qs = sub(q)
    kb, vb = with_prev(sub(k)), with_prev(sub(v))
    s = jnp.einsum('bhrnqd,bhrnkd->bhrnqk', qs, kb) * (Dh ** -0.5)
    qi = jnp.arange(DIL_BLOCK)[:, None]
    ki = jnp.arange(2 * DIL_BLOCK)[None, :]
    dist = qi + DIL_BLOCK - ki
    blk = jnp.arange(nb)[:, None, None]
    valid = (dist >= 0) & (dist <= n_back) & (blk * DIL_BLOCK + ki - DIL_BLOCK >= 0)
    s = jnp.where(valid, s, -jnp.inf)
    m = jnp.max(s, axis=-1)
    pexp = jnp.exp(s - m[..., None])
    l = jnp.sum(pexp, axis=-1)
    acc = jnp.einsum('bhrnqk,bhrnkd->bhrnqd', pexp, vb)

    def unsub(t):
        extra = t.shape[5:]
        t = t.reshape((B, H, dil, Lp) + extra)[:, :, :, :L]
        t = jnp.moveaxis(t, 2, 3)
        return t.reshape((B, H, S) + extra)

    return unsub(m), unsub(l), unsub(acc)


def dilated_attention(q, k, v):
    branches = [dilated_branch(q, k, v, w, d) for (w, d) in DIL_PATTERNS]
    m_max = jnp.max(jnp.stack([br[0] for br in branches]), axis=0)
    num = jnp.zeros(q.shape, jnp.float32)
    den = jnp.zeros(q.shape[:-1], jnp.float32)
    for m, l, acc in branches:
        c = jnp.exp(m - m_max)
        num = num + c[..., None] * acc
        den = den + c * l
    return num / den[..., None]


def even_mixer(h, w_in, w_out, pool_w, pool_scale, gate_w2, gate_b, gla_norm):
    B, S, _ = h.shape
    z = h @ w_in
    u, q, k, v, gout, glr = jnp.split(z, EVEN_SPLITS, axis=-1)
    a_out = pool_mixer(u, pool_w, pool_scale)
    gk = jax.nn.log_sigmoid((glr @ gate_w2 + gate_b).astype(jnp.float32)) / GLA_TAU
    o = gla_chunked(q.reshape(B, S, GLA_HEADS, GLA_DK), k.reshape(B, S, GLA_HEADS, GLA_DK),
                    v.reshape(B, S, GLA_HEADS, GLA_DV), gk.reshape(B, S, GLA_HEADS, GLA_DK))
    o = rmsnorm(o, gla_norm) * jax.nn.silu(gout.reshape(B, S, GLA_HEADS, GLA_DV).astype(jnp.float32))
    mix = jnp.concatenate([a_out.astype(h.dtype), o.reshape(B, S, GLA_WIDTH).astype(h.dtype)], axis=-1)
    return mix @ w_out


def odd_mixer(h, w_in, w_out, mu, w0, w2, a0, a2, g2, k_k, k_a, r_k, ln_w, ln_b):
    B, S, _ = h.shape
    z = h @ w_in
    hc, hd = z[..., :RWKV_IN], z[..., RWKV_IN:]
    c_out = rwkv7_time_mix(hc, mu, w0, w2, a0, a2, g2, k_k, k_a, r_k, ln_w, ln_b)
    q, k, v = jnp.split(hd, 3, axis=-1)
    pos = jnp.arange(S)
    q = rope(q.reshape(B, S, DIL_HEADS, DIL_HEAD), pos).transpose(0, 2, 1, 3)
    k = rope(k.reshape(B, S, DIL_HEADS, DIL_HEAD), pos).transpose(0, 2, 1, 3)
    v = v.reshape(B, S, DIL_HEADS, DIL_HEAD).transpose(0, 2, 1, 3)
    d_out = dilated_attention(q, k, v).transpose(0, 2, 1, 3).reshape(B, S, DIL_WIDTH)
    mix = jnp.concatenate([c_out.astype(h.dtype), d_out.astype(h.dtype)], axis=-1)
    return mix @ w_out


def setup_inputs(seed: int = 0) -> dict:
    key = jax.random.key(seed)
    ks = iter(jax.random.split(key, 40))
    f32 = jnp.float32

    def nrm(shape, scale):
        return jax.random.normal(next(ks), shape, f32) * scale

    def gain(shape):
        return 1.0 + 0.05 * jax.random.normal(next(ks), shape, f32)

    return {
        'x': nrm((BATCH, SEQ, D_MODEL), 1.0),
        'p': nrm((DEPTH, BATCH, SEQ, PLE_DIM), 1.0),
        'norm_mix_pre': gain((DEPTH, D_MODEL)),
        'norm_mix_post': gain((DEPTH, D_MODEL)),
        'norm_ffn_pre': gain((DEPTH, D_MODEL)),
        'norm_ffn_post': gain((DEPTH, D_MODEL)),
        'ev_w_in': nrm((N_EVEN, D_MODEL, EVEN_IN), D_MODEL ** -0.5),
        'ev_w_out': nrm((N_EVEN, EVEN_MIX, D_MODEL), EVEN_MIX ** -0.5),
        'pool_w': nrm((N_EVEN, len(POOL_WINDOWS), POOL_GROUP, POOL_GROUP), POOL_GROUP ** -0.5),
        'pool_scale': gain((N_EVEN, POOL_WIDTH)),
        'gla_gate_w2': nrm((N_EVEN, GLA_GATE_RANK, GLA_QK), GLA_GATE_RANK ** -0.5),
        'gla_gate_b': nrm((N_EVEN, GLA_QK), 0.1),
        'gla_norm': gain((N_EVEN, GLA_DV)),
        'od_w_in': nrm((N_ODD, D_MODEL, ODD_IN), D_MODEL ** -0.5),
        'od_w_out': nrm((N_ODD, ODD_MIX, D_MODEL), ODD_MIX ** -0.5),
        'rwkv_mu': jax.random.uniform(next(ks), (N_ODD, RWKV_IN), f32),
        'rwkv_w0': nrm((N_ODD, RWKV_WIDTH), 0.5) - 0.5,
        'rwkv_w2': nrm((N_ODD, RWKV_DECAY_RANK, RWKV_WIDTH), 0.5 * RWKV_DECAY_RANK ** -0.5),
        'rwkv_a0': nrm((N_ODD, RWKV_WIDTH), 0.1),
        'rwkv_a2': nrm((N_ODD, RWKV_A_RANK, RWKV_WIDTH), 0.5 * RWKV_A_RANK ** -0.5),
        'rwkv_g2': nrm((N_ODD, RWKV_GATE_RANK, RWKV_WIDTH), RWKV_GATE_RANK ** -0.5),
        'rwkv_k_k': 0.85 + nrm((N_ODD, RWKV_WIDTH), 0.05),
        'rwkv_k_a': gain((N_ODD, RWKV_WIDTH)),
        'rwkv_r_k': nrm((N_ODD, RWKV_HEADS, RWKV_HEAD), 0.1),
        'rwkv_ln_w': gain((N_ODD, RWKV_WIDTH)),
        'rwkv_ln_b': nrm((N_ODD, RWKV_WIDTH), 0.01),
        'ffn_up': nrm((DEPTH, D_MODEL, D_FF), D_MODEL ** -0.5),
        'ffn_down': nrm((DEPTH, D_FF, D_MODEL), D_FF ** -0.5),
        'ple_proj': nrm((DEPTH, PLE_DIM, D_MODEL), PLE_DIM ** -0.5),
        'ple_gate': nrm((DEPTH, D_MODEL, D_MODEL), D_MODEL ** -0.5),
        'ple_norm': gain((DEPTH, D_MODEL)),
    }


def reference(x, p, norm_mix_pre, norm_mix_post, norm_ffn_pre, norm_ffn_post,
              ev_w_in, ev_w_out, pool_w, pool_scale, gla_gate_w2, gla_gate_b, gla_norm,
              od_w_in, od_w_out, rwkv_mu, rwkv_w0, rwkv_w2, rwkv_a0, rwkv_a2, rwkv_g2,
              rwkv_k_k, rwkv_k_a, rwkv_r_k, rwkv_ln_w, rwkv_ln_b,
              ffn_up, ffn_down, ple_proj, ple_gate, ple_norm):
    for i in range(DEPTH):
        j = i // 2
        h = rmsnorm(x, norm_mix_pre[i])
        if i % 2 == 0:
            y = even_mixer(h, ev_w_in[j], ev_w_out[j], pool_w[j], pool_scale[j],
                           gla_gate_w2[j], gla_gate_b[j], gla_norm[j])
        else:
            y = odd_mixer(h, od_w_in[j], od_w_out[j], rwkv_mu[j], rwkv_w0[j], rwkv_w2[j],
                          rwkv_a0[j], rwkv_a2[j], rwkv_g2[j], rwkv_k_k[j], rwkv_k_a[j],
                          rwkv_r_k[j], rwkv_ln_w[j], rwkv_ln_b[j])
        x = x + rmsnorm(y, norm_mix_post[i])
        h = rmsnorm(x, norm_ffn_pre[i])
        y = jnp.square(jax.nn.relu(h @ ffn_up[i])) @ ffn_down[i]
        x = x + rmsnorm(y, norm_ffn_post[i])
        gate = jax.nn.sigmoid(rmsnorm(x, ple_norm[i]) @ ple_gate[i])
        x = x + (p[i] @ ple_proj[i]) * gate
    return x
```

```python
import contextlib
import numpy as np
import ml_dtypes
import concourse.bass as bass
import concourse.mybir as mybir
from concourse.bass_utils import run_bass_kernel_spmd

F32 = mybir.dt.float32
BF16 = mybir.dt.bfloat16
AF = mybir.ActivationFunctionType
ALU = mybir.AluOpType
AX = mybir.AxisListType

D = 2048
DFF = 8192
EPS = 1e-6
EVEN_IN = 5136
ODD_IN = 6432


class Buf:
    __slots__ = ("name", "w", "r")

    def __init__(self, name="b"):
        self.name = name
        self.w = None
        self.r = []


class Sched:
    NDMA = 32

    def __init__(self, nc, es):
        self.nc = nc
        self.engs = {"pe": nc.tensor, "dve": nc.vector, "act": nc.scalar,
                     "pool": nc.gpsimd, "sp": nc.sync}
        self.sems = {}
        self.cnt = {}
        for k in self.engs:
            self.sems[k] = es.enter_context(nc.semaphore("s_" + k))
            self.cnt[k] = 0
        self.dsems = [es.enter_context(nc.semaphore("d%d" % i)) for i in range(self.NDMA)]
        self.dcnt = [0] * self.NDMA
        self.dnext = 0
        self.waited = {}
        self.ninst = 0

    def _semobj(self, key):
        return self.sems[key] if isinstance(key, str) else self.dsems[key]

    def _wait(self, eng, key, val):
        if val <= 0 or self.waited.get((eng, key), 0) >= val:
            return
        self.waited[(eng, key)] = val
        self.engs[eng].wait_ge(self._semobj(key), val)
        self.ninst += 1

    def _deps(self, eng, reads, writes):
        need = {}
        for b in reads:
            if b.w is not None:
                k, v = b.w
                if v > need.get(k, 0):
                    need[k] = v
        for b in writes:
            if b.w is not None:
                k, v = b.w
                if v > need.get(k, 0):
                    need[k] = v
            for k, v in b.r:
                if v > need.get(k, 0):
                    need[k] = v
        for k, v in need.items():
            if eng == "pe" and k == "pe":
                continue
            self._wait(eng, k, v)

    def _mark(self, tok, reads, writes):
        for b in writes:
            b.w = tok
            b.r = []
        for b in reads:
            if b in writes:
                continue
            b.r.append(tok)
            if len(b.r) > 16:
                best = {}
                for k, v in b.r:
                    if v > best.get(k, 0):
                        best[k] = v
                b.r = list(best.items())

    def op(self, eng, fn, reads=(), writes=()):
        self._deps(eng, reads, writes)
        ins = fn()
        self.cnt[eng] += 1
        ins.then_inc(self.sems[eng], 1)
        self.ninst += 1
        self._mark((eng, self.cnt[eng]), reads, writes)
        return ins

    def dma(self, eng, out, in_, reads=(), writes=()):
        i = self.dnext
        self.dnext = (self.dnext + 1) % self.NDMA
        self._wait(eng, i, self.dcnt[i])
        self._deps(eng, reads, writes)
        ins = self.engs[eng].dma_start(out=out, in_=in_)
        self.dcnt[i] += 16
        ins.then_inc(self.dsems[i], 16)
        self.ninst += 1
        self._mark((i, self.dcnt[i]), reads, writes)
        return ins

    def drain(self, eng="sp"):
        for i in range(self.NDMA):
            self._wait(eng, i, self.dcnt[i])
        for k in self.engs:
            self._wait(eng, k, self.cnt[k])


class Ring:
    def __init__(self, tiles):
        self.tiles = tiles
        self.bufs = [Buf() for _ in tiles]
        self.i = 0

    def next(self):
        t, b = self.tiles[self.i], self.bufs[self.i]
        self.i = (self.i + 1) % len(self.tiles)
        return t, b


class Ctx:
    pass


def build(L, layers=(0, 1, 2, 3), dbg=(), single=False):
    nc = bass.Bass("TRN2", target_bir_lowering=False)
    C = Ctx()
    C.nc = nc
    C.L = L
    nE = 1 if single else 2
    nO = 1 if single else 2
    nL = 1 if single else 4
    want_even = (not single) or (layers[0] % 2 == 0)
    want_odd = (not single) or (layers[0] % 2 == 1)
    C.LI = (lambda li: 0) if single else (lambda li: li)
    C.J = (lambda j: 0) if single else (lambda j: j)

    def din(name, shape, dt=F32):
        return nc.dram_tensor(name, list(shape), dt, kind="ExternalInput").ap()

    def dscr(name, shape, dt=F32):
        return nc.dram_tensor(name, list(shape), dt, kind="Internal").ap()

    I = {}
    I["xT"] = din("xT", [D, L])
    I["pT"] = din("pT", [nL, 256, L])
    for n in ("norm_mix_pre", "norm_mix_post", "norm_ffn_pre", "norm_ffn_post", "ple_norm"):
        I[n] = din(n, [nL, 128, 16])
    if want_even:
        I["ev_w_in"] = din("ev_w_in", [nE, D, EVEN_IN])
        I["ev_w_out"] = din("ev_w_out", [nE, D, D])
        I["pool_w"] = din("pool_w", [nE, 4, 128, 128])
        I["pool_scale"] = din("pool_scale", [nE, 128, 4])
        I["gla_gate_wb"] = din("gla_gate_wb", [nE, 17, 768])
        I["gla_norm"] = din("gla_norm", [nE, 128, 3])
    if want_odd:
        I["od_w_in"] = din("od_w_in", [nO, D, ODD_IN])
        I["od_w_out"] = din("od_w_out", [nO, D, D])
    I["ffn_up"] = din("ffn_up", [nL, D, DFF])
    I["ffn_down"] = din("ffn_down", [nL, DFF, D])
    I["ple_proj"] = din("ple_proj", [nL, 256, D])
    I["ple_gate"] = din("ple_gate", [nL, D, D])
    if want_odd:
        I["rwkv_mu"] = din("rwkv_mu", [nO, 1, 3360])
        I["rwkv_mu_lr"] = din("rwkv_mu_lr", [nO, 288, 1])
        I["rwkv_w2a"] = din("rwkv_w2a", [nO, 65, 1024])
        I["rwkv_a2a"] = din("rwkv_a2a", [nO, 65, 1024])
        I["rwkv_g2"] = din("rwkv_g2", [nO, 160, 1024])
        for n in ("rwkv_k_k", "rwkv_k_a", "rwkv_r_k", "rwkv_ln_w", "rwkv_ln_b"):
            I[n] = din(n, [nO, 1, 1024])
    I["c_su"] = din("c_su", [64, 64])
    I["c_sl"] = din("c_sl", [64, 64])
    I["c_i64"] = din("c_i64", [64, 64])
    I["c_sel"] = din("c_sel", [2, 128])
    I["c_ident"] = din("c_ident", [128, 128])
    I["c_cos"] = din("c_cos", [128, L])
    I["c_sin"] = din("c_sin", [128, L])
    I["c_dmask"] = din("c_dmask", [128, 256])
    I["c_tri"] = din("c_tri", [64, 64])
    I["c_trimask"] = din("c_trimask", [64, 64])
    I["c_poolcnt"] = din("c_poolcnt", [4, 128, 16])
    yT = nc.dram_tensor("yT", [D, L], F32, kind="ExternalOutput").ap()
    DBG = {}
    for name, shape in dbg:
        DBG[name] = nc.dram_tensor("dbg_" + name, list(shape), F32, kind="ExternalOutput").ap()

    X = dscr("xs", [D, L])
    HT = dscr("hT", [D, L], BF16)
    YT = dscr("ysc", [D, L])
    AT = dscr("aT", [DFF, L], BF16)
    MIXT = dscr("mixT", [D, L], BF16)
    Z = {}
    Z["u"] = dscr("z_u", [512, L])
    Z["q"] = dscr("z_q", [768, L])
    Z["k"] = dscr("z_k", [768, L])
    Z["ktok"] = dscr("z_ktok", [L, 768])
    Z["vtok"] = dscr("z_vtok", [L, 1536])
    Z["gout"] = dscr("z_gout", [1536, L])
    Z["glr"] = dscr("z_glr", [16, L])
    OZ = {}
    OZ["rkv"] = dscr("o_rkv", [L, 3072])
    OZ["lr"] = dscr("o_lr", [288, L])
    OZ["dq"] = dscr("o_dq", [1024, L])
    OZ["dk"] = dscr("o_dk", [1024, L])
    OZ["dv"] = dscr("o_dv", [L, 1024])
    OZ["tok4"] = [dscr("o_tok4_%d" % i, [L, 1024]) for i in range(4)]
    OZ["fm4"] = [dscr("o_fm4_%d" % i, [64, 16, L]) for i in range(4)]
    OZ["ytok"] = dscr("o_ytok", [L, 1024])
    OZ["bon"] = dscr("o_bon", [L, 1024])
    OZ["gt"] = dscr("o_gt", [L, 1024])
    OZ["yfm"] = dscr("o_yfm", [1024, L])
    OZ["qr"] = dscr("o_qr", [1024, L], BF16)
    OZ["kr"] = dscr("o_kr", [1024, L], BF16)
    OZ["negc"] = dscr("o_negc", [8, L], BF16)
    OZ["acc"] = dscr("o_acc", [3, L, 8, 132])
    C.OZ = OZ

    with contextlib.ExitStack() as es:
        S = Sched(nc, es)
        C.S = S
        cnt = [0]

        stack = [es]

        def sb(shape, dt=F32, name=None):
            cnt[0] += 1
            return stack[-1].enter_context(nc.sbuf_tensor("%s_%d" % (name or "t", cnt[0]), list(shape), dt))

        def ps(shape, dt=F32, name=None):
            cnt[0] += 1
            return es.enter_context(nc.psum_tensor("%s_%d" % (name or "p", cnt[0]), list(shape), dt))

        def ring(n, shape, dt=F32, name=None, psum=False):
            return Ring([(ps if psum else sb)(shape, dt, name) for _ in range(n)])

        def barrier():
            for e in S.engs:
                for k in S.engs:
                    if k != e:
                        S._wait(e, k, S.cnt[k])
                for i in range(S.NDMA):
                    S._wait(e, i, S.dcnt[i])
            for b in DB.values():
                b.w = None
                b.r = []

        @contextlib.contextmanager
        def phase():
            st = contextlib.ExitStack()
            stack.append(st)
            try:
                yield
            finally:
                barrier()
                stack.pop()
                st.close()

        C.phase = phase
        C.barrier = barrier
        DB = {}

        def db(ap_name):
            if ap_name not in DB:
                DB[ap_name] = Buf(ap_name)
            return DB[ap_name]

        ones_f = sb([128, 128], F32, "ones")
        b_ones = Buf()
        S.op("pool", lambda: nc.gpsimd.memset(ones_f[:], 1.0), writes=[b_ones])
        ones_b = sb([128, 128], BF16, "onesb")
        b_onesb = Buf()
        S.op("pool", lambda: nc.gpsimd.memset(ones_b[:], 1.0), writes=[b_onesb])
        tri_f = sb([64, 64], F32, "tri")
        b_tri = Buf()
        S.dma("sp", tri_f[:], I["c_tri"], writes=[b_tri])
        ident_f = sb([128, 128], F32, "identf")
        b_identf = Buf()
        S.dma("sp", ident_f[:], I["c_ident"], writes=[b_identf])
        C.ident_f, C.b_identf = ident_f, b_identf
        trimask = sb([64, 64], F32, "trimask")
        b_trimask = Buf()
        S.dma("sp", trimask[:], I["c_trimask"], writes=[b_trimask])
        gains = {}
        for n in ("norm_mix_pre", "norm_mix_post", "norm_ffn_pre", "norm_ffn_post", "ple_norm"):
            t = sb([128, nL, 16], F32, n)
            b = Buf()
            S.dma("sp", t[:], I[n].rearrange("l p c -> p l c"), writes=[b])
            gains[n] = (t, b)

        acc = ring(4, [128, 512], F32, "acc", psum=True)
        aux = ring(2, [128, 512], F32, "aux", psum=True)
        aux2 = ring(2, [128, 512], F32, "aux2", psum=True)

        conv_i = [0]

        def convert(out_ap, in_ap, reads, writes):
            k = conv_i[0] % 3
            conv_i[0] += 1
            if k == 0:
                S.op("act", lambda: nc.scalar.copy(out=out_ap, in_=in_ap), reads=reads, writes=writes)
            elif k == 1:
                S.op("pool", lambda: nc.gpsimd.tensor_copy(out=out_ap, in_=in_ap), reads=reads, writes=writes)
            else:
                S.op("dve", lambda: nc.vector.tensor_copy(out=out_ap, in_=in_ap), reads=reads, writes=writes)

        def dense(a_dram, a_buf, K, wfn, n0, n1, mode, evac, TT=512):
            KC = K // 128
            NBmax = 32768 // KC
            if mode == "tok":
                NBmax = min(NBmax, 2048)
            with phase():
                wstage = ring(2, [128, 2048], F32, "wstage")
                wblk = sb([128, 32768], BF16, "wblk")
                b_wblk = Buf()
                atile = ring(2, [128, KC * TT], BF16, "atile")
                C.evr = ring(3, [128, 512], F32, "evr")
                C.evb = ring(3, [128, 512], BF16, "evb")
                nb0 = n0
                while nb0 < n1:
                    NB = min(NBmax, n1 - nb0)
                    for kc in range(KC):
                        for s0 in range(0, NB, 2048):
                            ssz = min(2048, NB - s0)
                            st, stb = wstage.next()
                            S.dma("sp", st[:, :ssz], wfn(kc * 128, kc * 128 + 128, nb0 + s0, nb0 + s0 + ssz), writes=[stb])
                            convert(wblk[:, kc * NB + s0: kc * NB + s0 + ssz], st[:, :ssz], [stb], [b_wblk])
                    for t0 in range(0, L, TT):
                        at, atb = atile.next()
                        atv = at[:, :].rearrange("p (c t) -> p c t", t=TT)
                        S.dma("sp", atv[:, :, :], a_dram.rearrange("(c p) l -> p c l", p=128)[:, :, t0:t0 + TT],
                              reads=[a_buf], writes=[atb])
                        if mode == "fm":
                            for c0 in range(0, NB, 128):
                                csz = min(128, NB - c0)
                                pt, ptb = acc.next()
                                for kc in range(KC):
                                    S.op("pe", lambda: nc.tensor.matmul(
                                        pt[:csz, :TT], lhsT=wblk[:, kc * NB + c0: kc * NB + c0 + csz],
                                        rhs=atv[:, kc, :], start=(kc == 0), stop=(kc == KC - 1)),
                                        reads=[b_wblk, atb], writes=[ptb])
                                evac(nb0 + c0, csz, t0, TT, pt[:csz, :TT], ptb)
                        else:
                            for jj in range(TT // 128):
                                for c0 in range(0, NB, 512):
                                    csz = min(512, NB - c0)
                                    pt, ptb = acc.next()
                                    for kc in range(KC):
                                        S.op("pe", lambda: nc.tensor.matmul(
                                            pt[:, :csz], lhsT=atv[:, kc, jj * 128:(jj + 1) * 128],
                                            rhs=wblk[:, kc * NB + c0: kc * NB + c0 + csz],
                                            start=(kc == 0), stop=(kc == KC - 1)),
                                            reads=[b_wblk, atb], writes=[ptb])
                                    evac(t0 + jj * 128, nb0 + c0, csz, pt[:, :csz], ptb)
                    nb0 += NB

        ev_i = [0]

        def evac_copy(out_ap, in_ap, reads, writes, func=None):
            ev_i[0] += 1
            if func is not None:
                S.op("act", lambda: nc.scalar.activation(out=out_ap, in_=in_ap, func=func), reads=reads, writes=writes)
            elif ev_i[0] % 2:
                S.op("act", lambda: nc.scalar.copy(out=out_ap, in_=in_ap), reads=reads, writes=writes)
            else:
                S.op("dve", lambda: nc.vector.tensor_copy(out=out_ap, in_=in_ap), reads=reads, writes=writes)

        def store_fm(dst_dram, dst_buf, base, dt=F32):
            def f(c0, csz, t0, tsz, pap, pb):
                st, stb = (C.evr if dt == F32 else C.evb).next()
                evac_copy(st[:csz, :tsz], pap, [pb], [stb])
                S.dma("pool", dst_dram[c0 - base: c0 - base + csz, t0:t0 + tsz], st[:csz, :tsz],
                      reads=[stb], writes=[dst_buf])
            return f

        def store_tok(dst_dram, dst_buf, base):
            def f(t0, c0, csz, pap, pb):
                st, stb = C.evr.next()
                evac_copy(st[:, :csz], pap, [pb], [stb])
                S.dma("pool", dst_dram[t0:t0 + 128, c0 - base: c0 - base + csz], st[:, :csz],
                      reads=[stb], writes=[dst_buf])
            return f

        NT = 256
        eps_t = sb([128, 1], F32, "eps")
        b_eps = Buf()
        S.op("pool", lambda: nc.gpsimd.memset(eps_t[:], EPS), writes=[b_eps])
        C.eps_t, C.b_eps = eps_t, b_eps

        def rstd_of(NR, src_t, src_b, n, ntok, eps_ap=None):
            sq, sqb = NR["sq"].next()
            S.op("act", lambda: nc.scalar.activation(out=sq[:, :n, :ntok], in_=src_t[:, :n, :ntok], func=AF.Square),
                 reads=[src_b], writes=[sqb])
            pa, pab = aux.next()
            for c in range(n):
                S.op("pe", lambda: nc.tensor.matmul(pa[:, :ntok], lhsT=ones_f[:, :], rhs=sq[:, c, :ntok],
                                                    start=(c == 0), stop=(c == n - 1)),
                     reads=[b_ones, sqb], writes=[pab])
            rs, rsb = NR["rs"].next()
            S.op("act", lambda: nc.scalar.activation(out=rs[:, :ntok], in_=pa[:, :ntok], func=AF.Sqrt,
                                                     bias=eps_t[:, 0:1], scale=1.0 / (n * 128)),
                 reads=[pab, b_eps], writes=[rsb])
            S.op("dve", lambda: nc.vector.reciprocal(out=rs[:, :ntok], in_=rs[:, :ntok]), reads=[rsb], writes=[rsb])
            return rs, rsb
        C.rstd_of = rstd_of

        def fm3(ap):
            return ap.rearrange("(c p) l -> p c l", p=128)

        def scale_rows(dst_t, dst_b, src_t, src_b, g, gb, li, rs, rsb, n=16):
            for c in range(n):
                eng = "dve"
                e = nc.vector
                S.op(eng, lambda: e.scalar_tensor_tensor(out=dst_t[:, c, :], in0=src_t[:, c, :], scalar=g[:, li, c:c + 1],
                                                         in1=rs[:, :], op0=ALU.mult, op1=ALU.mult),
                     reads=[src_b, rsb, gb], writes=[dst_b])

        def norm_pass(src_dram, src_buf, gname, li, dst_dram, dst_buf):
            g, gb = gains[gname]
            with phase():
                xr = ring(2, [128, 16, NT], F32, "xr")
                hr = ring(2, [128, 16, NT], BF16, "hr")
                NR = dict(sq=ring(1, [128, 16, NT], F32, "sqr"), rs=ring(2, [128, NT], F32, "rsr"))
                for t0 in range(0, L, NT):
                    xt, xb = xr.next()
                    S.dma("sp", xt[:], fm3(src_dram)[:, :, t0:t0 + NT], reads=[src_buf], writes=[xb])
                    rs, rsb = rstd_of(NR, xt, xb, 16, NT)
                    ht, hb = hr.next()
                    scale_rows(ht, hb, xt, xb, g, gb, li, rs, rsb)
                    S.dma("pool", fm3(dst_dram)[:, :, t0:t0 + NT], ht[:], reads=[hb], writes=[dst_buf])

        def resid_norm_pass(x_src, x_src_buf, y_dram, y_buf, gpost, gpre, li, h_dram, h_buf, x_dst, x_dst_buf):
            g1, g1b = gains[gpost]
            g2, g2b = gains[gpre]
            with phase():
                xr = ring(2, [128, 16, NT], F32, "xr")
                yr = ring(2, [128, 16, NT], F32, "yr")
                hr = ring(2, [128, 16, NT], BF16, "hr")
                NR = dict(sq=ring(1, [128, 16, NT], F32, "sqr"), rs=ring(2, [128, NT], F32, "rsr"))
                for t0 in range(0, L, NT):
                    xt, xb = xr.next()
                    S.dma("sp", xt[:], fm3(x_src)[:, :, t0:t0 + NT], reads=[x_src_buf], writes=[xb])
                    yt, yb = yr.next()
                    S.dma("sp", yt[:], fm3(y_dram)[:, :, t0:t0 + NT], reads=[y_buf], writes=[yb])
                    rs, rsb = rstd_of(NR, yt, yb, 16, NT)
                    scale_rows(yt, yb, yt, yb, g1, g1b, li, rs, rsb)
                    S.op("dve", lambda: nc.vector.tensor_tensor(out=xt[:], in0=xt[:], in1=yt[:], op=ALU.add),
                         reads=[yb], writes=[xb])
                    S.dma("pool", fm3(x_dst)[:, :, t0:t0 + NT], xt[:], reads=[xb], writes=[x_dst_buf])
                    rs2, rs2b = rstd_of(NR, xt, xb, 16, NT)
                    ht, hb = hr.next()
                    scale_rows(ht, hb, xt, xb, g2, g2b, li, rs2, rs2b)
                    S.dma("pool", fm3(h_dram)[:, :, t0:t0 + NT], ht[:], reads=[hb], writes=[h_buf])

        C.sb, C.ps, C.ring = sb, ps, ring
        C.I, C.Z, C.db = I, Z, db
        C.ones_f, C.b_ones = ones_f, b_ones
        C.ones_b, C.b_onesb = ones_b, b_onesb
        C.aux, C.aux2, C.acc = aux, aux2, acc
        C.tri_f, C.b_tri, C.trimask, C.b_trimask = tri_f, b_tri, trimask, b_trimask
        C.MIXT = MIXT
        C.DBG = DBG

        def dbg_copy(name, src_dram, src_buf):
            if name in DBG:
                S.dma("pool", DBG[name], src_dram, reads=[src_buf], writes=[db("dbg_" + name)])
                barrier()
        C.dbg_copy = dbg_copy

        barrier()
        x_src, x_src_buf = I["xT"], db("xT_in")
        for li_real in layers:
            even = (li_real % 2 == 0)
            li = C.LI(li_real)
            j = C.J(li_real // 2)
            norm_pass(x_src, x_src_buf, "norm_mix_pre", li, HT, db("hT"))
            if even:
                W = I["ev_w_in"][j]
                wfn = lambda k0, k1, a, b, W=W: W[k0:k1, a:b]
                dense(HT, db("hT"), D, wfn, 0, 512, "fm", store_fm(Z["u"], db("z_u"), 0))
                dense(HT, db("hT"), D, wfn, 512, 1280, "fm", store_fm(Z["q"], db("z_q"), 512))
                dense(HT, db("hT"), D, wfn, 1280, 2048, "fm", store_fm(Z["k"], db("z_k"), 1280))
                dense(HT, db("hT"), D, wfn, 1280, 2048, "tok", store_tok(Z["ktok"], db("z_ktok"), 1280))
                dense(HT, db("hT"), D, wfn, 2048, 3584, "tok", store_tok(Z["vtok"], db("z_vtok"), 2048))
                dense(HT, db("hT"), D, wfn, 3584, 5120, "fm", store_fm(Z["gout"], db("z_gout"), 3584))
                dense(HT, db("hT"), D, wfn, 5120, 5136, "fm", store_fm(Z["glr"], db("z_glr"), 5120))
                dbg_copy("z_u", Z["u"], db("z_u"))
                dbg_copy("z_ktok", Z["ktok"], db("z_ktok"))
                pool_mixer(C, j)
                gla_mixer(C, j)
                wout = I["ev_w_out"][j]
            else:
                odd_inproj(C, j, dense, store_fm, store_tok, HT)
                rwkv_pre(C, j)
                rwkv_rec(C)
                rwkv_post(C, j)
                dil_rope(C)
                dil_attn(C)
                dil_combine(C)
                wout = I["od_w_out"][j]
            dbg_copy("mix%d" % li_real, MIXT, db("mixT"))
            dense(MIXT, db("mixT"), D, lambda k0, k1, a, b: wout[k0:k1, a:b], 0, D, "fm",
                  store_fm(YT, db("ysc"), 0))
            resid_norm_pass(x_src, x_src_buf, YT, db("ysc"), "norm_mix_post", "norm_ffn_pre", li,
                            HT, db("hT"), X, db("xs_w"))
            x_src, x_src_buf = X, db("xs")
            dbg_copy("xa%d" % li_real, X, db("xs"))
            wup = I["ffn_up"][li]

            def ev_up(c0, csz, t0, tsz, pap, pb):
                st, stb = C.evr.next()
                S.op("act", lambda: nc.scalar.activation(out=st[:csz, :tsz], in_=pap, func=AF.Relu), reads=[pb], writes=[stb])
                sb2, sb2b = C.evb.next()
                S.op("dve", lambda: nc.vector.tensor_tensor(out=sb2[:csz, :tsz], in0=st[:csz, :tsz], in1=st[:csz, :tsz], op=ALU.mult),
                     reads=[stb], writes=[sb2b])
                S.dma("pool", AT[c0:c0 + csz, t0:t0 + tsz], sb2[:csz, :tsz], reads=[sb2b], writes=[db("aT")])
            dense(HT, db("hT"), D, lambda k0, k1, a, b: wup[k0:k1, a:b], 0, DFF, "fm", ev_up)
            wdn = I["ffn_down"][li]
            dense(AT, db("aT"), DFF, lambda k0, k1, a, b: wdn[k0:k1, a:b], 0, D, "fm",
                  store_fm(YT, db("ysc"), 0), TT=256)
            resid_norm_pass(X, db("xs"), YT, db("ysc"), "norm_ffn_post", "ple_norm", li, HT, db("hT_w"), X, db("xs_w"))
            dbg_copy("xb%d" % li_real, X, db("xs"))
            ple_pass(C, li, dense, HT, X)
            dbg_copy("x%d" % li_real, X, db("xs"))
        S.dma("sp", yT, X, reads=[db("xs")], writes=[db("yT")])
        barrier()
        C.ninst = S.ninst
    return nc, C


def ple_pass(C, li, dense, HT, X):
    nc, S, L = C.nc, C.S, C.L
    I, db = C.I, C.db
    with C.phase():
        R = dict(
            pw_st=C.sb([128, 2, D], F32, "plew_st"), pw=C.sb([128, 2, D], BF16, "plew"), pwb=Buf(), pwsb=Buf(),
            pst=C.ring(2, [128, 2, 512], F32, "pst"), pbf=C.ring(2, [128, 2, 512], BF16, "pbf"),
            xc=C.ring(2, [128, 512], F32, "plex"), gt=C.ring(2, [128, 512], F32, "pleg"))
        S.dma("sp", R["pw_st"][:], I["ple_proj"][li].rearrange("(c p) n -> p c n", p=128), writes=[R["pwsb"]])
        S.op("dve", lambda: nc.vector.tensor_copy(out=R["pw"][:], in_=R["pw_st"][:]), reads=[R["pwsb"]], writes=[R["pwb"]])
        wg = I["ple_gate"][li]
        cur = {"t0": -1, "pb": None, "pbb": None}

        def ev(c0, csz, t0, tsz, pap, pb):
            if cur["t0"] != t0:
                st, stb = R["pst"].next()
                S.dma("sp", st[:, :, :tsz], I["pT"][li].rearrange("(c p) l -> p c l", p=128)[:, :, t0:t0 + tsz], writes=[stb])
                pbf, pbfb = R["pbf"].next()
                S.op("pool", lambda: nc.gpsimd.tensor_copy(out=pbf[:, :, :tsz], in_=st[:, :, :tsz]), reads=[stb], writes=[pbfb])
                cur["t0"], cur["pb"], cur["pbb"] = t0, pbf, pbfb
            pbf, pbfb = cur["pb"], cur["pbb"]
            pp, ppb = C.aux2.next()
            for kc in range(2):
                S.op("pe", lambda: nc.tensor.matmul(pp[:csz, :tsz], lhsT=R["pw"][:, kc, c0:c0 + csz], rhs=pbf[:, kc, :tsz],
                                                    start=(kc == 0), stop=(kc == 1)),
                     reads=[R["pwb"], pbfb], writes=[ppb])
            gt, gtb = R["gt"].next()
            S.op("act", lambda: nc.scalar.activation(out=gt[:csz, :tsz], in_=pap, func=AF.Sigmoid), reads=[pb], writes=[gtb])
            S.op("dve", lambda: nc.vector.tensor_tensor(out=gt[:csz, :tsz], in0=gt[:csz, :tsz], in1=pp[:csz, :tsz], op=ALU.mult),
                 reads=[ppb], writes=[gtb])
            xc, xcb = R["xc"].next()
            S.dma("sp", xc[:csz, :tsz], X[c0:c0 + csz, t0:t0 + tsz], reads=[db("xs")], writes=[xcb])
            S.op("pool", lambda: nc.gpsimd.tensor_tensor(out=xc[:csz, :tsz], in0=xc[:csz, :tsz], in1=gt[:csz, :tsz], op=ALU.add),
                 reads=[gtb], writes=[xcb])
            S.dma("pool", X[c0:c0 + csz, t0:t0 + tsz], xc[:csz, :tsz], reads=[xcb], writes=[db("xs_w")])
        dense(HT, db("hT"), D, lambda k0, k1, a, b: wg[k0:k1, a:b], 0, D, "fm", ev)


def pool_mixer(C, j):
    nc, S, L = C.nc, C.S, C.L
    I, Z, db = C.I, C.Z, C.db
    TT = 512
    with C.phase():
        R = dict(
            w_st=C.sb([128, 4, 128], F32, "poolw_st"), w=C.sb([128, 4, 128], BF16, "poolw"), wb=Buf(), wsb=Buf(),
            sc=C.sb([128, 4], F32, "poolsc"), scb=Buf(),
            cnt=C.sb([128, 4, 16], F32, "poolcnt"), cntb=Buf(),
            u=C.ring(2, [128, 16 + TT], F32, "pool_u"), s=C.ring(2, [128, 16 + TT], F32, "pool_s"),
            s2=C.ring(2, [128, 16 + TT], F32, "pool_s2"), tmp=C.ring(1, [128, 16], F32, "pool_tmp"),
            pb=C.ring(2, [128, TT], BF16, "pool_pb"), o=C.ring(2, [128, TT], BF16, "pool_o"))
        S.dma("sp", R["w_st"][:], I["pool_w"][j].rearrange("g c d -> c g d"), writes=[R["wsb"]])
        S.op("dve", lambda: nc.vector.tensor_copy(out=R["w"][:], in_=R["w_st"][:]), reads=[R["wsb"]], writes=[R["wb"]])
        S.dma("sp", R["sc"][:], I["pool_scale"][j], writes=[R["scb"]])
        S.dma("sp", R["cnt"][:], I["c_poolcnt"].rearrange("g p t -> p g t"), writes=[R["cntb"]])
        for gi, w in enumerate((2, 4, 8, 16)):
            for t0 in range(0, L, TT):
                u, ub = R["u"].next()
                if t0 == 0:
                    S.op("pool", lambda: nc.gpsimd.memset(u[:, 0:16], 0.0), writes=[ub])
                    S.dma("sp", u[:, 16:16 + TT], Z["u"][gi * 128:(gi + 1) * 128, 0:TT], reads=[db("z_u")], writes=[ub])
                else:
                    S.dma("sp", u[:, :], Z["u"][gi * 128:(gi + 1) * 128, t0 - 16:t0 + TT], reads=[db("z_u")], writes=[ub])
                a, ab = u, ub
                m = 1
                srcs = [R["s"], R["s2"]]
                k = 0
                while m < w:
                    d, dbf = srcs[k % 2].next()
                    k += 1
                    S.op("dve", lambda: nc.vector.memset(d[:, 0:m], 0.0), writes=[dbf])
                    S.op("dve", lambda: nc.vector.tensor_tensor(out=d[:, m:], in0=a[:, m:], in1=a[:, :16 + TT - m], op=ALU.add),
                         reads=[ab], writes=[dbf])
                    a, ab = d, dbf
                    m *= 2
                pbt, pbb = R["pb"].next()
                S.op("dve", lambda: nc.vector.scalar_tensor_tensor(out=pbt[:, :], in0=a[:, 16:], scalar=1.0 / w, in1=u[:, 16:],
                                                                   op0=ALU.mult, op1=ALU.subtract),
                     reads=[ab, ub], writes=[pbb])
                if t0 == 0:
                    tmp, tmpb = R["tmp"].next()
                    S.op("dve", lambda: nc.vector.tensor_tensor(out=tmp[:, 0:16], in0=a[:, 16:32], in1=R["cnt"][:, gi, :], op=ALU.mult),
                         reads=[ab, R["cntb"]], writes=[tmpb])
                    S.op("dve", lambda: nc.vector.tensor_tensor(out=pbt[:, 0:16], in0=tmp[:, 0:16], in1=u[:, 16:32], op=ALU.subtract),
                         reads=[tmpb, ub], writes=[pbb])
                pp, ppb = C.aux2.next()
                S.op("pe", lambda: nc.tensor.matmul(pp[:, :TT], lhsT=R["w"][:, gi, :], rhs=pbt[:, :], start=True, stop=True),
                     reads=[R["wb"], pbb], writes=[ppb])
                o, ob = R["o"].next()
                S.op("act", lambda: nc.scalar.activation(out=o[:, :], in_=pp[:, :TT], func=AF.Copy, scale=R["sc"][:, gi:gi + 1]),
                     reads=[ppb, R["scb"]], writes=[ob])
                S.dma("pool", C.MIXT[gi * 128:(gi + 1) * 128, t0:t0 + TT], o[:, :], reads=[ob], writes=[db("mixT")])


def gla_mixer(C, j):
    nc, S, L = C.nc, C.S, C.L
    I, Z, db = C.I, C.Z, C.db
    TT = 512
    NCH = TT // 64
    DK, DV = 192, 384
    DCH = ((0, 128), (128, 64))
    with C.phase():
        sb, ring = C.sb, C.ring
        gwb = sb([17, 768], F32, "gwb"); b_gwb = Buf()
        S.dma("sp", gwb[:], I["gla_gate_wb"][j], writes=[b_gwb])
        gnorm = sb([128, 3], F32, "gnorm"); b_gnorm = Buf()
        S.dma("sp", gnorm[:], I["gla_norm"][j], writes=[b_gnorm])
        tri_s = sb([64, 64], F32, "tri_s"); b_tris = Buf()
        S.op("dve", lambda: nc.vector.tensor_scalar_mul(out=tri_s[:], in0=C.tri_f[:], scalar1=-1.0 / 16.0),
             reads=[C.b_tri], writes=[b_tris])
        ones_s = sb([64, 64], F32, "ones_s"); b_oness = Buf()
        S.op("pool", lambda: nc.gpsimd.memset(ones_s[:], -1.0 / 16.0), writes=[b_oness])
        for h in range(4):
          with C.phase():
              St = [sb([sz, DV], F32, "S%d" % i) for i, (_, sz) in enumerate(DCH)]
              Sb = [sb([sz, DV], BF16, "Sb%d" % i) for i, (_, sz) in enumerate(DCH)]
              bS = [Buf(), Buf()]
              bSb = [Buf(), Buf()]
              for i in range(2):
                  S.op("pool", lambda: nc.gpsimd.memset(St[i][:], 0.0), writes=[bS[i]])
                  S.op("pool", lambda: nc.gpsimd.memset(Sb[i][:], 0.0), writes=[bSb[i]])
              glr_r = ring(2, [17, TT], F32, "glr")
              q_r = [ring(2, [sz, TT], F32, "q%d" % i) for i, (_, sz) in enumerate(DCH)]
              k_r = [ring(2, [sz, TT], F32, "k%d" % i) for i, (_, sz) in enumerate(DCH)]
              kt_r = ring(2, [64, NCH, DK], F32, "kt")
              vt_r = ring(2, [64, NCH, DV], F32, "vt")
              vb_r = ring(2, [64, NCH, DV], BF16, "vb")
              go_r = ring(2, [128, 3, TT], F32, "go")
              o_r = ring(2, [128, 3, TT], F32, "o")
              e_r = ring(2, [64, DK], F32, "e")
              l_r = ring(2, [64, DK], F32, "l")
              bt_r = ring(2, [64, DK], F32, "bt")
              kh_r = ring(2, [64, DK], BF16, "kh")
              eb_r = [ring(2, [sz, 64], F32, "eb%d" % i) for i, (_, sz) in enumerate(DCH)]
              enb_r = [ring(2, [sz, 64], F32, "enb%d" % i) for i, (_, sz) in enumerate(DCH)]
              qs_r = [ring(2, [sz, 64], BF16, "qs%d" % i) for i, (_, sz) in enumerate(DCH)]
              ks_r = [ring(2, [sz, 64], BF16, "ks%d" % i) for i, (_, sz) in enumerate(DCH)]
              dec_r = [ring(2, [sz, 1], F32, "dec%d" % i) for i, (_, sz) in enumerate(DCH)]
              at_r = ring(2, [64, 64], BF16, "attb")
              NR = dict(sq=ring(1, [128, 3, TT], F32, "gsq"), rs=ring(2, [128, TT], F32, "grs"))
              sg_r = ring(2, [128, 3, TT], F32, "sg")
              ob_r = ring(2, [128, 3, TT], BF16, "ob")
              for t0 in range(0, L, TT):
                  glr, glrb = glr_r.next()
                  S.op("pool", lambda: nc.gpsimd.memset(glr[:, :], 1.0), writes=[glrb])
                  S.dma("sp", glr[0:16, :], Z["glr"][:, t0:t0 + TT], reads=[db("z_glr")], writes=[glrb])
                  qt, qb, kt_, kb_ = [], [], [], []
                  for i, (d0, sz) in enumerate(DCH):
                      t, b = q_r[i].next()
                      S.dma("sp", t[:, :], Z["q"][h * DK + d0: h * DK + d0 + sz, t0:t0 + TT], reads=[db("z_q")], writes=[b])
                      qt.append(t); qb.append(b)
                      t, b = k_r[i].next()
                      S.dma("sp", t[:, :], Z["k"][h * DK + d0: h * DK + d0 + sz, t0:t0 + TT], reads=[db("z_k")], writes=[b])
                      kt_.append(t); kb_.append(b)
                  ktok, ktokb = kt_r.next()
                  S.dma("sp", ktok[:], Z["ktok"][t0:t0 + TT, h * DK:(h + 1) * DK].rearrange("(c p) d -> p c d", p=64),
                        reads=[db("z_ktok")], writes=[ktokb])
                  vtok, vtokb = vt_r.next()
                  S.dma("sp", vtok[:], Z["vtok"][t0:t0 + TT, h * DV:(h + 1) * DV].rearrange("(c p) d -> p c d", p=64),
                        reads=[db("z_vtok")], writes=[vtokb])
                  vb, vbb = vb_r.next()
                  S.op("pool", lambda: nc.gpsimd.tensor_copy(out=vb[:], in_=vtok[:]), reads=[vtokb], writes=[vbb])
                  go, gob = go_r.next()
                  S.dma("sp", go[:], Z["gout"][h * DV:(h + 1) * DV, t0:t0 + TT].rearrange("(c p) l -> p c l", p=128),
                        reads=[db("z_gout")], writes=[gob])
                  ot, otb = o_r.next()
                  for c in range(NCH):
                      cs = slice(c * 64, (c + 1) * 64)
                      pg, pgb = C.aux.next()
                      S.op("pe", lambda: nc.tensor.matmul(pg[:64, :DK], lhsT=glr[:, cs], rhs=gwb[:, h * DK:(h + 1) * DK],
                                                          start=True, stop=True), reads=[glrb, b_gwb], writes=[pgb])
                      e, eb_ = e_r.next()
                      S.op("act", lambda: nc.scalar.activation(out=e[:], in_=pg[:64, :DK], func=AF.Exp, scale=-1.0),
                           reads=[pgb], writes=[eb_])
                      l, lb = l_r.next()
                      S.op("act", lambda: nc.scalar.activation(out=l[:], in_=e[:], func=AF.Ln, bias=1.0),
                           reads=[eb_], writes=[lb])
                      pbk, pbkb = C.aux.next()
                      S.op("pe", lambda: nc.tensor.matmul(pbk[:64, :DK], lhsT=tri_s[:, :], rhs=l[:, :], start=True, stop=True),
                           reads=[b_tris, lb], writes=[pbkb])
                      pbl, pblb = C.aux2.next()
                      S.op("pe", lambda: nc.tensor.matmul(pbl[:64, :DK], lhsT=ones_s[:, :], rhs=l[:, :], start=True, stop=True),
                           reads=[b_oness, lb], writes=[pblb])
                      bt, btb = bt_r.next()
                      S.op("act", lambda: nc.scalar.copy(out=bt[:], in_=pbk[:64, :DK]), reads=[pbkb], writes=[btb])
                      S.op("dve", lambda: nc.vector.tensor_tensor(out=bt[:], in0=pbl[:64, :DK], in1=bt[:], op=ALU.subtract),
                           reads=[pblb], writes=[btb])
                      S.op("act", lambda: nc.scalar.activation(out=bt[:], in_=bt[:], func=AF.Exp), reads=[], writes=[btb])
                      kh, khb = kh_r.next()
                      S.op("dve", lambda: nc.vector.tensor_tensor(out=kh[:], in0=bt[:], in1=ktok[:, c, :], op=ALU.mult),
                           reads=[btb, ktokb], writes=[khb])
                      qs, qsb, ks, ksb, dec, decb = [], [], [], [], [], []
                      for i, (d0, sz) in enumerate(DCH):
                          pf, pfb = C.aux2.next()
                          S.op("pe", lambda: nc.tensor.matmul(pf[:sz, :64], lhsT=l[:, d0:d0 + sz], rhs=tri_s[:, :], start=True, stop=True),
                               reads=[lb, b_tris], writes=[pfb])
                          ebt, ebb = eb_r[i].next()
                          S.op("act", lambda: nc.scalar.activation(out=ebt[:], in_=pf[:sz, :64], func=AF.Exp), reads=[pfb], writes=[ebb])
                          enb, enbb = enb_r[i].next()
                          S.op("act", lambda: nc.scalar.activation(out=enb[:], in_=pf[:sz, :64], func=AF.Exp, scale=-1.0),
                               reads=[pfb], writes=[enbb])
                          d_, db_ = dec_r[i].next()
                          S.op("dve", lambda: nc.vector.tensor_copy(out=d_[:], in_=ebt[:, 63:64]), reads=[ebb], writes=[db_])
                          dec.append(d_); decb.append(db_)
                          q_, qb_ = qs_r[i].next()
                          S.op("dve", lambda: nc.vector.scalar_tensor_tensor(out=q_[:], in0=qt[i][:, cs], scalar=DK ** -0.5, in1=ebt[:],
                                                                             op0=ALU.mult, op1=ALU.mult),
                               reads=[qb[i], ebb], writes=[qb_])
                          qs.append(q_); qsb.append(qb_)
                          k_, kb2 = ks_r[i].next()
                          S.op("pool", lambda: nc.gpsimd.tensor_tensor(out=k_[:], in0=kt_[i][:, cs], in1=enb[:], op=ALU.mult),
                               reads=[kb_[i], enbb], writes=[kb2])
                          ks.append(k_); ksb.append(kb2)
                      pa, pab = C.acc.next()
                      for i in range(2):
                          S.op("pe", lambda: nc.tensor.matmul(pa[:64, :64], lhsT=ks[i][:, :], rhs=qs[i][:, :], start=(i == 0), stop=(i == 1)),
                               reads=[ksb[i], qsb[i]], writes=[pab])
                      att, attb = at_r.next()
                      S.op("dve", lambda: nc.vector.tensor_tensor(out=att[:], in0=pa[:64, :64], in1=C.trimask[:, :], op=ALU.mult),
                           reads=[pab, C.b_trimask], writes=[attb])
                      po, pob = C.acc.next()
                      for ec in range(3):
                          es_ = slice(ec * 128, (ec + 1) * 128)
                          S.op("pe", lambda: nc.tensor.matmul(po[:, ec * 64:(ec + 1) * 64], lhsT=vb[:, c, es_], rhs=att[:, :],
                                                              start=True, stop=False), reads=[vbb, attb], writes=[pob])
                          for i in range(2):
                              S.op("pe", lambda: nc.tensor.matmul(po[:, ec * 64:(ec + 1) * 64], lhsT=Sb[i][:, es_], rhs=qs[i][:, :],
                                                                  start=False, stop=(i == 1)), reads=[bSb[i], qsb[i]], writes=[pob])
                      S.op("act", lambda: nc.scalar.copy(out=ot[:, :, cs], in_=po[:, 0:192].rearrange("p (c i) -> p c i", i=64)),
                           reads=[pob], writes=[otb])
                      for i, (d0, sz) in enumerate(DCH):
                          pst, pstb = C.acc.next()
                          S.op("pe", lambda: nc.tensor.matmul(pst[:sz, :DV], lhsT=kh[:, d0:d0 + sz], rhs=vb[:, c, :], start=True, stop=True),
                               reads=[khb, vbb], writes=[pstb])
                          S.op("dve", lambda: nc.vector.scalar_tensor_tensor(out=St[i][:], in0=St[i][:], scalar=dec[i][:, 0:1], in1=pst[:sz, :DV],
                                                                             op0=ALU.mult, op1=ALU.add),
                               reads=[decb[i], pstb, bSb[i]], writes=[bS[i]])
                          S.op("pool", lambda: nc.gpsimd.tensor_copy(out=Sb[i][:], in_=St[i][:]), reads=[bS[i]], writes=[bSb[i]])
                  rs, rsb = C.rstd_of(NR, ot, otb, 3, TT)
                  sg, sgb = sg_r.next()
                  S.op("act", lambda: nc.scalar.activation(out=sg[:], in_=go[:], func=AF.Silu), reads=[gob], writes=[sgb])
                  ob, obb = ob_r.next()
                  for ec in range(3):
                      S.op("dve", lambda: nc.vector.scalar_tensor_tensor(out=sg[:, ec, :], in0=sg[:, ec, :], scalar=gnorm[:, ec:ec + 1],
                                                                         in1=rs[:, :], op0=ALU.mult, op1=ALU.mult),
                           reads=[rsb, b_gnorm], writes=[sgb])
                  S.op("pool", lambda: nc.gpsimd.tensor_tensor(out=ob[:], in0=sg[:], in1=ot[:], op=ALU.mult),
                       reads=[sgb, otb], writes=[obb])
                  S.dma("pool", C.MIXT[512 + h * DV: 512 + (h + 1) * DV, t0:t0 + TT].rearrange("(c p) l -> p c l", p=128), ob[:],
                        reads=[obb], writes=[db("mixT")])
                  if "gla_o" in C.DBG:
                      S.dma("sp", C.DBG["gla_o"][h * DV:(h + 1) * DV, t0:t0 + TT].rearrange("(c p) l -> p c l", p=128), ot[:],
                            reads=[otb], writes=[db("dbg_gla_o")])


def odd_inproj(C, j, dense, store_fm, store_tok, HT):
    I, db, OZ = C.I, C.db, C.OZ
    W = I["od_w_in"][j]
    wfn = lambda k0, k1, a, b: W[k0:k1, a:b]
    dense(HT, db("hT"), D, wfn, 0, 3072, "tok", store_tok(OZ["rkv"], db("o_rkv"), 0))
    dense(HT, db("hT"), D, wfn, 3072, 3360, "fm", store_fm(OZ["lr"], db("o_lr"), 3072))
    dense(HT, db("hT"), D, wfn, 3360, 4384, "fm", store_fm(OZ["dq"], db("o_dq"), 3360))
    dense(HT, db("hT"), D, wfn, 4384, 5408, "fm", store_fm(OZ["dk"], db("o_dk"), 4384))
    dense(HT, db("hT"), D, wfn, 5408, 6432, "tok", store_tok(OZ["dv"], db("o_dv"), 5408))


def bc3(ap16):
    return ap16.unsqueeze(2).to_broadcast([128, 16, 64])


def v3(ap):
    return ap.rearrange("p (h k) -> p h k", k=64)


def rwkv_pre(C, j):
    nc, S, L = C.nc, C.S, C.L
    I, db, OZ = C.I, C.db, C.OZ
    sb, ring = C.sb, C.ring
    with C.phase():
        def bcast(name, src, n):
            t = sb([128, n], F32, name); b = Buf()
            S.dma("sp", t[:], src.partition_broadcast(128), writes=[b])
            return t, b
        mu, mub = bcast("mu", I["rwkv_mu"][j][:, 0:3072], 3072)
        kk_, kkb = bcast("kk_", I["rwkv_k_k"][j], 1024)
        ka_, kab = bcast("ka_", I["rwkv_k_a"][j], 1024)
        rk_, rkb = bcast("rk_", I["rwkv_r_k"][j], 1024)
        w2a = sb([65, 1024], F32, "w2a"); w2ab = Buf()
        S.dma("sp", w2a[:], I["rwkv_w2a"][j], writes=[w2ab])
        a2a = sb([65, 1024], F32, "a2a"); a2ab = Buf()
        S.dma("sp", a2a[:], I["rwkv_a2a"][j], writes=[a2ab])
        g2a = sb([128, 1024], F32, "g2a"); g2ab = Buf()
        S.dma("sp", g2a[:], I["rwkv_g2"][j][0:128, :], writes=[g2ab])
        g2b = sb([32, 1024], F32, "g2b"); g2bb = Buf()
        S.dma("sp", g2b[:], I["rwkv_g2"][j][128:160, :], writes=[g2bb])
        LRS = ((0, 64), (64, 64), (128, 128), (256, 32))
        mul = []
        for i, (r0, sz) in enumerate(LRS):
            t = sb([sz, 1], F32, "mul%d" % i); b = Buf()
            S.dma("sp", t[:], I["rwkv_mu_lr"][j][r0:r0 + sz, :], writes=[b])
            mul.append((t, b))
        ident, identb = C.ident_f, C.b_identf
        cur_r = ring(2, [128, 3072], F32, "cur")
        prev_r = ring(2, [128, 3072], F32, "prev")
        hs_r = ring(1, [128, 3072], F32, "hs")
        lr_r = [ring(2, [sz, 129], F32, "lr%d" % i) for i, (_, sz) in enumerate(LRS)]
        ls_r = [ring(1, [sz, 128], F32, "ls%d" % i) for i, (_, sz) in enumerate(LRS)]
        th = sb([65, 128], F32, "th"); thb = Buf()
        S.op("pool", lambda: nc.gpsimd.memset(th[:], 1.0), writes=[thb])
        haa = sb([65, 128], F32, "haa"); haab = Buf()
        S.op("pool", lambda: nc.gpsimd.memset(haa[:], 1.0), writes=[haab])
        sg0 = sb([128, 128], F32, "sg0"); sg0b = Buf()
        sg1 = sb([32, 128], F32, "sg1"); sg1b = Buf()
        T = {n: ring(1, [128, 1024], F32, n) for n in ("dec", "a", "g", "kk", "tmp", "kmod", "avec", "bvec", "tmp2", "bon")}
        small = {n: ring(1, [128, 16], F32, n) for n in ("ss", "rn", "rkk")}
        fm_r = ring(2, [64, 16, 128], F32, "fmq")
        lw_r = ring(1, [128, 1024], F32, "lw")
        prr = Ring(C.acc.tiles + C.aux.tiles + C.aux2.tiles)
        evac_i = [0]
        for t0 in range(0, L, 128):
            cur, curb = cur_r.next()
            S.dma("sp", cur[:], OZ["rkv"][t0:t0 + 128, :], reads=[db("o_rkv")], writes=[curb])
            prev, prevb = prev_r.next()
            if t0 == 0:
                S.op("pool", lambda: nc.gpsimd.memset(prev[:], 0.0), writes=[prevb])
                S.dma("sp", prev[1:128, :], OZ["rkv"][0:127, :], reads=[db("o_rkv")], writes=[prevb])
            else:
                S.dma("sp", prev[:], OZ["rkv"][t0 - 1:t0 + 127, :], reads=[db("o_rkv")], writes=[prevb])
            hs, hsb = hs_r.next()
            S.op("dve", lambda: nc.vector.tensor_tensor(out=hs[:], in0=prev[:], in1=cur[:], op=ALU.subtract),
                 reads=[prevb, curb], writes=[hsb])
            S.op("pool", lambda: nc.gpsimd.tensor_tensor(out=hs[:], in0=hs[:], in1=mu[:], op=ALU.mult), reads=[mub], writes=[hsb])
            S.op("dve", lambda: nc.vector.tensor_tensor(out=hs[:], in0=hs[:], in1=cur[:], op=ALU.add), reads=[curb], writes=[hsb])
            r_, k_, v_ = hs[:, 0:1024], hs[:, 1024:2048], hs[:, 2048:3072]
            ls = []
            for i, (r0, sz) in enumerate(LRS):
                lt, ltb = lr_r[i].next()
                if t0 == 0:
                    S.op("pool", lambda: nc.gpsimd.memset(lt[:, 0:1], 0.0), writes=[ltb])
                    S.dma("sp", lt[:, 1:129], OZ["lr"][r0:r0 + sz, 0:128], reads=[db("o_lr")], writes=[ltb])
                else:
                    S.dma("sp", lt[:, :], OZ["lr"][r0:r0 + sz, t0 - 1:t0 + 128], reads=[db("o_lr")], writes=[ltb])
                st, stb = ls_r[i].next()
                S.op("dve", lambda: nc.vector.tensor_tensor(out=st[:], in0=lt[:, 0:128], in1=lt[:, 1:129], op=ALU.subtract),
                     reads=[ltb], writes=[stb])
                S.op("dve", lambda: nc.vector.scalar_tensor_tensor(out=st[:], in0=st[:], scalar=mul[i][0][:, 0:1], in1=lt[:, 1:129],
                                                                   op0=ALU.mult, op1=ALU.add), reads=[ltb, mul[i][1]], writes=[stb])
                ls.append((st, stb))
            S.op("act", lambda: nc.scalar.activation(out=th[0:64, :], in_=ls[0][0][:], func=AF.Tanh), reads=[ls[0][1]], writes=[thb])
            S.op("act", lambda: nc.scalar.copy(out=haa[0:64, :], in_=ls[1][0][:]), reads=[ls[1][1]], writes=[haab])
            S.op("act", lambda: nc.scalar.activation(out=sg0[:], in_=ls[2][0][:], func=AF.Sigmoid), reads=[ls[2][1]], writes=[sg0b])
            S.op("act", lambda: nc.scalar.activation(out=sg1[:], in_=ls[3][0][:], func=AF.Sigmoid), reads=[ls[3][1]], writes=[sg1b])
            dec, decb = T["dec"].next()
            a_, ab = T["a"].next()
            g_, gb = T["g"].next()
            for hf in range(2):
                cs = slice(hf * 512, (hf + 1) * 512)
                p1, p1b = prr.next()
                S.op("pe", lambda: nc.tensor.matmul(p1[:, :], lhsT=th[:, :], rhs=w2a[:, cs], start=True, stop=True),
                     reads=[thb, w2ab], writes=[p1b])
                S.op("act", lambda: nc.scalar.activation(out=dec[:, cs], in_=p1[:, :], func=AF.Sigmoid), reads=[p1b], writes=[decb])
                p2, p2b = prr.next()
                S.op("pe", lambda: nc.tensor.matmul(p2[:, :], lhsT=haa[:, :], rhs=a2a[:, cs], start=True, stop=True),
                     reads=[haab, a2ab], writes=[p2b])
                S.op("act", lambda: nc.scalar.activation(out=a_[:, cs], in_=p2[:, :], func=AF.Sigmoid), reads=[p2b], writes=[ab])
                p3, p3b = prr.next()
                S.op("pe", lambda: nc.tensor.matmul(p3[:, :], lhsT=sg0[:, :], rhs=g2a[:, cs], start=True, stop=False),
                     reads=[sg0b, g2ab], writes=[p3b])
                S.op("pe", lambda: nc.tensor.matmul(p3[:, :], lhsT=sg1[:, :], rhs=g2b[:, cs], start=False, stop=True),
                     reads=[sg1b, g2bb], writes=[p3b])
                S.op("dve", lambda: nc.vector.tensor_copy(out=g_[:, cs], in_=p3[:, :]), reads=[p3b], writes=[gb])
            lw, lwb = lw_r.next()
            S.op("act", lambda: nc.scalar.mul(out=lw[:], in_=dec[:], mul=-float(np.exp(-0.5))), reads=[decb], writes=[lwb])
            S.dma("pool", OZ["gt"][t0:t0 + 128, :], g_[:], reads=[gb], writes=[db("o_gt")])
            kk, kkb2 = T["kk"].next()
            S.op("dve", lambda: nc.vector.tensor_tensor(out=kk[:], in0=k_, in1=kk_[:], op=ALU.mult), reads=[hsb, kkb], writes=[kkb2])
            tmp, tmpb = T["tmp"].next()
            S.op("pool", lambda: nc.gpsimd.tensor_tensor(out=tmp[:], in0=kk[:], in1=kk[:], op=ALU.mult), reads=[kkb2], writes=[tmpb])
            ss, ssb = small["ss"].next()
            S.op("dve", lambda: nc.vector.tensor_reduce(out=ss[:], in_=v3(tmp[:]), axis=AX.X, op=ALU.add), reads=[tmpb], writes=[ssb])
            S.op("act", lambda: nc.scalar.activation(out=ss[:], in_=ss[:], func=AF.Sqrt), reads=[], writes=[ssb])
            S.op("dve", lambda: nc.vector.tensor_scalar_max(out=ss[:], in0=ss[:], scalar1=1e-12), reads=[], writes=[ssb])
            S.op("dve", lambda: nc.vector.reciprocal(out=ss[:], in_=ss[:]), reads=[], writes=[ssb])
            S.op("dve", lambda: nc.vector.tensor_tensor(out=v3(kk[:]), in0=v3(kk[:]), in1=bc3(ss[:]), op=ALU.mult), reads=[ssb], writes=[kkb2])
            km, kmb = T["kmod"].next()
            S.op("dve", lambda: nc.vector.scalar_tensor_tensor(out=km[:], in0=a_[:], scalar=-1.0, in1=ka_[:], op0=ALU.add, op1=ALU.mult),
                 reads=[ab, kab], writes=[kmb])
            S.op("dve", lambda: nc.vector.scalar_tensor_tensor(out=km[:], in0=km[:], scalar=1.0, in1=k_, op0=ALU.add, op1=ALU.mult),
                 reads=[hsb], writes=[kmb])
            av, avb = T["avec"].next()
            S.op("act", lambda: nc.scalar.mul(out=av[:], in_=kk[:], mul=-1.0), reads=[kkb2], writes=[avb])
            bv, bvb = T["bvec"].next()
            S.op("pool", lambda: nc.gpsimd.tensor_tensor(out=bv[:], in0=kk[:], in1=a_[:], op=ALU.mult), reads=[kkb2, ab], writes=[bvb])
            t2, t2b = T["tmp2"].next()
            S.op("pool", lambda: nc.gpsimd.tensor_tensor(out=t2[:], in0=r_, in1=km[:], op=ALU.mult), reads=[hsb, kmb], writes=[t2b])
            S.op("pool", lambda: nc.gpsimd.tensor_tensor(out=t2[:], in0=t2[:], in1=rk_[:], op=ALU.mult), reads=[rkb], writes=[t2b])
            rkk, rkkb = small["rkk"].next()
            S.op("dve", lambda: nc.vector.tensor_reduce(out=rkk[:], in_=v3(t2[:]), axis=AX.X, op=ALU.add), reads=[t2b], writes=[rkkb])
            bon, bonb = T["bon"].next()
            S.op("dve", lambda: nc.vector.tensor_tensor(out=v3(bon[:]), in0=v3(v_), in1=bc3(rkk[:]), op=ALU.mult),
                 reads=[hsb, rkkb], writes=[bonb])
            S.dma("pool", OZ["bon"][t0:t0 + 128, :], bon[:], reads=[bonb], writes=[db("o_bon")])
            for qi, (tq, tqb) in enumerate(((lw[:], lwb), (bv[:], bvb), (km[:], kmb), (v_, hsb))):
                S.dma("pool", OZ["tok4"][qi][t0:t0 + 128, :], tq, reads=[tqb], writes=[db("o_tok4")])
            for qi, (tq, tqb) in enumerate(((av[:], avb), (bv[:], bvb), (km[:], kmb), (r_, hsb))):
                ft, ftb = fm_r.next()
                for g4 in range(4):
                    pt, ptb = prr.next()
                    for q4 in range(4):
                        h = g4 * 4 + q4
                        S.op("pe", lambda: nc.tensor.transpose(out=pt[:64, q4 * 128:(q4 + 1) * 128], in_=tq[:, h * 64:(h + 1) * 64],
                                                               identity=ident[:]), reads=[tqb, identb], writes=[ptb])
                    evac_i[0] += 1
                    if evac_i[0] % 2:
                        S.op("act", lambda: nc.scalar.copy(out=ft[:, g4 * 4:(g4 + 1) * 4, :], in_=pt[:64, :].rearrange("p (c t) -> p c t", t=128)),
                             reads=[ptb], writes=[ftb])
                    else:
                        S.op("dve", lambda: nc.vector.tensor_copy(out=ft[:, g4 * 4:(g4 + 1) * 4, :], in_=pt[:64, :].rearrange("p (c t) -> p c t", t=128)),
                             reads=[ptb], writes=[ftb])
                S.dma("pool", OZ["fm4"][qi][:, :, t0:t0 + 128], ft[:], reads=[ftb], writes=[db("o_fm4")])


def rwkv_rec(C):
    nc, S, L = C.nc, C.S, C.L
    I, db, OZ = C.I, C.db, C.OZ
    sb, ring = C.sb, C.ring
    CH = 64
    with C.phase():
        def const(name):
            t = sb([64, 64], F32, name); b = Buf()
            S.dma("sp", t[:], I[name], writes=[b])
            return t, b
        tri_i, tri_ib = C.tri_f, C.b_tri
        tri_s, tri_sb = const("c_su")
        m_sl, m_slb = const("c_sl")
        m_id, m_idb = const("c_i64")

        def bc8(m):
            return m[:, :].unsqueeze(1).to_broadcast([64, 8, 64])
        ST = sb([64, 16, 64], F32, "ST"); STb = Buf()
        S.op("pool", lambda: nc.gpsimd.memset(ST[:], 0.0), writes=[STb])
        pr = Ring(C.acc.tiles + C.aux.tiles + C.aux2.tiles)
        fin_r = [ring(2, [64, 16, 64], F32, "fin%d" % i) for i in range(4)]
        tin_r = [ring(2, [64, 1024], F32, "tin%d" % i) for i in range(4)]
        names = ("G", "Gi", "Ge", "At", "Bt", "Kt", "Rt", "Q", "IQ", "P", "Q2", "IQ2", "P2", "Mak", "Lrb", "Lrk", "X", "X2", "Dt")
        W = {n: (sb([64, 16, 64], F32, "w" + n), Buf()) for n in names}
        TT_ = {n: (sb([64, 1024], F32, "t" + n), Buf()) for n in ("GiT", "Btok", "Ktok", "Y")}

        def hv(t, h):
            return t[:, h, :]

        def mm16(terms):
            banks = []
            for half in range(2):
                p, pb = pr.next()
                for hh in range(8):
                    h = half * 8 + hh
                    for idx, (lf, rf, rd) in enumerate(terms):
                        S.op("pe", lambda: nc.tensor.matmul(p[:64, hh * 64:(hh + 1) * 64], lhsT=lf(h), rhs=rf(h),
                                                            start=(idx == 0), stop=(idx == len(terms) - 1)),
                             reads=rd, writes=[pb])
                banks.append((p, pb))
            return banks

        def pv(p):
            return p[:64, :].rearrange("p (c k) -> p c k", k=64)
        ei = [0]

        def ev_copy(banks, dst, dstb):
            for half, (p, pb) in enumerate(banks):
                ei[0] += 1
                o = dst[:, half * 8:(half + 1) * 8, :]
                if ei[0] % 2:
                    S.op("act", lambda: nc.scalar.copy(out=o, in_=pv(p)), reads=[pb], writes=[dstb])
                else:
                    S.op("dve", lambda: nc.vector.tensor_copy(out=o, in_=pv(p)), reads=[pb], writes=[dstb])

        def ev_mask(banks, dst, dstb, m, mb):
            for half, (p, pb) in enumerate(banks):
                o = dst[:, half * 8:(half + 1) * 8, :]
                S.op("dve", lambda: nc.vector.tensor_tensor(out=o, in0=pv(p), in1=bc8(m), op=ALU.mult), reads=[pb, mb], writes=[dstb])

        def add_id(src, srcb, dst, dstb):
            for half in range(2):
                sl = slice(half * 8, (half + 1) * 8)
                S.op("pool", lambda: nc.gpsimd.tensor_tensor(out=dst[:, sl, :], in0=src[:, sl, :], in1=bc8(m_id), op=ALU.add),
                     reads=[srcb, m_idb], writes=[dstb])

        for t0 in range(0, L, CH):
            fin = []
            for qi in range(4):
                t, b = fin_r[qi].next()
                S.dma("sp", t[:], OZ["fm4"][qi][:, :, t0:t0 + CH], reads=[db("o_fm4")], writes=[b])
                fin.append((t, b))
            tin = []
            for qi in range(4):
                t, b = tin_r[qi].next()
                S.dma("sp", t[:], OZ["tok4"][qi][t0:t0 + CH, :], reads=[db("o_tok4")], writes=[b])
                tin.append((t, b))
            (aT, aTb), (bT, bTb), (kT, kTb), (rT, rTb) = fin
            (lwk, lwkb), (btk, btkb), (ktk, ktkb), (vtk, vtkb) = tin
            bLW = mm16([(lambda h: lwk[:, h * 64:(h + 1) * 64], lambda h: tri_i[:, :], [lwkb, tri_ib])])
            bLE = mm16([(lambda h: lwk[:, h * 64:(h + 1) * 64], lambda h: tri_s[:, :], [lwkb, tri_sb])])
            (G, Gb), (Gi, Gib), (Ge, Geb) = W["G"], W["Gi"], W["Ge"]
            for half in range(2):
                sl = slice(half * 8, (half + 1) * 8)
                S.op("act", lambda: nc.scalar.activation(out=G[:, sl, :], in_=pv(bLW[half][0]), func=AF.Exp), reads=[bLW[half][1]], writes=[Gb])
                S.op("act", lambda: nc.scalar.activation(out=Gi[:, sl, :], in_=pv(bLW[half][0]), func=AF.Exp, scale=-1.0),
                     reads=[bLW[half][1]], writes=[Gib])
                S.op("act", lambda: nc.scalar.activation(out=Ge[:, sl, :], in_=pv(bLE[half][0]), func=AF.Exp), reads=[bLE[half][1]], writes=[Geb])
            (At, Atb), (Bt, Btb), (Kt, Ktb), (Rt, Rtb) = W["At"], W["Bt"], W["Kt"], W["Rt"]
            S.op("dve", lambda: nc.vector.tensor_tensor(out=At[:], in0=aT[:], in1=Ge[:], op=ALU.mult), reads=[aTb, Geb], writes=[Atb])
            S.op("pool", lambda: nc.gpsimd.tensor_tensor(out=Bt[:], in0=bT[:], in1=Gi[:], op=ALU.mult), reads=[bTb, Gib], writes=[Btb])
            S.op("dve", lambda: nc.vector.tensor_tensor(out=Kt[:], in0=kT[:], in1=Gi[:], op=ALU.mult), reads=[kTb, Gib], writes=[Ktb])
            S.op("pool", lambda: nc.gpsimd.tensor_tensor(out=Rt[:], in0=rT[:], in1=G[:], op=ALU.mult), reads=[rTb, Gb], writes=[Rtb])
            (GiT, GiTb), (Btok, Btokb), (Ktok, Ktokb), (Yt, Ytb) = TT_["GiT"], TT_["Btok"], TT_["Ktok"], TT_["Y"]
            for half in range(2):
                cs = slice(half * 512, (half + 1) * 512)
                p, pb = pr.next()
                S.op("pe", lambda: nc.tensor.matmul(p[:64, :], lhsT=tri_i[:, :], rhs=lwk[:, cs], start=True, stop=True),
                     reads=[tri_ib, lwkb], writes=[pb])
                S.op("act", lambda: nc.scalar.activation(out=GiT[:, cs], in_=p[:64, :], func=AF.Exp, scale=-1.0), reads=[pb], writes=[GiTb])
            S.op("dve", lambda: nc.vector.tensor_tensor(out=Btok[:], in0=btk[:], in1=GiT[:], op=ALU.mult), reads=[btkb, GiTb], writes=[Btokb])
            S.op("pool", lambda: nc.gpsimd.tensor_tensor(out=Ktok[:], in0=ktk[:], in1=GiT[:], op=ALU.mult), reads=[ktkb, GiTb], writes=[Ktokb])
            (Q, Qb), (IQ, IQb), (P, Pb) = W["Q"], W["IQ"], W["P"]
            ev_mask(mm16([(lambda h: hv(Bt, h), lambda h: hv(At, h), [Btb, Atb])]), Q, Qb, tri_s, tri_sb)
            add_id(Q, Qb, IQ, IQb)
            ev_mask(mm16([(lambda h: hv(At, h), lambda h: hv(Bt, h), [Btb, Atb])]), P, Pb, m_sl, m_slb)
            (Mak, Makb), (Lrb, Lrbb), (Lrk, Lrkb) = W["Mak"], W["Lrb"], W["Lrk"]
            ev_mask(mm16([(lambda h: hv(Kt, h), lambda h: hv(At, h), [Ktb, Atb])]), Mak, Makb, tri_s, tri_sb)
            ev_mask(mm16([(lambda h: hv(Bt, h), lambda h: hv(Rt, h), [Btb, Rtb])]), Lrb, Lrbb, tri_i, tri_ib)
            ev_mask(mm16([(lambda h: hv(Kt, h), lambda h: hv(Rt, h), [Ktb, Rtb])]), Lrk, Lrkb, tri_i, tri_ib)
            X, Xb = W["X"]
            X2, X2b = W["X2"]
            ev_copy(mm16([(lambda h: hv(At, h), lambda h: hv(ST, h), [Atb, STb]),
                          (lambda h: hv(Mak, h), lambda h: vtk[:, h * 64:(h + 1) * 64], [Makb, vtkb])]), X, Xb)
            cur = (Q, Qb, IQ, IQb, P, Pb)
            nxt = (W["Q2"][0], W["Q2"][1], W["IQ2"][0], W["IQ2"][1], W["P2"][0], W["P2"][1])
            for lvl in range(6):
                q, qb, iq, iqb, p_, pb_ = cur
                ev_copy(mm16([(lambda h: hv(iq, h), lambda h: hv(X, h), [iqb, Xb])]), X2, X2b)
                X, Xb, X2, X2b = X2, X2b, X, Xb
                if lvl < 5:
                    q2, q2b, iq2, iq2b, p2, p2b = nxt
                    ev_copy(mm16([(lambda h: hv(p_, h), lambda h: hv(q, h), [pb_, qb])]), q2, q2b)
                    add_id(q2, q2b, iq2, iq2b)
                    if lvl < 4:
                        ev_copy(mm16([(lambda h: hv(q, h), lambda h: hv(p_, h), [pb_, qb])]), p2, p2b)
                    cur, nxt = nxt, cur
            U, Ub = X, Xb
            bY = mm16([(lambda h: hv(Rt, h), lambda h: hv(ST, h), [Rtb, STb]),
                       (lambda h: hv(Lrb, h), lambda h: hv(U, h), [Lrbb, Ub]),
                       (lambda h: hv(Lrk, h), lambda h: vtk[:, h * 64:(h + 1) * 64], [Lrkb, vtkb])])
            for half, (p, pb) in enumerate(bY):
                S.op("act", lambda: nc.scalar.copy(out=Yt[:, half * 512:(half + 1) * 512], in_=p[:64, :]), reads=[pb], writes=[Ytb])
            S.dma("pool", OZ["ytok"][t0:t0 + CH, :], Yt[:], reads=[Ytb], writes=[db("o_ytok")])
            bD = mm16([(lambda h: Btok[:, h * 64:(h + 1) * 64], lambda h: hv(U, h), [Btokb, Ub]),
                       (lambda h: Ktok[:, h * 64:(h + 1) * 64], lambda h: vtk[:, h * 64:(h + 1) * 64], [Ktokb, vtkb])])
            for half, (p, pb) in enumerate(bD):
                sl = slice(half * 8, (half + 1) * 8)
                S.op("dve", lambda: nc.vector.tensor_tensor(out=ST[:, sl, :], in0=ST[:, sl, :], in1=pv(p), op=ALU.add), reads=[pb], writes=[STb])
                S.op("dve", lambda: nc.vector.tensor_tensor(out=ST[:, sl, :], in0=ST[:, sl, :], in1=G[:, sl, 63:64].to_broadcast([64, 8, 64]),
                                                            op=ALU.mult), reads=[Gb], writes=[STb])


def tok_to_mixT(C, src_t, src_b, row0, t0, pbr, obr):
    nc, S, db = C.nc, C.S, C.db
    ob, obb = obr.next()
    for half in range(2):
        pt, ptb = pbr.next()
        for q4 in range(4):
            cb = half * 4 + q4
            S.op("pe", lambda: nc.tensor.transpose(out=pt[:, q4 * 128:(q4 + 1) * 128], in_=src_t[:, cb * 128:(cb + 1) * 128],
                                                   identity=C.ident_f[:]), reads=[src_b, C.b_identf], writes=[ptb])
        S.op("act", lambda: nc.scalar.copy(out=ob[:, half * 4:(half + 1) * 4, :], in_=pt[:, :].rearrange("p (c t) -> p c t", t=128)),
             reads=[ptb], writes=[obb])
    S.dma("pool", C.MIXT[row0:row0 + 1024, t0:t0 + 128].rearrange("(cb p) l -> p cb l", p=128), ob[:], reads=[obb], writes=[db("mixT")])


def rwkv_post(C, j):
    nc, S, L = C.nc, C.S, C.L
    I, db, OZ = C.I, C.db, C.OZ
    sb, ring = C.sb, C.ring
    with C.phase():
        def bcast(name, src, n):
            t = sb([128, n], F32, name); b = Buf()
            S.dma("sp", t[:], src.partition_broadcast(128), writes=[b])
            return t, b
        lnw, lnwb = bcast("lnw", I["rwkv_ln_w"][j], 1024)
        lnb, lnbb = bcast("lnb", I["rwkv_ln_b"][j], 1024)
        eps2 = sb([128, 1], F32, "eps2"); eps2b = Buf()
        S.op("pool", lambda: nc.gpsimd.memset(eps2[:], 64e-5), writes=[eps2b])
        yf_r = ring(2, [128, 8, 128], F32, "yf")
        y_r = ring(2, [128, 1024], F32, "y")
        sq_r = ring(1, [128, 1024], F32, "sq")
        bon_r = ring(2, [128, 1024], F32, "bonl")
        g_r = ring(2, [128, 1024], F32, "gl")
        sm = {n: ring(2, [128, 16], F32, n) for n in ("mean", "var")}
        ob_r = ring(2, [128, 8, 128], BF16, "obm")
        for t0 in range(0, L, 128):
            bon, bonb = bon_r.next()
            S.dma("sp", bon[:], OZ["bon"][t0:t0 + 128, :], reads=[db("o_bon")], writes=[bonb])
            gl, glb = g_r.next()
            S.dma("sp", gl[:], OZ["gt"][t0:t0 + 128, :], reads=[db("o_gt")], writes=[glb])
            y, yb = y_r.next()
            S.dma("sp", y[:], OZ["ytok"][t0:t0 + 128, :], reads=[db("o_ytok")], writes=[yb])
            mean, meanb = sm["mean"].next()
            S.op("dve", lambda: nc.vector.tensor_reduce(out=mean[:], in_=v3(y[:]), axis=AX.X, op=ALU.add), reads=[yb], writes=[meanb])
            S.op("dve", lambda: nc.vector.tensor_scalar_mul(out=mean[:], in0=mean[:], scalar1=1.0 / 64.0), reads=[], writes=[meanb])
            S.op("dve", lambda: nc.vector.tensor_tensor(out=v3(y[:]), in0=v3(y[:]), in1=bc3(mean[:]), op=ALU.subtract), reads=[meanb], writes=[yb])
            sq, sqb = sq_r.next()
            S.op("pool", lambda: nc.gpsimd.tensor_tensor(out=sq[:], in0=y[:], in1=y[:], op=ALU.mult), reads=[yb], writes=[sqb])
            var, varb = sm["var"].next()
            S.op("dve", lambda: nc.vector.tensor_reduce(out=var[:], in_=v3(sq[:]), axis=AX.X, op=ALU.add), reads=[sqb], writes=[varb])
            S.op("act", lambda: nc.scalar.activation(out=var[:], in_=var[:], func=AF.Sqrt, bias=eps2[:, 0:1], scale=1.0 / 64.0),
                 reads=[eps2b], writes=[varb])
            S.op("dve", lambda: nc.vector.reciprocal(out=var[:], in_=var[:]), reads=[], writes=[varb])
            S.op("dve", lambda: nc.vector.tensor_tensor(out=v3(y[:]), in0=v3(y[:]), in1=bc3(var[:]), op=ALU.mult), reads=[varb], writes=[yb])
            S.op("pool", lambda: nc.gpsimd.tensor_tensor(out=y[:], in0=y[:], in1=lnw[:], op=ALU.mult), reads=[lnwb], writes=[yb])
            S.op("pool", lambda: nc.gpsimd.tensor_tensor(out=y[:], in0=y[:], in1=lnb[:], op=ALU.add), reads=[lnbb], writes=[yb])
            S.op("dve", lambda: nc.vector.tensor_tensor(out=y[:], in0=y[:], in1=bon[:], op=ALU.add), reads=[bonb], writes=[yb])
            S.op("dve", lambda: nc.vector.tensor_tensor(out=y[:], in0=y[:], in1=gl[:], op=ALU.mult), reads=[glb], writes=[yb])
            tok_to_mixT(C, y, yb, 0, t0, C.aux2, ob_r)


def dil_rope(C):
    nc, S, L = C.nc, C.S, C.L
    I, db, OZ = C.I, C.db, C.OZ
    sb, ring = C.sb, C.ring
    TT = 512
    with C.phase():
        x_r = ring(2, [128, TT], F32, "rx")
        xs_r = ring(2, [128, TT], F32, "rxs")
        c_r = ring(2, [128, TT], F32, "rc")
        s_r = ring(2, [128, TT], F32, "rs")
        o_r = ring(2, [128, TT], F32, "ro")
        ob_r = ring(2, [128, TT], BF16, "rob")
        sq_r = ring(2, [128, TT], F32, "rsq")
        n_r = ring(4, [1, TT], F32, "rn")
        kmax = sb([1, 8], F32, "kmax"); kmaxb = Buf()
        S.op("pool", lambda: nc.gpsimd.memset(kmax[:], 0.0), writes=[kmaxb])
        ng_r = ring(2, [1, TT], BF16, "rng")
        for which, src, dst in (("k", OZ["dk"], OZ["kr"]), ("q", OZ["dq"], OZ["qr"])):
            for h in range(8):
                for t0 in range(0, L, TT):
                    x, xb = x_r.next()
                    S.dma("sp", x[:], src[h * 128:(h + 1) * 128, t0:t0 + TT], reads=[db("o_d" + which)], writes=[xb])
                    xs, xsb = xs_r.next()
                    S.dma("sp", xs[0:64, :], src[h * 128 + 64:(h + 1) * 128, t0:t0 + TT], reads=[db("o_d" + which)], writes=[xsb])
                    S.dma("sp", xs[64:128, :], src[h * 128:h * 128 + 64, t0:t0 + TT], reads=[db("o_d" + which)], writes=[xsb])
                    c, cb = c_r.next()
                    S.dma("sp", c[:], I["c_cos"][:, t0:t0 + TT], writes=[cb])
                    s_, sb_ = s_r.next()
                    S.dma("sp", s_[:], I["c_sin"][:, t0:t0 + TT], writes=[sb_])
                    o, ob = o_r.next()
                    S.op("dve", lambda: nc.vector.tensor_tensor(out=o[:], in0=x[:], in1=c[:], op=ALU.mult), reads=[xb, cb], writes=[ob])
                    S.op("pool", lambda: nc.gpsimd.tensor_tensor(out=xs[:], in0=xs[:], in1=s_[:], op=ALU.mult), reads=[sb_], writes=[xsb])
                    obf, obfb = ob_r.next()
                    S.op("dve", lambda: nc.vector.tensor_tensor(out=obf[:], in0=o[:], in1=xs[:], op=ALU.add), reads=[ob, xsb], writes=[obfb])
                    S.dma("pool", dst[h * 128:(h + 1) * 128, t0:t0 + TT], obf[:], reads=[obfb], writes=[db("o_%sr" % which)])
                    sq, sqb = sq_r.next()
                    S.op("act", lambda: nc.scalar.activation(out=sq[:], in_=x[:], func=AF.Square), reads=[xb], writes=[sqb])
                    pn, pnb = C.aux.next()
                    S.op("pe", lambda: nc.tensor.matmul(pn[0:1, :TT], lhsT=C.ones_f[:, 0:1], rhs=sq[:], start=True, stop=True),
                         reads=[C.b_ones, sqb], writes=[pnb])
                    nr, nrb = n_r.next()
                    S.op("act", lambda: nc.scalar.activation(out=nr[:], in_=pn[0:1, :TT], func=AF.Sqrt), reads=[pnb], writes=[nrb])
                    if which == "k":
                        mx, mxb = n_r.next()
                        S.op("dve", lambda: nc.vector.tensor_reduce(out=mx[:, 0:1], in_=nr[:], axis=AX.X, op=ALU.max), reads=[nrb], writes=[mxb])
                        S.op("dve", lambda: nc.vector.tensor_tensor(out=kmax[:, h:h + 1], in0=kmax[:, h:h + 1], in1=mx[:, 0:1], op=ALU.max),
                             reads=[mxb], writes=[kmaxb])
                    else:
                        ng, ngb = ng_r.next()
                        S.op("dve", lambda: nc.vector.tensor_scalar(out=ng[:], in0=nr[:], scalar1=kmax[:, h:h + 1], scalar2=-1.0,
                                                                    op0=ALU.mult, op1=ALU.mult), reads=[nrb, kmaxb], writes=[ngb])
                        S.dma("pool", OZ["negc"][h:h + 1, t0:t0 + TT], ng[:], reads=[ngb], writes=[db("o_negc")])


def dil_attn(C):
    nc, S, L = C.nc, C.S, C.L
    I, db, OZ = C.I, C.db, C.OZ
    sb, ring = C.sb, C.ring
    SCALE = 128 ** -0.5
    VB = 32
    with C.phase():
        mask = sb([128, 256], BF16, "dmask"); maskb = Buf()
        mask_f = sb([128, 256], F32, "dmaskf"); maskfb = Buf()
        S.dma("sp", mask_f[:], I["c_dmask"], writes=[maskfb])
        S.op("dve", lambda: nc.vector.tensor_copy(out=mask[:], in_=mask_f[:]), reads=[maskfb], writes=[maskb])
        onesr = sb([1, 128], BF16, "onesr"); onesrb = Buf()
        S.op("pool", lambda: nc.gpsimd.memset(onesr[:], 1.0), writes=[onesrb])
        q_r = ring(1, [128, L], BF16, "dq")
        k_r = ring(1, [128, L], BF16, "dkk")
        nc_b = ring(1, [1, L], BF16, "ncb")
        vf_r = ring(2, [128, VB, 128], F32, "dvf")
        vb_r = ring(2, [128, VB, 129], BF16, "dvb")
        p_r = ring(2, [128, 256], BF16, "dp")
        pm_r = ring(2, [128, 256], BF16, "dpm")
        o_r = ring(3, [128, 129], F32, "dout")
        ps_r = Ring(C.acc.tiles)
        po_r = Ring(C.aux.tiles + C.aux2.tiles)
        for h in range(8):
            qs, qsb = q_r.next()
            S.dma("sp", qs[:], OZ["qr"][h * 128:(h + 1) * 128, :], reads=[db("o_qr")], writes=[qsb])
            ks, ksb = k_r.next()
            S.dma("sp", ks[:], OZ["kr"][h * 128:(h + 1) * 128, :], reads=[db("o_kr")], writes=[ksb])
            ncb, ncbb = nc_b.next()
            S.dma("sp", ncb[:], OZ["negc"][h:h + 1, :], reads=[db("o_negc")], writes=[ncbb])
            for br, dil in enumerate((1, 4, 16)):
                nblk = L // dil // 128
                for r in range(dil):
                    pend = None
                    for m in range(nblk):
                        if m % VB == 0:
                            nb_ = min(VB, nblk - m)
                            vf, vfb = vf_r.next()
                            src = OZ["dv"][:, h * 128:(h + 1) * 128].rearrange("(m i d) c -> d i m c", d=dil, i=128)[r]
                            S.dma("sp", vf[:, :nb_, :], src[:, m:m + nb_, :], reads=[db("o_dv")], writes=[vfb])
                            vb, vbb = vb_r.next()
                            S.op("pool", lambda: nc.gpsimd.memset(vb[:, :nb_, 128:129], 1.0), writes=[vbb])
                            S.op("pool", lambda: nc.gpsimd.tensor_copy(out=vb[:, :nb_, 0:128], in_=vf[:, :nb_, :]), reads=[vfb], writes=[vbb])
                        last = (m == nblk - 1)
                        nq = 128 if last else 256
                        base = r + dil * 128 * m
                        kv = ks[:, base: base + dil * 127 + 1: dil]
                        qv = qs[:, base: base + dil * (nq - 1) + 1: dil]
                        cv = ncb[:, base: base + dil * (nq - 1) + 1: dil]
                        psc, pscb = ps_r.next()
                        S.op("pe", lambda: nc.tensor.matmul(psc[:, :nq], lhsT=kv, rhs=qv, start=True, stop=False),
                             reads=[ksb, qsb], writes=[pscb])
                        S.op("pe", lambda: nc.tensor.matmul(psc[:, :nq], lhsT=onesr[:, :], rhs=cv, start=False, stop=True),
                             reads=[onesrb, ncbb], writes=[pscb])
                        p, pb = p_r.next()
                        S.op("act", lambda: nc.scalar.activation(out=p[:, :nq], in_=psc[:, :nq], func=AF.Exp, scale=SCALE),
                             reads=[pscb], writes=[pb])
                        pm, pmb = pm_r.next()
                        S.op("dve", lambda: nc.vector.tensor_tensor(out=pm[:, :nq], in0=p[:, :nq], in1=mask[:, :nq], op=ALU.mult),
                             reads=[pb, maskb], writes=[pmb])
                        vmm = vb[:, m % VB, :]
                        if pend is None:
                            pcur, pcurb = po_r.next()
                            S.op("pe", lambda: nc.tensor.matmul(pcur[:, :129], lhsT=pm[:, 0:128], rhs=vmm, start=True, stop=True),
                                 reads=[pmb, vbb], writes=[pcurb])
                        else:
                            pcur, pcurb = pend
                            S.op("pe", lambda: nc.tensor.matmul(pcur[:, :129], lhsT=pm[:, 0:128], rhs=vmm, start=False, stop=True),
                                 reads=[pmb, vbb], writes=[pcurb])
                        if not last:
                            pnx, pnxb = po_r.next()
                            S.op("pe", lambda: nc.tensor.matmul(pnx[:, :129], lhsT=pm[:, 128:256], rhs=vmm, start=True, stop=False),
                                 reads=[pmb, vbb], writes=[pnxb])
                            pend = (pnx, pnxb)
                        o, ob = o_r.next()
                        S.op("act", lambda: nc.scalar.copy(out=o[:], in_=pcur[:, :129]), reads=[pcurb], writes=[ob])
                        dst = OZ["acc"][br].rearrange("(m i d) hh c -> d i m hh c", d=dil, i=128)[r][:, m, h, 0:129]
                        S.dma("pool", dst, o[:], reads=[ob], writes=[db("o_acc")])


def dil_combine(C):
    nc, S, L = C.nc, C.S, C.L
    I, db, OZ = C.I, C.db, C.OZ
    sb, ring = C.sb, C.ring
    with C.phase():
        a_r = [ring(2, [128, 8, 132], F32, "ca%d" % i) for i in range(3)]
        den_r = ring(2, [128, 8], F32, "cden")
        out_r = ring(2, [128, 1024], F32, "cout")
        ob_r = ring(2, [128, 8, 128], BF16, "cob")
        for t0 in range(0, L, 128):
            A = []
            for br in range(3):
                a, ab = a_r[br].next()
                S.dma("sp", a[:], OZ["acc"][br][t0:t0 + 128], reads=[db("o_acc")], writes=[ab])
                A.append((a, ab))
            a0, a0b = A[0]
            S.op("dve", lambda: nc.vector.tensor_tensor(out=a0[:], in0=a0[:], in1=A[1][0][:], op=ALU.add), reads=[A[1][1]], writes=[a0b])
            S.op("dve", lambda: nc.vector.tensor_tensor(out=a0[:], in0=a0[:], in1=A[2][0][:], op=ALU.add), reads=[A[2][1]], writes=[a0b])
            den, denb = den_r.next()
            S.op("dve", lambda: nc.vector.reciprocal(out=den[:], in_=a0[:, :, 128]), reads=[a0b], writes=[denb])
            o, ob = out_r.next()
            S.op("dve", lambda: nc.vector.tensor_tensor(out=o[:].rearrange("p (h c) -> p h c", c=128), in0=a0[:, :, 0:128],
                                                        in1=den[:].unsqueeze(2).to_broadcast([128, 8, 128]), op=ALU.mult),
                 reads=[a0b, denb], writes=[ob])
            tok_to_mixT(C, o, ob, 1024, t0, C.aux2, ob_r)


def _prep_common(inputs, L):
    def g16(a):
        return np.ascontiguousarray(a.reshape(a.shape[0], 16, 128).transpose(0, 2, 1))
    m = {}
    for n in ("norm_mix_pre", "norm_mix_post", "norm_ffn_pre", "norm_ffn_post", "ple_norm"):
        m[n] = g16(np.asarray(inputs[n], np.float32))
    for n in ("ev_w_in", "ev_w_out", "pool_w", "od_w_in", "od_w_out", "ffn_up", "ffn_down", "ple_proj", "ple_gate"):
        m[n] = np.ascontiguousarray(np.asarray(inputs[n], np.float32))
    ps_ = np.asarray(inputs["pool_scale"], np.float32)
    m["pool_scale"] = np.ascontiguousarray(ps_.reshape(-1, 4, 128).transpose(0, 2, 1))
    gw = np.asarray(inputs["gla_gate_w2"], np.float32)
    gb = np.asarray(inputs["gla_gate_b"], np.float32)
    m["gla_gate_wb"] = np.ascontiguousarray(np.concatenate([gw, gb[:, None, :]], axis=1))
    gn = np.asarray(inputs["gla_norm"], np.float32)
    m["gla_norm"] = np.ascontiguousarray(gn.reshape(-1, 3, 128).transpose(0, 2, 1))
    mu = np.asarray(inputs["rwkv_mu"], np.float32)
    m["rwkv_mu"] = np.ascontiguousarray(mu[:, None, :])
    m["rwkv_mu_lr"] = np.ascontiguousarray(mu[:, 3072:3360, None])
    m["rwkv_w2a"] = np.ascontiguousarray(np.concatenate([np.asarray(inputs["rwkv_w2"], np.float32),
                                                         np.asarray(inputs["rwkv_w0"], np.float32)[:, None, :]], axis=1))
    m["rwkv_a2a"] = np.ascontiguousarray(np.concatenate([np.asarray(inputs["rwkv_a2"], np.float32),
                                                         np.asarray(inputs["rwkv_a0"], np.float32)[:, None, :]], axis=1))
    m["rwkv_g2"] = np.ascontiguousarray(np.asarray(inputs["rwkv_g2"], np.float32))
    for n in ("rwkv_k_k", "rwkv_k_a", "rwkv_r_k", "rwkv_ln_w", "rwkv_ln_b"):
        a = np.asarray(inputs[n], np.float32)
        m[n] = np.ascontiguousarray(a.reshape(a.shape[0], 1, 1024))
    sel = np.zeros((2, 128), np.float32)
    sel[0, :64] = 1.0
    sel[1, 64:] = 1.0
    m["c_sel"] = sel
    m["c_ident"] = np.eye(128, dtype=np.float32)
    m["c_su"] = np.triu(np.ones((64, 64), np.float32), 1)
    m["c_sl"] = np.tril(np.ones((64, 64), np.float32), -1)
    m["c_i64"] = np.eye(64, dtype=np.float32)
    half = 64
    inv = (np.float32(10000.0) ** (-np.arange(half, dtype=np.float32) / np.float32(half))).astype(np.float32)
    ang = (np.arange(L, dtype=np.float32)[:, None] * inv[None, :]).astype(np.float32)
    cos = np.cos(ang).astype(np.float32).T
    sin = np.sin(ang).astype(np.float32).T
    m["c_cos"] = np.ascontiguousarray(np.concatenate([cos, cos], axis=0))
    m["c_sin"] = np.ascontiguousarray(np.concatenate([-sin, sin], axis=0))
    kidx = np.arange(128)[:, None]
    qidx = np.arange(256)[None, :]
    m["c_dmask"] = ((qidx - kidx >= 0) & (qidx - kidx <= 128)).astype(np.float32)
    tri = np.triu(np.ones((64, 64), np.float32))
    m["c_tri"] = tri
    m["c_trimask"] = tri.copy()
    cnt = np.zeros((4, 128, 16), np.float32)
    for gi, w in enumerate((2, 4, 8, 16)):
        cnt[gi, :, :] = 1.0 / np.minimum(np.arange(1, 17), w).astype(np.float32)[None, :]
    m["c_poolcnt"] = cnt
    return m


LAYER_KEYS = ("norm_mix_pre", "norm_mix_post", "norm_ffn_pre", "norm_ffn_post", "ple_norm",
              "ffn_up", "ffn_down", "ple_proj", "ple_gate")
EVEN_KEYS = ("ev_w_in", "ev_w_out", "pool_w", "pool_scale", "gla_gate_wb", "gla_norm")
ODD_KEYS = ("od_w_in", "od_w_out", "rwkv_mu", "rwkv_mu_lr", "rwkv_w2a", "rwkv_a2a", "rwkv_g2",
            "rwkv_k_k", "rwkv_k_a", "rwkv_r_k", "rwkv_ln_w", "rwkv_ln_b")


def make_in_maps(inputs, L, B, layer=None, xT_list=None, common=None):
    if common is None:
        common = _prep_common(inputs, L)
    p = np.asarray(inputs["p"], np.float32)
    maps = []
    for b in range(B):
        if layer is None:
            m = dict(common)
            m["pT"] = np.ascontiguousarray(p[:, b].transpose(0, 2, 1))
        else:
            m = {}
            for k, v in common.items():
                if k in LAYER_KEYS:
                    m[k] = np.ascontiguousarray(v[layer:layer + 1])
                elif k in EVEN_KEYS:
                    if layer % 2 == 0:
                        m[k] = np.ascontiguousarray(v[layer // 2:layer // 2 + 1])
                elif k in ODD_KEYS:
                    if layer % 2 == 1:
                        m[k] = np.ascontiguousarray(v[layer // 2:layer // 2 + 1])
                else:
                    m[k] = v
            m["pT"] = np.ascontiguousarray(p[layer:layer + 1, b].transpose(0, 2, 1))
        if xT_list is not None:
            m["xT"] = xT_list[b]
        else:
            m["xT"] = np.ascontiguousarray(np.asarray(inputs["x"], np.float32)[b].T)
        maps.append(m)
    return maps


N_LAUNCH_MODE = 4


def kernel(**inputs):
    x = np.asarray(inputs["x"])
    B, L, _ = x.shape
    if N_LAUNCH_MODE == 1:
        nc, C = build(L)
        maps = make_in_maps(inputs, L, B)
        res = run_bass_kernel_spmd(nc, maps, core_ids=list(range(B)))
        xT = [res.results[b]["yT"] for b in range(B)]
    else:
        common = _prep_common(inputs, L)
        xT = None
        for li in range(4):
            nc, C = build(L, layers=(li,), single=True)
            maps = make_in_maps(inputs, L, B, layer=li, xT_list=xT, common=common)
            res = run_bass_kernel_spmd(nc, maps, core_ids=list(range(B)))
            xT = [np.ascontiguousarray(res.results[b]["yT"]) for b in range(B)]
            del maps, res
    out = np.stack([np.ascontiguousarray(xT[b].T) for b in range(B)], axis=0)
    return out.astype(np.float32)
```

```python
import contextlib
import numpy as np
import ml_dtypes
import concourse.bass as bass
import concourse.mybir as mybir
from concourse.bass_utils import run_bass_kernel_spmd

F32 = mybir.dt.float32
BF16 = mybir.dt.bfloat16
AF = mybir.ActivationFunctionType
ALU = mybir.AluOpType
AX = mybir.AxisListType

D = 2048
DFF = 8192
EPS = 1e-6
EVEN_IN = 5136
STQ = "pool"
ODD_IN = 6432


class Buf:
    __slots__ = ("name", "w", "r")

    def __init__(self, name="b"):
        self.name = name
        self.w = None
        self.r = []


class Sched:
    NDMA = 32

    def __init__(self, nc, es):
        self.nc = nc
        self.engs = {"pe": nc.tensor, "dve": nc.vector, "act": nc.scalar,
                     "pool": nc.gpsimd, "sp": nc.sync}
        self.sems = {}
        self.cnt = {}
        for k in self.engs:
            self.sems[k] = es.enter_context(nc.semaphore("s_" + k))
            self.cnt[k] = 0
        self.dsems = [es.enter_context(nc.semaphore("d%d" % i)) for i in range(self.NDMA)]
        self.dcnt = [0] * self.NDMA
        self.dnext = 0
        self.waited = {}
        self.ninst = 0

    def _semobj(self, key):
        return self.sems[key] if isinstance(key, str) else self.dsems[key]

    def _wait(self, eng, key, val):
        if val <= 0 or self.waited.get((eng, key), 0) >= val:
            return
        self.waited[(eng, key)] = val
        self.engs[eng].wait_ge(self._semobj(key), val)
        self.ninst += 1

    def _deps(self, eng, reads, writes):
        need = {}
        for b in reads:
            if b.w is not None:
                k, v = b.w
                if v > need.get(k, 0):
                    need[k] = v
        for b in writes:
            if b.w is not None:
                k, v = b.w
                if v > need.get(k, 0):
                    need[k] = v
            for k, v in b.r:
                if v > need.get(k, 0):
                    need[k] = v
        for k, v in need.items():
            if eng == "pe" and k == "pe":
                continue
            self._wait(eng, k, v)

    def _mark(self, tok, reads, writes):
        for b in writes:
            b.w = tok
            b.r = []
        for b in reads:
            if b in writes:
                continue
            b.r.append(tok)
            if len(b.r) > 16:
                best = {}
                for k, v in b.r:
                    if v > best.get(k, 0):
                        best[k] = v
                b.r = list(best.items())

    def op(self, eng, fn, reads=(), writes=()):
        self._deps(eng, reads, writes)
        ins = fn()
        self.cnt[eng] += 1
        ins.then_inc(self.sems[eng], 1)
        self.ninst += 1
        self._mark((eng, self.cnt[eng]), reads, writes)
        return ins

    def dma(self, eng, out, in_, reads=(), writes=()):
        i = self.dnext
        self.dnext = (self.dnext + 1) % self.NDMA
        self._wait(eng, i, self.dcnt[i])
        self._deps(eng, reads, writes)
        ins = self.engs[eng].dma_start(out=out, in_=in_)
        self.dcnt[i] += 16
        ins.then_inc(self.dsems[i], 16)
        self.ninst += 1
        self._mark((i, self.dcnt[i]), reads, writes)
        return ins

    def drain(self, eng="sp"):
        for i in range(self.NDMA):
            self._wait(eng, i, self.dcnt[i])
        for k in self.engs:
            self._wait(eng, k, self.cnt[k])


class Ring:
    def __init__(self, tiles):
        self.tiles = tiles
        self.bufs = [Buf() for _ in tiles]
        self.i = 0

    def next(self):
        t, b = self.tiles[self.i], self.bufs[self.i]
        self.i = (self.i + 1) % len(self.tiles)
        return t, b


class Ctx:
    pass


def build(L, layers=(0, 1, 2, 3), dbg=(), single=False):
    nc = bass.Bass("TRN2", target_bir_lowering=False)
    C = Ctx()
    C.nc = nc
    C.L = L
    nE = 1 if single else 2
    nO = 1 if single else 2
    nL = 1 if single else 4
    want_even = (not single) or (layers[0] % 2 == 0)
    want_odd = (not single) or (layers[0] % 2 == 1)
    C.LI = (lambda li: 0) if single else (lambda li: li)
    C.J = (lambda j: 0) if single else (lambda j: j)

    def din(name, shape, dt=F32):
        return nc.dram_tensor(name, list(shape), dt, kind="ExternalInput").ap()

    def dscr(name, shape, dt=F32):
        return nc.dram_tensor(name, list(shape), dt, kind="Internal").ap()

    I = {}
    I["xT"] = din("xT", [D, L])
    I["pT"] = din("pT", [nL, 256, L])
    for n in ("norm_mix_pre", "norm_mix_post", "norm_ffn_pre", "norm_ffn_post", "ple_norm"):
        I[n] = din(n, [nL, 128, 16])
    if want_even:
        I["ev_w_in"] = din("ev_w_in", [nE, D, EVEN_IN])
        I["ev_w_out"] = din("ev_w_out", [nE, D, D])
        I["pool_w"] = din("pool_w", [nE, 4, 128, 128])
        I["pool_scale"] = din("pool_scale", [nE, 128, 4])
        I["gla_gate_wb"] = din("gla_gate_wb", [nE, 17, 768])
        I["gla_norm"] = din("gla_norm", [nE, 128, 3])
    if want_odd:
        I["od_w_in"] = din("od_w_in", [nO, D, ODD_IN])
        I["od_w_out"] = din("od_w_out", [nO, D, D])
    I["ffn_up"] = din("ffn_up", [nL, D, DFF])
    I["ffn_down"] = din("ffn_down", [nL, DFF, D])
    I["ple_proj"] = din("ple_proj", [nL, 256, D])
    I["ple_gate"] = din("ple_gate", [nL, D, D])
    if want_odd:
        I["rwkv_mu"] = din("rwkv_mu", [nO, 1, 3360])
        I["rwkv_mu_lr"] = din("rwkv_mu_lr", [nO, 288, 1])
        I["rwkv_w2a"] = din("rwkv_w2a", [nO, 65, 1024])
        I["rwkv_a2a"] = din("rwkv_a2a", [nO, 65, 1024])
        I["rwkv_g2"] = din("rwkv_g2", [nO, 160, 1024])
        for n in ("rwkv_k_k", "rwkv_k_a", "rwkv_r_k", "rwkv_ln_w", "rwkv_ln_b"):
            I[n] = din(n, [nO, 1, 1024])
    I["c_su"] = din("c_su", [64, 64])
    I["c_sl"] = din("c_sl", [64, 64])
    I["c_i64"] = din("c_i64", [64, 64])
    I["c_sel"] = din("c_sel", [2, 128])
    I["c_ident"] = din("c_ident", [128, 128])
    I["c_cos"] = din("c_cos", [128, L])
    I["c_sin"] = din("c_sin", [128, L])
    I["c_dmask"] = din("c_dmask", [128, 256])
    I["c_tri"] = din("c_tri", [64, 64])
    I["c_trimask"] = din("c_trimask", [64, 64])
    I["c_poolcnt"] = din("c_poolcnt", [4, 128, 16])
    yT = nc.dram_tensor("yT", [D, L], F32, kind="ExternalOutput").ap()
    DBG = {}
    for name, shape in dbg:
        DBG[name] = nc.dram_tensor("dbg_" + name, list(shape), F32, kind="ExternalOutput").ap()

    X = dscr("xs", [D, L])
    HT = dscr("hT", [L // 512, 128, D // 128, 512], BF16)
    YT = dscr("ysc", [D, L])
    AT = dscr("aT", [L // 256, 128, DFF // 128, 256], BF16)
    MIXT = dscr("mixT", [D, L], BF16)
    Z = {}
    Z["u"] = dscr("z_u", [512, L])
    Z["q"] = dscr("z_q", [768, L])
    Z["k"] = dscr("z_k", [768, L])
    Z["ktok"] = dscr("z_ktok", [L, 768])
    Z["vtok"] = dscr("z_vtok", [L, 1536])
    Z["gout"] = dscr("z_gout", [1536, L])
    Z["glr"] = dscr("z_glr", [16, L])
    OZ = {}
    OZ["rkv"] = dscr("o_rkv", [L, 3072])
    OZ["lr"] = dscr("o_lr", [288, L])
    OZ["dq"] = dscr("o_dq", [1024, L])
    OZ["dk"] = dscr("o_dk", [1024, L])
    OZ["dv"] = dscr("o_dv", [L, 1024])
    OZ["tok4"] = [dscr("o_tok4_%d" % i, [L, 1024]) for i in range(4)]
    OZ["fm4"] = [dscr("o_fm4_%d" % i, [64, 16, L]) for i in range(4)]
    OZ["ytok"] = dscr("o_ytok", [L, 1024])
    OZ["bon"] = dscr("o_bon", [L, 1024])
    OZ["gt"] = dscr("o_gt", [L, 1024])
    OZ["yfm"] = dscr("o_yfm", [1024, L])
    OZ["qr"] = dscr("o_qr", [1024, L], BF16)
    OZ["kr"] = dscr("o_kr", [1024, L], BF16)
    OZ["negc"] = dscr("o_negc", [8, L], BF16)
    OZ["acc"] = dscr("o_acc", [3, L, 8, 132])
    C.OZ = OZ

    with contextlib.ExitStack() as es:
        S = Sched(nc, es)
        C.S = S
        cnt = [0]

        stack = [es]

        def sb(shape, dt=F32, name=None):
            cnt[0] += 1
            return stack[-1].enter_context(nc.sbuf_tensor("%s_%d" % (name or "t", cnt[0]), list(shape), dt))

        def ps(shape, dt=F32, name=None):
            cnt[0] += 1
            return es.enter_context(nc.psum_tensor("%s_%d" % (name or "p", cnt[0]), list(shape), dt))

        def ring(n, shape, dt=F32, name=None, psum=False):
            return Ring([(ps if psum else sb)(shape, dt, name) for _ in range(n)])

        def barrier():
            for e in S.engs:
                for k in S.engs:
                    if k != e:
                        S._wait(e, k, S.cnt[k])
                for i in range(S.NDMA):
                    S._wait(e, i, S.dcnt[i])
            for b in DB.values():
                b.w = None
                b.r = []

        @contextlib.contextmanager
        def phase():
            st = contextlib.ExitStack()
            stack.append(st)
            try:
                yield
            finally:
                barrier()
                stack.pop()
                st.close()

        C.phase = phase
        C.barrier = barrier
        DB = {}

        def db(ap_name):
            if ap_name not in DB:
                DB[ap_name] = Buf(ap_name)
            return DB[ap_name]

        ones_f = sb([128, 128], F32, "ones")
        b_ones = Buf()
        S.op("pool", lambda: nc.gpsimd.memset(ones_f[:], 1.0), writes=[b_ones])
        ones_b = sb([128, 128], BF16, "onesb")
        b_onesb = Buf()
        S.op("pool", lambda: nc.gpsimd.memset(ones_b[:], 1.0), writes=[b_onesb])
        tri_f = sb([64, 64], F32, "tri")
        b_tri = Buf()
        S.dma("sp", tri_f[:], I["c_tri"], writes=[b_tri])
        ident_f = sb([128, 128], F32, "identf")
        b_identf = Buf()
        S.dma("sp", ident_f[:], I["c_ident"], writes=[b_identf])
        C.ident_f, C.b_identf = ident_f, b_identf
        trimask = sb([64, 64], F32, "trimask")
        b_trimask = Buf()
        S.dma("sp", trimask[:], I["c_trimask"], writes=[b_trimask])
        gains = {}
        for n in ("norm_mix_pre", "norm_mix_post", "norm_ffn_pre", "norm_ffn_post", "ple_norm"):
            t = sb([128, nL, 16], F32, n)
            b = Buf()
            S.dma("sp", t[:], I[n].rearrange("l p c -> p l c"), writes=[b])
            gains[n] = (t, b)

        acc = ring(4, [128, 512], F32, "acc", psum=True)
        aux = ring(2, [128, 512], F32, "aux", psum=True)
        aux2 = ring(2, [128, 512], F32, "aux2", psum=True)

        conv_i = [0]

        def convert(out_ap, in_ap, reads, writes):
            k = conv_i[0] % 3
            conv_i[0] += 1
            if k == 0:
                S.op("act", lambda: nc.scalar.copy(out=out_ap, in_=in_ap), reads=reads, writes=writes)
            elif k == 1:
                S.op("pool", lambda: nc.gpsimd.tensor_copy(out=out_ap, in_=in_ap), reads=reads, writes=writes)
            else:
                S.op("dve", lambda: nc.vector.tensor_copy(out=out_ap, in_=in_ap), reads=reads, writes=writes)

        def dense(a_dram, a_buf, K, wfn, n0, n1, mode, evac, TT=512):
            KC = K // 128
            NBmax = 32768 // KC
            if mode == "tok":
                NBmax = min(NBmax, 2048)
            with phase():
                wstage = ring(2, [128, 2048], F32, "wstage")
                wblk = sb([128, 32768], BF16, "wblk")
                b_wblk = Buf()
                atile = ring(2, [128, KC * TT], BF16, "atile")
                C.evr = ring(3, [128, 512], F32, "evr")
                C.evb = ring(3, [128, 512], BF16, "evb")
                nb0 = n0
                while nb0 < n1:
                    NB = min(NBmax, n1 - nb0)
                    for kc in range(KC):
                        for s0 in range(0, NB, 2048):
                            ssz = min(2048, NB - s0)
                            st, stb = wstage.next()
                            S.dma("sp", st[:, :ssz], wfn(kc * 128, kc * 128 + 128, nb0 + s0, nb0 + s0 + ssz), writes=[stb])
                            convert(wblk[:, kc * NB + s0: kc * NB + s0 + ssz], st[:, :ssz], [stb], [b_wblk])
                    for t0 in range(0, L, TT):
                        at, atb = atile.next()
                        atv = at[:, :].rearrange("p (c t) -> p c t", t=TT)
                        if len(a_dram.shape) == 4:
                            S.dma("sp", atv[:, :, :], a_dram[t0 // TT], reads=[a_buf], writes=[atb])
                        else:
                            S.dma("sp", atv[:, :, :], a_dram.rearrange("(c p) l -> p c l", p=128)[:, :, t0:t0 + TT],
                                  reads=[a_buf], writes=[atb])
                        if mode == "fm":
                            for c0 in range(0, NB, 128):
                                csz = min(128, NB - c0)
                                pt, ptb = acc.next()
                                for kc in range(KC):
                                    S.op("pe", lambda: nc.tensor.matmul(
                                        pt[:csz, :TT], lhsT=wblk[:, kc * NB + c0: kc * NB + c0 + csz],
                                        rhs=atv[:, kc, :], start=(kc == 0), stop=(kc == KC - 1)),
                                        reads=[b_wblk, atb], writes=[ptb])
                                evac(nb0 + c0, csz, t0, TT, pt[:csz, :TT], ptb)
                        else:
                            for jj in range(TT // 128):
                                for c0 in range(0, NB, 512):
                                    csz = min(512, NB - c0)
                                    pt, ptb = acc.next()
                                    for kc in range(KC):
                                        S.op("pe", lambda: nc.tensor.matmul(
                                            pt[:, :csz], lhsT=atv[:, kc, jj * 128:(jj + 1) * 128],
                                            rhs=wblk[:, kc * NB + c0: kc * NB + c0 + csz],
                                            start=(kc == 0), stop=(kc == KC - 1)),
                                            reads=[b_wblk, atb], writes=[ptb])
                                    evac(t0 + jj * 128, nb0 + c0, csz, pt[:, :csz], ptb)
                    nb0 += NB

        ev_i = [0]

        def evac_copy(out_ap, in_ap, reads, writes, func=None):
            ev_i[0] += 1
            if func is not None:
                S.op("act", lambda: nc.scalar.activation(out=out_ap, in_=in_ap, func=func), reads=reads, writes=writes)
            elif ev_i[0] % 2:
                S.op("act", lambda: nc.scalar.copy(out=out_ap, in_=in_ap), reads=reads, writes=writes)
            else:
                S.op("dve", lambda: nc.vector.tensor_copy(out=out_ap, in_=in_ap), reads=reads, writes=writes)

        def store_fm(dst_dram, dst_buf, base, dt=F32):
            def f(c0, csz, t0, tsz, pap, pb):
                st, stb = (C.evr if dt == F32 else C.evb).next()
                evac_copy(st[:csz, :tsz], pap, [pb], [stb])
                S.dma(STQ, dst_dram[c0 - base: c0 - base + csz, t0:t0 + tsz], st[:csz, :tsz],
                      reads=[stb], writes=[dst_buf])
            return f

        def store_tok(dst_dram, dst_buf, base):
            def f(t0, c0, csz, pap, pb):
                st, stb = C.evr.next()
                evac_copy(st[:, :csz], pap, [pb], [stb])
                S.dma(STQ, dst_dram[t0:t0 + 128, c0 - base: c0 - base + csz], st[:, :csz],
                      reads=[stb], writes=[dst_buf])
            return f

        NT = 256
        eps_t = sb([128, 1], F32, "eps")
        b_eps = Buf()
        S.op("pool", lambda: nc.gpsimd.memset(eps_t[:], EPS), writes=[b_eps])
        C.eps_t, C.b_eps = eps_t, b_eps

        def rstd_of(NR, src_t, src_b, n, ntok, eps_ap=None):
            sq, sqb = NR["sq"].next()
            S.op("act", lambda: nc.scalar.activation(out=sq[:, :n, :ntok], in_=src_t[:, :n, :ntok], func=AF.Square),
                 reads=[src_b], writes=[sqb])
            pa, pab = aux.next()
            for c in range(n):
                S.op("pe", lambda: nc.tensor.matmul(pa[:, :ntok], lhsT=ones_f[:, :], rhs=sq[:, c, :ntok],
                                                    start=(c == 0), stop=(c == n - 1)),
                     reads=[b_ones, sqb], writes=[pab])
            rs, rsb = NR["rs"].next()
            S.op("act", lambda: nc.scalar.activation(out=rs[:, :ntok], in_=pa[:, :ntok], func=AF.Sqrt,
                                                     bias=eps_t[:, 0:1], scale=1.0 / (n * 128)),
                 reads=[pab, b_eps], writes=[rsb])
            S.op("dve", lambda: nc.vector.reciprocal(out=rs[:, :ntok], in_=rs[:, :ntok]), reads=[rsb], writes=[rsb])
            return rs, rsb
        C.rstd_of = rstd_of

        def fm3(ap):
            return ap.rearrange("(c p) l -> p c l", p=128)

        def hview(ap, t0, n):
            if len(ap.shape) == 4:
                return ap[t0 // 512][:, :, t0 % 512: t0 % 512 + n]
            return fm3(ap)[:, :, t0:t0 + n]

        def scale_rows(dst_t, dst_b, src_t, src_b, g, gb, li, rs, rsb, n=16):
            for c in range(n):
                eng = "dve"
                e = nc.vector
                S.op(eng, lambda: e.scalar_tensor_tensor(out=dst_t[:, c, :], in0=src_t[:, c, :], scalar=g[:, li, c:c + 1],
                                                         in1=rs[:, :], op0=ALU.mult, op1=ALU.mult),
                     reads=[src_b, rsb, gb], writes=[dst_b])

        def norm_pass(src_dram, src_buf, gname, li, dst_dram, dst_buf):
            g, gb = gains[gname]
            with phase():
                xr = ring(2, [128, 16, NT], F32, "xr")
                hr = ring(2, [128, 16, NT], BF16, "hr")
                NR = dict(sq=ring(1, [128, 16, NT], F32, "sqr"), rs=ring(2, [128, NT], F32, "rsr"))
                for t0 in range(0, L, NT):
                    xt, xb = xr.next()
                    S.dma("sp", xt[:], fm3(src_dram)[:, :, t0:t0 + NT], reads=[src_buf], writes=[xb])
                    rs, rsb = rstd_of(NR, xt, xb, 16, NT)
                    ht, hb = hr.next()
                    scale_rows(ht, hb, xt, xb, g, gb, li, rs, rsb)
                    S.dma("pool", hview(dst_dram, t0, NT), ht[:], reads=[hb], writes=[dst_buf])

        def resid_norm_pass(x_src, x_src_buf, y_dram, y_buf, gpost, gpre, li, h_dram, h_buf, x_dst, x_dst_buf):
            g1, g1b = gains[gpost]
            g2, g2b = gains[gpre]
            with phase():
                xr = ring(2, [128, 16, NT], F32, "xr")
                yr = ring(2, [128, 16, NT], F32, "yr")
                hr = ring(2, [128, 16, NT], BF16, "hr")
                NR = dict(sq=ring(1, [128, 16, NT], F32, "sqr"), rs=ring(2, [128, NT], F32, "rsr"))
                for t0 in range(0, L, NT):
                    xt, xb = xr.next()
                    S.dma("sp", xt[:], fm3(x_src)[:, :, t0:t0 + NT], reads=[x_src_buf], writes=[xb])
                    yt, yb = yr.next()
                    S.dma("sp", yt[:], fm3(y_dram)[:, :, t0:t0 + NT], reads=[y_buf], writes=[yb])
                    rs, rsb = rstd_of(NR, yt, yb, 16, NT)
                    scale_rows(yt, yb, yt, yb, g1, g1b, li, rs, rsb)
                    S.op("dve", lambda: nc.vector.tensor_tensor(out=xt[:], in0=xt[:], in1=yt[:], op=ALU.add),
                         reads=[yb], writes=[xb])
                    S.dma("pool", fm3(x_dst)[:, :, t0:t0 + NT], xt[:], reads=[xb], writes=[x_dst_buf])
                    rs2, rs2b = rstd_of(NR, xt, xb, 16, NT)
                    ht, hb = hr.next()
                    scale_rows(ht, hb, xt, xb, g2, g2b, li, rs2, rs2b)
                    S.dma("pool", hview(h_dram, t0, NT), ht[:], reads=[hb], writes=[h_buf])

        C.sb, C.ps, C.ring = sb, ps, ring
        C.I, C.Z, C.db = I, Z, db
        C.ones_f, C.b_ones = ones_f, b_ones
        C.ones_b, C.b_onesb = ones_b, b_onesb
        C.aux, C.aux2, C.acc = aux, aux2, acc
        C.tri_f, C.b_tri, C.trimask, C.b_trimask = tri_f, b_tri, trimask, b_trimask
        C.MIXT = MIXT
        C.DBG = DBG

        def dbg_copy(name, src_dram, src_buf):
            if name in DBG:
                S.dma("pool", DBG[name], src_dram, reads=[src_buf], writes=[db("dbg_" + name)])
                barrier()
        C.dbg_copy = dbg_copy

        barrier()
        x_src, x_src_buf = I["xT"], db("xT_in")
        for li_real in layers:
            even = (li_real % 2 == 0)
            li = C.LI(li_real)
            j = C.J(li_real // 2)
            norm_pass(x_src, x_src_buf, "norm_mix_pre", li, HT, db("hT"))
            if even:
                W = I["ev_w_in"][j]
                wfn = lambda k0, k1, a, b, W=W: W[k0:k1, a:b]
                dense(HT, db("hT"), D, wfn, 0, 512, "fm", store_fm(Z["u"], db("z_u"), 0))
                dense(HT, db("hT"), D, wfn, 512, 1280, "fm", store_fm(Z["q"], db("z_q"), 512))
                dense(HT, db("hT"), D, wfn, 1280, 2048, "fm", store_fm(Z["k"], db("z_k"), 1280))
                dense(HT, db("hT"), D, wfn, 1280, 2048, "tok", store_tok(Z["ktok"], db("z_ktok"), 1280))
                dense(HT, db("hT"), D, wfn, 2048, 3584, "tok", store_tok(Z["vtok"], db("z_vtok"), 2048))
                dense(HT, db("hT"), D, wfn, 3584, 5120, "fm", store_fm(Z["gout"], db("z_gout"), 3584))
                dense(HT, db("hT"), D, wfn, 5120, 5136, "fm", store_fm(Z["glr"], db("z_glr"), 5120))
                dbg_copy("z_u", Z["u"], db("z_u"))
                dbg_copy("z_ktok", Z["ktok"], db("z_ktok"))
                pool_mixer(C, j)
                gla_mixer(C, j)
                wout = I["ev_w_out"][j]
            else:
                odd_inproj(C, j, dense, store_fm, store_tok, HT)
                rwkv_pre(C, j)
                rwkv_rec(C)
                rwkv_post(C, j)
                dil_rope(C)
                dil_attn(C)
                dil_combine(C)
                wout = I["od_w_out"][j]
            dbg_copy("mix%d" % li_real, MIXT, db("mixT"))
            dense(MIXT, db("mixT"), D, lambda k0, k1, a, b: wout[k0:k1, a:b], 0, D, "fm",
                  store_fm(YT, db("ysc"), 0))
            resid_norm_pass(x_src, x_src_buf, YT, db("ysc"), "norm_mix_post", "norm_ffn_pre", li,
                            HT, db("hT"), X, db("xs_w"))
            x_src, x_src_buf = X, db("xs")
            dbg_copy("xa%d" % li_real, X, db("xs"))
            wup = I["ffn_up"][li]

            def ev_up(c0, csz, t0, tsz, pap, pb):
                st, stb = C.evr.next()
                S.op("act", lambda: nc.scalar.activation(out=st[:csz, :tsz], in_=pap, func=AF.Relu), reads=[pb], writes=[stb])
                sb2, sb2b = C.evb.next()
                S.op("dve", lambda: nc.vector.tensor_tensor(out=sb2[:csz, :tsz], in0=st[:csz, :tsz], in1=st[:csz, :tsz], op=ALU.mult),
                     reads=[stb], writes=[sb2b])
                S.dma(STQ, AT[t0 // 256:(t0 + tsz) // 256, :, c0 // 128, :].rearrange("i p t -> p i t"),
                      sb2[:, :tsz].rearrange("p (i t) -> p i t", t=256), reads=[sb2b], writes=[db("aT")])
            dense(HT, db("hT"), D, lambda k0, k1, a, b: wup[k0:k1, a:b], 0, DFF, "fm", ev_up)
            wdn = I["ffn_down"][li]
            dense(AT, db("aT"), DFF, lambda k0, k1, a, b: wdn[k0:k1, a:b], 0, D, "fm",
                  store_fm(YT, db("ysc"), 0), TT=256)
            resid_norm_pass(X, db("xs"), YT, db("ysc"), "norm_ffn_post", "ple_norm", li, HT, db("hT_w"), X, db("xs_w"))
            dbg_copy("xb%d" % li_real, X, db("xs"))
            ple_pass(C, li, dense, HT, X)
            dbg_copy("x%d" % li_real, X, db("xs"))
        S.dma("sp", yT, X, reads=[db("xs")], writes=[db("yT")])
        barrier()
        C.ninst = S.ninst
    return nc, C


def ple_pass(C, li, dense, HT, X):
    nc, S, L = C.nc, C.S, C.L
    I, db = C.I, C.db
    with C.phase():
        R = dict(
            pw_st=C.sb([128, 2, D], F32, "plew_st"), pw=C.sb([128, 2, D], BF16, "plew"), pwb=Buf(), pwsb=Buf(),
            pst=C.ring(2, [128, 2, 512], F32, "pst"), pbf=C.ring(2, [128, 2, 512], BF16, "pbf"),
            xc=C.ring(2, [128, 512], F32, "plex"), gt=C.ring(2, [128, 512], F32, "pleg"))
        S.dma("sp", R["pw_st"][:], I["ple_proj"][li].rearrange("(c p) n -> p c n", p=128), writes=[R["pwsb"]])
        S.op("dve", lambda: nc.vector.tensor_copy(out=R["pw"][:], in_=R["pw_st"][:]), reads=[R["pwsb"]], writes=[R["pwb"]])
        wg = I["ple_gate"][li]
        cur = {"t0": -1, "pb": None, "pbb": None}

        def ev(c0, csz, t0, tsz, pap, pb):
            if cur["t0"] != t0:
                st, stb = R["pst"].next()
                S.dma("sp", st[:, :, :tsz], I["pT"][li].rearrange("(c p) l -> p c l", p=128)[:, :, t0:t0 + tsz], writes=[stb])
                pbf, pbfb = R["pbf"].next()
                S.op("pool", lambda: nc.gpsimd.tensor_copy(out=pbf[:, :, :tsz], in_=st[:, :, :tsz]), reads=[stb], writes=[pbfb])
                cur["t0"], cur["pb"], cur["pbb"] = t0, pbf, pbfb
            pbf, pbfb = cur["pb"], cur["pbb"]
            pp, ppb = C.aux2.next()
            for kc in range(2):
                S.op("pe", lambda: nc.tensor.matmul(pp[:csz, :tsz], lhsT=R["pw"][:, kc, c0:c0 + csz], rhs=pbf[:, kc, :tsz],
                                                    start=(kc == 0), stop=(kc == 1)),
                     reads=[R["pwb"], pbfb], writes=[ppb])
            gt, gtb = R["gt"].next()
            S.op("act", lambda: nc.scalar.activation(out=gt[:csz, :tsz], in_=pap, func=AF.Sigmoid), reads=[pb], writes=[gtb])
            S.op("dve", lambda: nc.vector.tensor_tensor(out=gt[:csz, :tsz], in0=gt[:csz, :tsz], in1=pp[:csz, :tsz], op=ALU.mult),
                 reads=[ppb], writes=[gtb])
            xc, xcb = R["xc"].next()
            S.dma("sp", xc[:csz, :tsz], X[c0:c0 + csz, t0:t0 + tsz], reads=[db("xs")], writes=[xcb])
            S.op("pool", lambda: nc.gpsimd.tensor_tensor(out=xc[:csz, :tsz], in0=xc[:csz, :tsz], in1=gt[:csz, :tsz], op=ALU.add),
                 reads=[gtb], writes=[xcb])
            S.dma(STQ, X[c0:c0 + csz, t0:t0 + tsz], xc[:csz, :tsz], reads=[xcb], writes=[db("xs_w")])
        dense(HT, db("hT"), D, lambda k0, k1, a, b: wg[k0:k1, a:b], 0, D, "fm", ev)


def pool_mixer(C, j):
    nc, S, L = C.nc, C.S, C.L
    I, Z, db = C.I, C.Z, C.db
    TT = 512
    with C.phase():
        R = dict(
            w_st=C.sb([128, 4, 128], F32, "poolw_st"), w=C.sb([128, 4, 128], BF16, "poolw"), wb=Buf(), wsb=Buf(),
            sc=C.sb([128, 4], F32, "poolsc"), scb=Buf(),
            cnt=C.sb([128, 4, 16], F32, "poolcnt"), cntb=Buf(),
            u=C.ring(2, [128, 16 + TT], F32, "pool_u"), s=C.ring(2, [128, 16 + TT], F32, "pool_s"),
            s2=C.ring(2, [128, 16 + TT], F32, "pool_s2"), tmp=C.ring(1, [128, 16], F32, "pool_tmp"),
            pb=C.ring(2, [128, TT], BF16, "pool_pb"), o=C.ring(2, [128, TT], BF16, "pool_o"))
        S.dma("sp", R["w_st"][:], I["pool_w"][j].rearrange("g c d -> c g d"), writes=[R["wsb"]])
        S.op("dve", lambda: nc.vector.tensor_copy(out=R["w"][:], in_=R["w_st"][:]), reads=[R["wsb"]], writes=[R["wb"]])
        S.dma("sp", R["sc"][:], I["pool_scale"][j], writes=[R["scb"]])
        S.dma("sp", R["cnt"][:], I["c_poolcnt"].rearrange("g p t -> p g t"), writes=[R["cntb"]])
        for gi, w in enumerate((2, 4, 8, 16)):
            for t0 in range(0, L, TT):
                u, ub = R["u"].next()
                if t0 == 0:
                    S.op("pool", lambda: nc.gpsimd.memset(u[:, 0:16], 0.0), writes=[ub])
                    S.dma("sp", u[:, 16:16 + TT], Z["u"][gi * 128:(gi + 1) * 128, 0:TT], reads=[db("z_u")], writes=[ub])
                else:
                    S.dma("sp", u[:, :], Z["u"][gi * 128:(gi + 1) * 128, t0 - 16:t0 + TT], reads=[db("z_u")], writes=[ub])
                a, ab = u, ub
                m = 1
                srcs = [R["s"], R["s2"]]
                k = 0
                while m < w:
                    d, dbf = srcs[k % 2].next()
                    k += 1
                    S.op("dve", lambda: nc.vector.memset(d[:, 0:m], 0.0), writes=[dbf])
                    S.op("dve", lambda: nc.vector.tensor_tensor(out=d[:, m:], in0=a[:, m:], in1=a[:, :16 + TT - m], op=ALU.add),
                         reads=[ab], writes=[dbf])
                    a, ab = d, dbf
                    m *= 2
                pbt, pbb = R["pb"].next()
                S.op("dve", lambda: nc.vector.scalar_tensor_tensor(out=pbt[:, :], in0=a[:, 16:], scalar=1.0 / w, in1=u[:, 16:],
                                                                   op0=ALU.mult, op1=ALU.subtract),
                     reads=[ab, ub], writes=[pbb])
                if t0 == 0:
                    tmp, tmpb = R["tmp"].next()
                    S.op("dve", lambda: nc.vector.tensor_tensor(out=tmp[:, 0:16], in0=a[:, 16:32], in1=R["cnt"][:, gi, :], op=ALU.mult),
                         reads=[ab, R["cntb"]], writes=[tmpb])
                    S.op("dve", lambda: nc.vector.tensor_tensor(out=pbt[:, 0:16], in0=tmp[:, 0:16], in1=u[:, 16:32], op=ALU.subtract),
                         reads=[tmpb, ub], writes=[pbb])
                pp, ppb = C.aux2.next()
                S.op("pe", lambda: nc.tensor.matmul(pp[:, :TT], lhsT=R["w"][:, gi, :], rhs=pbt[:, :], start=True, stop=True),
                     reads=[R["wb"], pbb], writes=[ppb])
                o, ob = R["o"].next()
                S.op("act", lambda: nc.scalar.activation(out=o[:, :], in_=pp[:, :TT], func=AF.Copy, scale=R["sc"][:, gi:gi + 1]),
                     reads=[ppb, R["scb"]], writes=[ob])
                S.dma("pool", C.MIXT[gi * 128:(gi + 1) * 128, t0:t0 + TT], o[:, :], reads=[ob], writes=[db("mixT")])


def gla_mixer(C, j):
    nc, S, L = C.nc, C.S, C.L
    I, Z, db = C.I, C.Z, C.db
    TT = 512
    NCH = TT // 64
    DK, DV = 192, 384
    DCH = ((0, 128), (128, 64))
    with C.phase():
        sb, ring = C.sb, C.ring
        gwb = sb([17, 768], F32, "gwb"); b_gwb = Buf()
        S.dma("sp", gwb[:], I["gla_gate_wb"][j], writes=[b_gwb])
        gnorm = sb([128, 3], F32, "gnorm"); b_gnorm = Buf()
        S.dma("sp", gnorm[:], I["gla_norm"][j], writes=[b_gnorm])
        tri_s = sb([64, 64], F32, "tri_s"); b_tris = Buf()
        S.op("dve", lambda: nc.vector.tensor_scalar_mul(out=tri_s[:], in0=C.tri_f[:], scalar1=-1.0 / 16.0),
             reads=[C.b_tri], writes=[b_tris])
        ones_s = sb([64, 64], F32, "ones_s"); b_oness = Buf()
        S.op("pool", lambda: nc.gpsimd.memset(ones_s[:], -1.0 / 16.0), writes=[b_oness])
        for h in range(4):
          with C.phase():
              St = [sb([sz, DV], F32, "S%d" % i) for i, (_, sz) in enumerate(DCH)]
              Sb = [sb([sz, DV], BF16, "Sb%d" % i) for i, (_, sz) in enumerate(DCH)]
              bS = [Buf(), Buf()]
              bSb = [Buf(), Buf()]
              for i in range(2):
                  S.op("pool", lambda: nc.gpsimd.memset(St[i][:], 0.0), writes=[bS[i]])
                  S.op("pool", lambda: nc.gpsimd.memset(Sb[i][:], 0.0), writes=[bSb[i]])
              glr_r = ring(2, [17, TT], F32, "glr")
              q_r = [ring(2, [sz, TT], F32, "q%d" % i) for i, (_, sz) in enumerate(DCH)]
              k_r = [ring(2, [sz, TT], F32, "k%d" % i) for i, (_, sz) in enumerate(DCH)]
              kt_r = ring(2, [64, NCH, DK], F32, "kt")
              vt_r = ring(2, [64, NCH, DV], F32, "vt")
              vb_r = ring(2, [64, NCH, DV], BF16, "vb")
              go_r = ring(2, [128, 3, TT], F32, "go")
              o_r = ring(2, [128, 3, TT], F32, "o")
              e_r = ring(2, [64, DK], F32, "e")
              l_r = ring(2, [64, DK], F32, "l")
              bt_r = ring(2, [64, DK], F32, "bt")
              kh_r = ring(2, [64, DK], BF16, "kh")
              eb_r = [ring(2, [sz, 64], F32, "eb%d" % i) for i, (_, sz) in enumerate(DCH)]
              enb_r = [ring(2, [sz, 64], F32, "enb%d" % i) for i, (_, sz) in enumerate(DCH)]
              qs_r = [ring(2, [sz, 64], BF16, "qs%d" % i) for i, (_, sz) in enumerate(DCH)]
              ks_r = [ring(2, [sz, 64], BF16, "ks%d" % i) for i, (_, sz) in enumerate(DCH)]
              dec_r = [ring(2, [sz, 1], F32, "dec%d" % i) for i, (_, sz) in enumerate(DCH)]
              at_r = ring(2, [64, 64], BF16, "attb")
              NR = dict(sq=ring(1, [128, 3, TT], F32, "gsq"), rs=ring(2, [128, TT], F32, "grs"))
              sg_r = ring(2, [128, 3, TT], F32, "sg")
              ob_r = ring(2, [128, 3, TT], BF16, "ob")
              for t0 in range(0, L, TT):
                  glr, glrb = glr_r.next()
                  S.op("pool", lambda: nc.gpsimd.memset(glr[:, :], 1.0), writes=[glrb])
                  S.dma("sp", glr[0:16, :], Z["glr"][:, t0:t0 + TT], reads=[db("z_glr")], writes=[glrb])
                  qt, qb, kt_, kb_ = [], [], [], []
                  for i, (d0, sz) in enumerate(DCH):
                      t, b = q_r[i].next()
                      S.dma("sp", t[:, :], Z["q"][h * DK + d0: h * DK + d0 + sz, t0:t0 + TT], reads=[db("z_q")], writes=[b])
                      qt.append(t); qb.append(b)
                      t, b = k_r[i].next()
                      S.dma("sp", t[:, :], Z["k"][h * DK + d0: h * DK + d0 + sz, t0:t0 + TT], reads=[db("z_k")], writes=[b])
                      kt_.append(t); kb_.append(b)
                  ktok, ktokb = kt_r.next()
                  S.dma("sp", ktok[:], Z["ktok"][t0:t0 + TT, h * DK:(h + 1) * DK].rearrange("(c p) d -> p c d", p=64),
                        reads=[db("z_ktok")], writes=[ktokb])
                  vtok, vtokb = vt_r.next()
                  S.dma("sp", vtok[:], Z["vtok"][t0:t0 + TT, h * DV:(h + 1) * DV].rearrange("(c p) d -> p c d", p=64),
                        reads=[db("z_vtok")], writes=[vtokb])
                  vb, vbb = vb_r.next()
                  S.op("pool", lambda: nc.gpsimd.tensor_copy(out=vb[:], in_=vtok[:]), reads=[vtokb], writes=[vbb])
                  go, gob = go_r.next()
                  S.dma("sp", go[:], Z["gout"][h * DV:(h + 1) * DV, t0:t0 + TT].rearrange("(c p) l -> p c l", p=128),
                        reads=[db("z_gout")], writes=[gob])
                  ot, otb = o_r.next()
                  for c in range(NCH):
                      cs = slice(c * 64, (c + 1) * 64)
                      pg, pgb = C.aux.next()
                      S.op("pe", lambda: nc.tensor.matmul(pg[:64, :DK], lhsT=glr[:, cs], rhs=gwb[:, h * DK:(h + 1) * DK],
                                                          start=True, stop=True), reads=[glrb, b_gwb], writes=[pgb])
                      e, eb_ = e_r.next()
                      S.op("act", lambda: nc.scalar.activation(out=e[:], in_=pg[:64, :DK], func=AF.Exp, scale=-1.0),
                           reads=[pgb], writes=[eb_])
                      l, lb = l_r.next()
                      S.op("act", lambda: nc.scalar.activation(out=l[:], in_=e[:], func=AF.Ln, bias=1.0),
                           reads=[eb_], writes=[lb])
                      pbk, pbkb = C.aux.next()
                      S.op("pe", lambda: nc.tensor.matmul(pbk[:64, :DK], lhsT=tri_s[:, :], rhs=l[:, :], start=True, stop=True),
                           reads=[b_tris, lb], writes=[pbkb])
                      pbl, pblb = C.aux2.next()
                      S.op("pe", lambda: nc.tensor.matmul(pbl[:64, :DK], lhsT=ones_s[:, :], rhs=l[:, :], start=True, stop=True),
                           reads=[b_oness, lb], writes=[pblb])
                      bt, btb = bt_r.next()
                      S.op("act", lambda: nc.scalar.copy(out=bt[:], in_=pbk[:64, :DK]), reads=[pbkb], writes=[btb])
                      S.op("dve", lambda: nc.vector.tensor_tensor(out=bt[:], in0=pbl[:64, :DK], in1=bt[:], op=ALU.subtract),
                           reads=[pblb], writes=[btb])
                      S.op("act", lambda: nc.scalar.activation(out=bt[:], in_=bt[:], func=AF.Exp), reads=[], writes=[btb])
                      kh, khb = kh_r.next()
                      S.op("dve", lambda: nc.vector.tensor_tensor(out=kh[:], in0=bt[:], in1=ktok[:, c, :], op=ALU.mult),
                           reads=[btb, ktokb], writes=[khb])
                      qs, qsb, ks, ksb, dec, decb = [], [], [], [], [], []
                      for i, (d0, sz) in enumerate(DCH):
                          pf, pfb = C.aux2.next()
                          S.op("pe", lambda: nc.tensor.matmul(pf[:sz, :64], lhsT=l[:, d0:d0 + sz], rhs=tri_s[:, :], start=True, stop=True),
                               reads=[lb, b_tris], writes=[pfb])
                          ebt, ebb = eb_r[i].next()
                          S.op("act", lambda: nc.scalar.activation(out=ebt[:], in_=pf[:sz, :64], func=AF.Exp), reads=[pfb], writes=[ebb])
                          enb, enbb = enb_r[i].next()
                          S.op("act", lambda: nc.scalar.activation(out=enb[:], in_=pf[:sz, :64], func=AF.Exp, scale=-1.0),
                               reads=[pfb], writes=[enbb])
                          d_, db_ = dec_r[i].next()
                          S.op("dve", lambda: nc.vector.tensor_copy(out=d_[:], in_=ebt[:, 63:64]), reads=[ebb], writes=[db_])
                          dec.append(d_); decb.append(db_)
                          q_, qb_ = qs_r[i].next()
                          S.op("dve", lambda: nc.vector.scalar_tensor_tensor(out=q_[:], in0=qt[i][:, cs], scalar=DK ** -0.5, in1=ebt[:],
                                                                             op0=ALU.mult, op1=ALU.mult),
                               reads=[qb[i], ebb], writes=[qb_])
                          qs.append(q_); qsb.append(qb_)
                          k_, kb2 = ks_r[i].next()
                          S.op("pool", lambda: nc.gpsimd.tensor_tensor(out=k_[:], in0=kt_[i][:, cs], in1=enb[:], op=ALU.mult),
                               reads=[kb_[i], enbb], writes=[kb2])
                          ks.append(k_); ksb.append(kb2)
                      pa, pab = C.acc.next()
                      for i in range(2):
                          S.op("pe", lambda: nc.tensor.matmul(pa[:64, :64], lhsT=ks[i][:, :], rhs=qs[i][:, :], start=(i == 0), stop=(i == 1)),
                               reads=[ksb[i], qsb[i]], writes=[pab])
                      att, attb = at_r.next()
                      S.op("dve", lambda: nc.vector.tensor_tensor(out=att[:], in0=pa[:64, :64], in1=C.trimask[:, :], op=ALU.mult),
                           reads=[pab, C.b_trimask], writes=[attb])
                      po, pob = C.acc.next()
                      for ec in range(3):
                          es_ = slice(ec * 128, (ec + 1) * 128)
                          S.op("pe", lambda: nc.tensor.matmul(po[:, ec * 64:(ec + 1) * 64], lhsT=vb[:, c, es_], rhs=att[:, :],
                                                              start=True, stop=False), reads=[vbb, attb], writes=[pob])
                          for i in range(2):
                              S.op("pe", lambda: nc.tensor.matmul(po[:, ec * 64:(ec + 1) * 64], lhsT=Sb[i][:, es_], rhs=qs[i][:, :],
                                                                  start=False, stop=(i == 1)), reads=[bSb[i], qsb[i]], writes=[pob])
                      S.op("act", lambda: nc.scalar.copy(out=ot[:, :, cs], in_=po[:, 0:192].rearrange("p (c i) -> p c i", i=64)),
                           reads=[pob], writes=[otb])
                      for i, (d0, sz) in enumerate(DCH):
                          pst, pstb = C.acc.next()
                          S.op("pe", lambda: nc.tensor.matmul(pst[:sz, :DV], lhsT=kh[:, d0:d0 + sz], rhs=vb[:, c, :], start=True, stop=True),
                               reads=[khb, vbb], writes=[pstb])
                          S.op("dve", lambda: nc.vector.scalar_tensor_tensor(out=St[i][:], in0=St[i][:], scalar=dec[i][:, 0:1], in1=pst[:sz, :DV],
                                                                             op0=ALU.mult, op1=ALU.add),
                               reads=[decb[i], pstb, bSb[i]], writes=[bS[i]])
                          S.op("pool", lambda: nc.gpsimd.tensor_copy(out=Sb[i][:], in_=St[i][:]), reads=[bS[i]], writes=[bSb[i]])
                  rs, rsb = C.rstd_of(NR, ot, otb, 3, TT)
                  sg, sgb = sg_r.next()
                  S.op("act", lambda: nc.scalar.activation(out=sg[:], in_=go[:], func=AF.Silu), reads=[gob], writes=[sgb])
                  ob, obb = ob_r.next()
                  for ec in range(3):
                      S.op("dve", lambda: nc.vector.scalar_tensor_tensor(out=sg[:, ec, :], in0=sg[:, ec, :], scalar=gnorm[:, ec:ec + 1],
                                                                         in1=rs[:, :], op0=ALU.mult, op1=ALU.mult),
                           reads=[rsb, b_gnorm], writes=[sgb])
                  S.op("pool", lambda: nc.gpsimd.tensor_tensor(out=ob[:], in0=sg[:], in1=ot[:], op=ALU.mult),
                       reads=[sgb, otb], writes=[obb])
                  S.dma("pool", C.MIXT[512 + h * DV: 512 + (h + 1) * DV, t0:t0 + TT].rearrange("(c p) l -> p c l", p=128), ob[:],
                        reads=[obb], writes=[db("mixT")])
                  if "gla_o" in C.DBG:
                      S.dma("sp", C.DBG["gla_o"][h * DV:(h + 1) * DV, t0:t0 + TT].rearrange("(c p) l -> p c l", p=128), ot[:],
                            reads=[otb], writes=[db("dbg_gla_o")])


def odd_inproj(C, j, dense, store_fm, store_tok, HT):
    I, db, OZ = C.I, C.db, C.OZ
    W = I["od_w_in"][j]
    wfn = lambda k0, k1, a, b: W[k0:k1, a:b]
    dense(HT, db("hT"), D, wfn, 0, 3072, "tok", store_tok(OZ["rkv"], db("o_rkv"), 0))
    dense(HT, db("hT"), D, wfn, 3072, 3360, "fm", store_fm(OZ["lr"], db("o_lr"), 3072))
    dense(HT, db("hT"), D, wfn, 3360, 4384, "fm", store_fm(OZ["dq"], db("o_dq"), 3360))
    dense(HT, db("hT"), D, wfn, 4384, 5408, "fm", store_fm(OZ["dk"], db("o_dk"), 4384))
    dense(HT, db("hT"), D, wfn, 5408, 6432, "tok", store_tok(OZ["dv"], db("o_dv"), 5408))


def bc3(ap16):
    return ap16.unsqueeze(2).to_broadcast([128, 16, 64])


def v3(ap):
    return ap.rearrange("p (h k) -> p h k", k=64)


def rwkv_pre(C, j):
    nc, S, L = C.nc, C.S, C.L
    I, db, OZ = C.I, C.db, C.OZ
    sb, ring = C.sb, C.ring
    with C.phase():
        def bcast(name, src, n):
            t = sb([128, n], F32, name); b = Buf()
            S.dma("sp", t[:], src.partition_broadcast(128), writes=[b])
            return t, b
        mu, mub = bcast("mu", I["rwkv_mu"][j][:, 0:3072], 3072)
        kk_, kkb = bcast("kk_", I["rwkv_k_k"][j], 1024)
        ka_, kab = bcast("ka_", I["rwkv_k_a"][j], 1024)
        rk_, rkb = bcast("rk_", I["rwkv_r_k"][j], 1024)
        w2a = sb([65, 1024], F32, "w2a"); w2ab = Buf()
        S.dma("sp", w2a[:], I["rwkv_w2a"][j], writes=[w2ab])
        a2a = sb([65, 1024], F32, "a2a"); a2ab = Buf()
        S.dma("sp", a2a[:], I["rwkv_a2a"][j], writes=[a2ab])
        g2a = sb([128, 1024], F32, "g2a"); g2ab = Buf()
        S.dma("sp", g2a[:], I["rwkv_g2"][j][0:128, :], writes=[g2ab])
        g2b = sb([32, 1024], F32, "g2b"); g2bb = Buf()
        S.dma("sp", g2b[:], I["rwkv_g2"][j][128:160, :], writes=[g2bb])
        LRS = ((0, 64), (64, 64), (128, 128), (256, 32))
        mul = []
        for i, (r0, sz) in enumerate(LRS):
            t = sb([sz, 1], F32, "mul%d" % i); b = Buf()
            S.dma("sp", t[:], I["rwkv_mu_lr"][j][r0:r0 + sz, :], writes=[b])
            mul.append((t, b))
        ident, identb = C.ident_f, C.b_identf
        cur_r = ring(2, [128, 3072], F32, "cur")
        prev_r = ring(2, [128, 3072], F32, "prev")
        hs_r = ring(1, [128, 3072], F32, "hs")
        lr_r = [ring(2, [sz, 129], F32, "lr%d" % i) for i, (_, sz) in enumerate(LRS)]
        ls_r = [ring(1, [sz, 128], F32, "ls%d" % i) for i, (_, sz) in enumerate(LRS)]
        th = sb([65, 128], F32, "th"); thb = Buf()
        S.op("pool", lambda: nc.gpsimd.memset(th[:], 1.0), writes=[thb])
        haa = sb([65, 128], F32, "haa"); haab = Buf()
        S.op("pool", lambda: nc.gpsimd.memset(haa[:], 1.0), writes=[haab])
        sg0 = sb([128, 128], F32, "sg0"); sg0b = Buf()
        sg1 = sb([32, 128], F32, "sg1"); sg1b = Buf()
        T = {n: ring(1, [128, 1024], F32, n) for n in ("dec", "a", "g", "kk", "tmp", "kmod", "avec", "bvec", "tmp2", "bon")}
        small = {n: ring(1, [128, 16], F32, n) for n in ("ss", "rn", "rkk")}
        fm_r = ring(2, [64, 16, 128], F32, "fmq")
        lw_r = ring(1, [128, 1024], F32, "lw")
        prr = Ring(C.acc.tiles + C.aux.tiles + C.aux2.tiles)
        evac_i = [0]
        for t0 in range(0, L, 128):
            cur, curb = cur_r.next()
            S.dma("sp", cur[:], OZ["rkv"][t0:t0 + 128, :], reads=[db("o_rkv")], writes=[curb])
            prev, prevb = prev_r.next()
            if t0 == 0:
                S.op("pool", lambda: nc.gpsimd.memset(prev[:], 0.0), writes=[prevb])
                S.dma("sp", prev[1:128, :], OZ["rkv"][0:127, :], reads=[db("o_rkv")], writes=[prevb])
            else:
                S.dma("sp", prev[:], OZ["rkv"][t0 - 1:t0 + 127, :], reads=[db("o_rkv")], writes=[prevb])
            hs, hsb = hs_r.next()
            S.op("dve", lambda: nc.vector.tensor_tensor(out=hs[:], in0=prev[:], in1=cur[:], op=ALU.subtract),
                 reads=[prevb, curb], writes=[hsb])
            S.op("pool", lambda: nc.gpsimd.tensor_tensor(out=hs[:], in0=hs[:], in1=mu[:], op=ALU.mult), reads=[mub], writes=[hsb])
            S.op("dve", lambda: nc.vector.tensor_tensor(out=hs[:], in0=hs[:], in1=cur[:], op=ALU.add), reads=[curb], writes=[hsb])
            r_, k_, v_ = hs[:, 0:1024], hs[:, 1024:2048], hs[:, 2048:3072]
            ls = []
            for i, (r0, sz) in enumerate(LRS):
                lt, ltb = lr_r[i].next()
                if t0 == 0:
                    S.op("pool", lambda: nc.gpsimd.memset(lt[:, 0:1], 0.0), writes=[ltb])
                    S.dma("sp", lt[:, 1:129], OZ["lr"][r0:r0 + sz, 0:128], reads=[db("o_lr")], writes=[ltb])
                else:
                    S.dma("sp", lt[:, :], OZ["lr"][r0:r0 + sz, t0 - 1:t0 + 128], reads=[db("o_lr")], writes=[ltb])
                st, stb = ls_r[i].next()
                S.op("dve", lambda: nc.vector.tensor_tensor(out=st[:], in0=lt[:, 0:128], in1=lt[:, 1:129], op=ALU.subtract),
                     reads=[ltb], writes=[stb])
                S.op("dve", lambda: nc.vector.scalar_tensor_tensor(out=st[:], in0=st[:], scalar=mul[i][0][:, 0:1], in1=lt[:, 1:129],
                                                                   op0=ALU.mult, op1=ALU.add), reads=[ltb, mul[i][1]], writes=[stb])
                ls.append((st, stb))
            S.op("act", lambda: nc.scalar.activation(out=th[0:64, :], in_=ls[0][0][:], func=AF.Tanh), reads=[ls[0][1]], writes=[thb])
            S.op("act", lambda: nc.scalar.copy(out=haa[0:64, :], in_=ls[1][0][:]), reads=[ls[1][1]], writes=[haab])
            S.op("act", lambda: nc.scalar.activation(out=sg0[:], in_=ls[2][0][:], func=AF.Sigmoid), reads=[ls[2][1]], writes=[sg0b])
            S.op("act", lambda: nc.scalar.activation(out=sg1[:], in_=ls[3][0][:], func=AF.Sigmoid), reads=[ls[3][1]], writes=[sg1b])
            dec, decb = T["dec"].next()
            a_, ab = T["a"].next()
            g_, gb = T["g"].next()
            for hf in range(2):
                cs = slice(hf * 512, (hf + 1) * 512)
                p1, p1b = prr.next()
                S.op("pe", lambda: nc.tensor.matmul(p1[:, :], lhsT=th[:, :], rhs=w2a[:, cs], start=True, stop=True),
                     reads=[thb, w2ab], writes=[p1b])
                S.op("act", lambda: nc.scalar.activation(out=dec[:, cs], in_=p1[:, :], func=AF.Sigmoid), reads=[p1b], writes=[decb])
                p2, p2b = prr.next()
                S.op("pe", lambda: nc.tensor.matmul(p2[:, :], lhsT=haa[:, :], rhs=a2a[:, cs], start=True, stop=True),
                     reads=[haab, a2ab], writes=[p2b])
                S.op("act", lambda: nc.scalar.activation(out=a_[:, cs], in_=p2[:, :], func=AF.Sigmoid), reads=[p2b], writes=[ab])
                p3, p3b = prr.next()
                S.op("pe", lambda: nc.tensor.matmul(p3[:, :], lhsT=sg0[:, :], rhs=g2a[:, cs], start=True, stop=False),
                     reads=[sg0b, g2ab], writes=[p3b])
                S.op("pe", lambda: nc.tensor.matmul(p3[:, :], lhsT=sg1[:, :], rhs=g2b[:, cs], start=False, stop=True),
                     reads=[sg1b, g2bb], writes=[p3b])
                S.op("dve", lambda: nc.vector.tensor_copy(out=g_[:, cs], in_=p3[:, :]), reads=[p3b], writes=[gb])
            lw, lwb = lw_r.next()
            S.op("act", lambda: nc.scalar.mul(out=lw[:], in_=dec[:], mul=-float(np.exp(-0.5))), reads=[decb], writes=[lwb])
            S.dma("pool", OZ["gt"][t0:t0 + 128, :], g_[:], reads=[gb], writes=[db("o_gt")])
            kk, kkb2 = T["kk"].next()
            S.op("dve", lambda: nc.vector.tensor_tensor(out=kk[:], in0=k_, in1=kk_[:], op=ALU.mult), reads=[hsb, kkb], writes=[kkb2])
            tmp, tmpb = T["tmp"].next()
            S.op("pool", lambda: nc.gpsimd.tensor_tensor(out=tmp[:], in0=kk[:], in1=kk[:], op=ALU.mult), reads=[kkb2], writes=[tmpb])
            ss, ssb = small["ss"].next()
            S.op("dve", lambda: nc.vector.tensor_reduce(out=ss[:], in_=v3(tmp[:]), axis=AX.X, op=ALU.add), reads=[tmpb], writes=[ssb])
            S.op("act", lambda: nc.scalar.activation(out=ss[:], in_=ss[:], func=AF.Sqrt), reads=[], writes=[ssb])
            S.op("dve", lambda: nc.vector.tensor_scalar_max(out=ss[:], in0=ss[:], scalar1=1e-12), reads=[], writes=[ssb])
            S.op("dve", lambda: nc.vector.reciprocal(out=ss[:], in_=ss[:]), reads=[], writes=[ssb])
            S.op("dve", lambda: nc.vector.tensor_tensor(out=v3(kk[:]), in0=v3(kk[:]), in1=bc3(ss[:]), op=ALU.mult), reads=[ssb], writes=[kkb2])
            km, kmb = T["kmod"].next()
            S.op("dve", lambda: nc.vector.scalar_tensor_tensor(out=km[:], in0=a_[:], scalar=-1.0, in1=ka_[:], op0=ALU.add, op1=ALU.mult),
                 reads=[ab, kab], writes=[kmb])
            S.op("dve", lambda: nc.vector.scalar_tensor_tensor(out=km[:], in0=km[:], scalar=1.0, in1=k_, op0=ALU.add, op1=ALU.mult),
                 reads=[hsb], writes=[kmb])
            av, avb = T["avec"].next()
            S.op("act", lambda: nc.scalar.mul(out=av[:], in_=kk[:], mul=-1.0), reads=[kkb2], writes=[avb])
            bv, bvb = T["bvec"].next()
            S.op("pool", lambda: nc.gpsimd.tensor_tensor(out=bv[:], in0=kk[:], in1=a_[:], op=ALU.mult), reads=[kkb2, ab], writes=[bvb])
            t2, t2b = T["tmp2"].next()
            S.op("pool", lambda: nc.gpsimd.tensor_tensor(out=t2[:], in0=r_, in1=km[:], op=ALU.mult), reads=[hsb, kmb], writes=[t2b])
            S.op("pool", lambda: nc.gpsimd.tensor_tensor(out=t2[:], in0=t2[:], in1=rk_[:], op=ALU.mult), reads=[rkb], writes=[t2b])
            rkk, rkkb = small["rkk"].next()
            S.op("dve", lambda: nc.vector.tensor_reduce(out=rkk[:], in_=v3(t2[:]), axis=AX.X, op=ALU.add), reads=[t2b], writes=[rkkb])
            bon, bonb = T["bon"].next()
            S.op("dve", lambda: nc.vector.tensor_tensor(out=v3(bon[:]), in0=v3(v_), in1=bc3(rkk[:]), op=ALU.mult),
                 reads=[hsb, rkkb], writes=[bonb])
            S.dma("pool", OZ["bon"][t0:t0 + 128, :], bon[:], reads=[bonb], writes=[db("o_bon")])
            for qi, (tq, tqb) in enumerate(((lw[:], lwb), (bv[:], bvb), (km[:], kmb), (v_, hsb))):
                S.dma("pool", OZ["tok4"][qi][t0:t0 + 128, :], tq, reads=[tqb], writes=[db("o_tok4")])
            for qi, (tq, tqb) in enumerate(((av[:], avb), (bv[:], bvb), (km[:], kmb), (r_, hsb))):
                ft, ftb = fm_r.next()
                for g4 in range(4):
                    pt, ptb = prr.next()
                    for q4 in range(4):
                        h = g4 * 4 + q4
                        S.op("pe", lambda: nc.tensor.transpose(out=pt[:64, q4 * 128:(q4 + 1) * 128], in_=tq[:, h * 64:(h + 1) * 64],
                                                               identity=ident[:]), reads=[tqb, identb], writes=[ptb])
                    evac_i[0] += 1
                    if evac_i[0] % 2:
                        S.op("act", lambda: nc.scalar.copy(out=ft[:, g4 * 4:(g4 + 1) * 4, :], in_=pt[:64, :].rearrange("p (c t) -> p c t", t=128)),
                             reads=[ptb], writes=[ftb])
                    else:
                        S.op("dve", lambda: nc.vector.tensor_copy(out=ft[:, g4 * 4:(g4 + 1) * 4, :], in_=pt[:64, :].rearrange("p (c t) -> p c t", t=128)),
                             reads=[ptb], writes=[ftb])
                S.dma("pool", OZ["fm4"][qi][:, :, t0:t0 + 128], ft[:], reads=[ftb], writes=[db("o_fm4")])


def rwkv_rec(C):
    nc, S, L = C.nc, C.S, C.L
    I, db, OZ = C.I, C.db, C.OZ
    sb, ring = C.sb, C.ring
    CH = 64
    with C.phase():
        def const(name):
            t = sb([64, 64], F32, name); b = Buf()
            S.dma("sp", t[:], I[name], writes=[b])
            return t, b
        tri_i, tri_ib = C.tri_f, C.b_tri
        tri_s, tri_sb = const("c_su")
        m_sl, m_slb = const("c_sl")
        m_id, m_idb = const("c_i64")

        def bc8(m):
            return m[:, :].unsqueeze(1).to_broadcast([64, 8, 64])
        ST = sb([64, 16, 64], F32, "ST"); STb = Buf()
        S.op("pool", lambda: nc.gpsimd.memset(ST[:], 0.0), writes=[STb])
        pr = Ring(C.acc.tiles + C.aux.tiles + C.aux2.tiles)
        fin_r = [ring(2, [64, 16, 64], F32, "fin%d" % i) for i in range(4)]
        tin_r = [ring(2, [64, 1024], F32, "tin%d" % i) for i in range(4)]
        names = ("G", "Gi", "Ge", "At", "Bt", "Kt", "Rt", "Q", "IQ", "P", "Q2", "IQ2", "P2", "Mak", "Lrb", "Lrk", "X", "X2", "Dt")
        W = {n: (sb([64, 16, 64], F32, "w" + n), Buf()) for n in names}
        TT_ = {n: (sb([64, 1024], F32, "t" + n), Buf()) for n in ("GiT", "Btok", "Ktok", "Y")}

        def hv(t, h):
            return t[:, h, :]

        def mm16(terms):
            banks = []
            for half in range(2):
                p, pb = pr.next()
                for hh in range(8):
                    h = half * 8 + hh
                    for idx, (lf, rf, rd) in enumerate(terms):
                        S.op("pe", lambda: nc.tensor.matmul(p[:64, hh * 64:(hh + 1) * 64], lhsT=lf(h), rhs=rf(h),
                                                            start=(idx == 0), stop=(idx == len(terms) - 1)),
                             reads=rd, writes=[pb])
                banks.append((p, pb))
            return banks

        def pv(p):
            return p[:64, :].rearrange("p (c k) -> p c k", k=64)
        ei = [0]

        def ev_copy(banks, dst, dstb):
            for half, (p, pb) in enumerate(banks):
                ei[0] += 1
                o = dst[:, half * 8:(half + 1) * 8, :]
                if ei[0] % 2:
                    S.op("act", lambda: nc.scalar.copy(out=o, in_=pv(p)), reads=[pb], writes=[dstb])
                else:
                    S.op("dve", lambda: nc.vector.tensor_copy(out=o, in_=pv(p)), reads=[pb], writes=[dstb])

        def ev_mask(banks, dst, dstb, m, mb):
            for half, (p, pb) in enumerate(banks):
                o = dst[:, half * 8:(half + 1) * 8, :]
                S.op("dve", lambda: nc.vector.tensor_tensor(out=o, in0=pv(p), in1=bc8(m), op=ALU.mult), reads=[pb, mb], writes=[dstb])

        def add_id(src, srcb, dst, dstb):
            for half in range(2):
                sl = slice(half * 8, (half + 1) * 8)
                S.op("pool", lambda: nc.gpsimd.tensor_tensor(out=dst[:, sl, :], in0=src[:, sl, :], in1=bc8(m_id), op=ALU.add),
                     reads=[srcb, m_idb], writes=[dstb])

        for t0 in range(0, L, CH):
            fin = []
            for qi in range(4):
                t, b = fin_r[qi].next()
                S.dma("sp", t[:], OZ["fm4"][qi][:, :, t0:t0 + CH], reads=[db("o_fm4")], writes=[b])
                fin.append((t, b))
            tin = []
            for qi in range(4):
                t, b = tin_r[qi].next()
                S.dma("sp", t[:], OZ["tok4"][qi][t0:t0 + CH, :], reads=[db("o_tok4")], writes=[b])
                tin.append((t, b))
            (aT, aTb), (bT, bTb), (kT, kTb), (rT, rTb) = fin
            (lwk, lwkb), (btk, btkb), (ktk, ktkb), (vtk, vtkb) = tin
            bLW = mm16([(lambda h: lwk[:, h * 64:(h + 1) * 64], lambda h: tri_i[:, :], [lwkb, tri_ib])])
            bLE = mm16([(lambda h: lwk[:, h * 64:(h + 1) * 64], lambda h: tri_s[:, :], [lwkb, tri_sb])])
            (G, Gb), (Gi, Gib), (Ge, Geb) = W["G"], W["Gi"], W["Ge"]
            for half in range(2):
                sl = slice(half * 8, (half + 1) * 8)
                S.op("act", lambda: nc.scalar.activation(out=G[:, sl, :], in_=pv(bLW[half][0]), func=AF.Exp), reads=[bLW[half][1]], writes=[Gb])
                S.op("act", lambda: nc.scalar.activation(out=Gi[:, sl, :], in_=pv(bLW[half][0]), func=AF.Exp, scale=-1.0),
                     reads=[bLW[half][1]], writes=[Gib])
                S.op("act", lambda: nc.scalar.activation(out=Ge[:, sl, :], in_=pv(bLE[half][0]), func=AF.Exp), reads=[bLE[half][1]], writes=[Geb])
            (At, Atb), (Bt, Btb), (Kt, Ktb), (Rt, Rtb) = W["At"], W["Bt"], W["Kt"], W["Rt"]
            S.op("dve", lambda: nc.vector.tensor_tensor(out=At[:], in0=aT[:], in1=Ge[:], op=ALU.mult), reads=[aTb, Geb], writes=[Atb])
            S.op("pool", lambda: nc.gpsimd.tensor_tensor(out=Bt[:], in0=bT[:], in1=Gi[:], op=ALU.mult), reads=[bTb, Gib], writes=[Btb])
            S.op("dve", lambda: nc.vector.tensor_tensor(out=Kt[:], in0=kT[:], in1=Gi[:], op=ALU.mult), reads=[kTb, Gib], writes=[Ktb])
            S.op("pool", lambda: nc.gpsimd.tensor_tensor(out=Rt[:], in0=rT[:], in1=G[:], op=ALU.mult), reads=[rTb, Gb], writes=[Rtb])
            (GiT, GiTb), (Btok, Btokb), (Ktok, Ktokb), (Yt, Ytb) = TT_["GiT"], TT_["Btok"], TT_["Ktok"], TT_["Y"]
            for half in range(2):
                cs = slice(half * 512, (half + 1) * 512)
                p, pb = pr.next()
                S.op("pe", lambda: nc.tensor.matmul(p[:64, :], lhsT=tri_i[:, :], rhs=lwk[:, cs], start=True, stop=True),
                     reads=[tri_ib, lwkb], writes=[pb])
                S.op("act", lambda: nc.scalar.activation(out=GiT[:, cs], in_=p[:64, :], func=AF.Exp, scale=-1.0), reads=[pb], writes=[GiTb])
            S.op("dve", lambda: nc.vector.tensor_tensor(out=Btok[:], in0=btk[:], in1=GiT[:], op=ALU.mult), reads=[btkb, GiTb], writes=[Btokb])
            S.op("pool", lambda: nc.gpsimd.tensor_tensor(out=Ktok[:], in0=ktk[:], in1=GiT[:], op=ALU.mult), reads=[ktkb, GiTb], writes=[Ktokb])
            (Q, Qb), (IQ, IQb), (P, Pb) = W["Q"], W["IQ"], W["P"]
            ev_mask(mm16([(lambda h: hv(Bt, h), lambda h: hv(At, h), [Btb, Atb])]), Q, Qb, tri_s, tri_sb)
            add_id(Q, Qb, IQ, IQb)
            ev_mask(mm16([(lambda h: hv(At, h), lambda h: hv(Bt, h), [Btb, Atb])]), P, Pb, m_sl, m_slb)
            (Mak, Makb), (Lrb, Lrbb), (Lrk, Lrkb) = W["Mak"], W["Lrb"], W["Lrk"]
            ev_mask(mm16([(lambda h: hv(Kt, h), lambda h: hv(At, h), [Ktb, Atb])]), Mak, Makb, tri_s, tri_sb)
            ev_mask(mm16([(lambda h: hv(Bt, h), lambda h: hv(Rt, h), [Btb, Rtb])]), Lrb, Lrbb, tri_i, tri_ib)
            ev_mask(mm16([(lambda h: hv(Kt, h), lambda h: hv(Rt, h), [Ktb, Rtb])]), Lrk, Lrkb, tri_i, tri_ib)
            X, Xb = W["X"]
            X2, X2b = W["X2"]
            ev_copy(mm16([(lambda h: hv(At, h), lambda h: hv(ST, h), [Atb, STb]),
                          (lambda h: hv(Mak, h), lambda h: vtk[:, h * 64:(h + 1) * 64], [Makb, vtkb])]), X, Xb)
            cur = (Q, Qb, IQ, IQb, P, Pb)
            nxt = (W["Q2"][0], W["Q2"][1], W["IQ2"][0], W["IQ2"][1], W["P2"][0], W["P2"][1])
            for lvl in range(6):
                q, qb, iq, iqb, p_, pb_ = cur
                ev_copy(mm16([(lambda h: hv(iq, h), lambda h: hv(X, h), [iqb, Xb])]), X2, X2b)
                X, Xb, X2, X2b = X2, X2b, X, Xb
                if lvl < 5:
                    q2, q2b, iq2, iq2b, p2, p2b = nxt
                    ev_copy(mm16([(lambda h: hv(p_, h), lambda h: hv(q, h), [pb_, qb])]), q2, q2b)
                    add_id(q2, q2b, iq2, iq2b)
                    if lvl < 4:
                        ev_copy(mm16([(lambda h: hv(q, h), lambda h: hv(p_, h), [pb_, qb])]), p2, p2b)
                    cur, nxt = nxt, cur
            U, Ub = X, Xb
            bY = mm16([(lambda h: hv(Rt, h), lambda h: hv(ST, h), [Rtb, STb]),
                       (lambda h: hv(Lrb, h), lambda h: hv(U, h), [Lrbb, Ub]),
                       (lambda h: hv(Lrk, h), lambda h: vtk[:, h * 64:(h + 1) * 64], [Lrkb, vtkb])])
            for half, (p, pb) in enumerate(bY):
                S.op("act", lambda: nc.scalar.copy(out=Yt[:, half * 512:(half + 1) * 512], in_=p[:64, :]), reads=[pb], writes=[Ytb])
            S.dma("pool", OZ["ytok"][t0:t0 + CH, :], Yt[:], reads=[Ytb], writes=[db("o_ytok")])
            bD = mm16([(lambda h: Btok[:, h * 64:(h + 1) * 64], lambda h: hv(U, h), [Btokb, Ub]),
                       (lambda h: Ktok[:, h * 64:(h + 1) * 64], lambda h: vtk[:, h * 64:(h + 1) * 64], [Ktokb, vtkb])])
            for half, (p, pb) in enumerate(bD):
                sl = slice(half * 8, (half + 1) * 8)
                S.op("dve", lambda: nc.vector.tensor_tensor(out=ST[:, sl, :], in0=ST[:, sl, :], in1=pv(p), op=ALU.add), reads=[pb], writes=[STb])
                S.op("dve", lambda: nc.vector.tensor_tensor(out=ST[:, sl, :], in0=ST[:, sl, :], in1=G[:, sl, 63:64].to_broadcast([64, 8, 64]),
                                                            op=ALU.mult), reads=[Gb], writes=[STb])


def tok_to_mixT(C, src_t, src_b, row0, t0, pbr, obr):
    nc, S, db = C.nc, C.S, C.db
    ob, obb = obr.next()
    for half in range(2):
        pt, ptb = pbr.next()
        for q4 in range(4):
            cb = half * 4 + q4
            S.op("pe", lambda: nc.tensor.transpose(out=pt[:, q4 * 128:(q4 + 1) * 128], in_=src_t[:, cb * 128:(cb + 1) * 128],
                                                   identity=C.ident_f[:]), reads=[src_b, C.b_identf], writes=[ptb])
        S.op("act", lambda: nc.scalar.copy(out=ob[:, half * 4:(half + 1) * 4, :], in_=pt[:, :].rearrange("p (c t) -> p c t", t=128)),
             reads=[ptb], writes=[obb])
    S.dma("pool", C.MIXT[row0:row0 + 1024, t0:t0 + 128].rearrange("(cb p) l -> p cb l", p=128), ob[:], reads=[obb], writes=[db("mixT")])


def rwkv_post(C, j):
    nc, S, L = C.nc, C.S, C.L
    I, db, OZ = C.I, C.db, C.OZ
    sb, ring = C.sb, C.ring
    with C.phase():
        def bcast(name, src, n):
            t = sb([128, n], F32, name); b = Buf()
            S.dma("sp", t[:], src.partition_broadcast(128), writes=[b])
            return t, b
        lnw, lnwb = bcast("lnw", I["rwkv_ln_w"][j], 1024)
        lnb, lnbb = bcast("lnb", I["rwkv_ln_b"][j], 1024)
        eps2 = sb([128, 1], F32, "eps2"); eps2b = Buf()
        S.op("pool", lambda: nc.gpsimd.memset(eps2[:], 64e-5), writes=[eps2b])
        yf_r = ring(2, [128, 8, 128], F32, "yf")
        y_r = ring(2, [128, 1024], F32, "y")
        sq_r = ring(1, [128, 1024], F32, "sq")
        bon_r = ring(2, [128, 1024], F32, "bonl")
        g_r = ring(2, [128, 1024], F32, "gl")
        sm = {n: ring(2, [128, 16], F32, n) for n in ("mean", "var")}
        ob_r = ring(2, [128, 8, 128], BF16, "obm")
        for t0 in range(0, L, 128):
            bon, bonb = bon_r.next()
            S.dma("sp", bon[:], OZ["bon"][t0:t0 + 128, :], reads=[db("o_bon")], writes=[bonb])
            gl, glb = g_r.next()
            S.dma("sp", gl[:], OZ["gt"][t0:t0 + 128, :], reads=[db("o_gt")], writes=[glb])
            y, yb = y_r.next()
            S.dma("sp", y[:], OZ["ytok"][t0:t0 + 128, :], reads=[db("o_ytok")], writes=[yb])
            mean, meanb = sm["mean"].next()
            S.op("dve", lambda: nc.vector.tensor_reduce(out=mean[:], in_=v3(y[:]), axis=AX.X, op=ALU.add), reads=[yb], writes=[meanb])
            S.op("dve", lambda: nc.vector.tensor_scalar_mul(out=mean[:], in0=mean[:], scalar1=1.0 / 64.0), reads=[], writes=[meanb])
            S.op("dve", lambda: nc.vector.tensor_tensor(out=v3(y[:]), in0=v3(y[:]), in1=bc3(mean[:]), op=ALU.subtract), reads=[meanb], writes=[yb])
            sq, sqb = sq_r.next()
            S.op("pool", lambda: nc.gpsimd.tensor_tensor(out=sq[:], in0=y[:], in1=y[:], op=ALU.mult), reads=[yb], writes=[sqb])
            var, varb = sm["var"].next()
            S.op("dve", lambda: nc.vector.tensor_reduce(out=var[:], in_=v3(sq[:]), axis=AX.X, op=ALU.add), reads=[sqb], writes=[varb])
            S.op("act", lambda: nc.scalar.activation(out=var[:], in_=var[:], func=AF.Sqrt, bias=eps2[:, 0:1], scale=1.0 / 64.0),
                 reads=[eps2b], writes=[varb])
            S.op("dve", lambda: nc.vector.reciprocal(out=var[:], in_=var[:]), reads=[], writes=[varb])
            S.op("dve", lambda: nc.vector.tensor_tensor(out=v3(y[:]), in0=v3(y[:]), in1=bc3(var[:]), op=ALU.mult), reads=[varb], writes=[yb])
            S.op("pool", lambda: nc.gpsimd.tensor_tensor(out=y[:], in0=y[:], in1=lnw[:], op=ALU.mult), reads=[lnwb], writes=[yb])
            S.op("pool", lambda: nc.gpsimd.tensor_tensor(out=y[:], in0=y[:], in1=lnb[:], op=ALU.add), reads=[lnbb], writes=[yb])
            S.op("dve", lambda: nc.vector.tensor_tensor(out=y[:], in0=y[:], in1=bon[:], op=ALU.add), reads=[bonb], writes=[yb])
            S.op("dve", lambda: nc.vector.tensor_tensor(out=y[:], in0=y[:], in1=gl[:], op=ALU.mult), reads=[glb], writes=[yb])
            tok_to_mixT(C, y, yb, 0, t0, C.aux2, ob_r)


def dil_rope(C):
    nc, S, L = C.nc, C.S, C.L
    I, db, OZ = C.I, C.db, C.OZ
    sb, ring = C.sb, C.ring
    TT = 512
    with C.phase():
        x_r = ring(2, [128, TT], F32, "rx")
        xs_r = ring(2, [128, TT], F32, "rxs")
        c_r = ring(2, [128, TT], F32, "rc")
        s_r = ring(2, [128, TT], F32, "rs")
        o_r = ring(2, [128, TT], F32, "ro")
        ob_r = ring(2, [128, TT], BF16, "rob")
        sq_r = ring(2, [128, TT], F32, "rsq")
        n_r = ring(4, [1, TT], F32, "rn")
        kmax = sb([1, 8], F32, "kmax"); kmaxb = Buf()
        S.op("pool", lambda: nc.gpsimd.memset(kmax[:], 0.0), writes=[kmaxb])
        ng_r = ring(2, [1, TT], BF16, "rng")
        for which, src, dst in (("k", OZ["dk"], OZ["kr"]), ("q", OZ["dq"], OZ["qr"])):
            for h in range(8):
                for t0 in range(0, L, TT):
                    x, xb = x_r.next()
                    S.dma("sp", x[:], src[h * 128:(h + 1) * 128, t0:t0 + TT], reads=[db("o_d" + which)], writes=[xb])
                    xs, xsb = xs_r.next()
                    S.dma("sp", xs[0:64, :], src[h * 128 + 64:(h + 1) * 128, t0:t0 + TT], reads=[db("o_d" + which)], writes=[xsb])
                    S.dma("sp", xs[64:128, :], src[h * 128:h * 128 + 64, t0:t0 + TT], reads=[db("o_d" + which)], writes=[xsb])
                    c, cb = c_r.next()
                    S.dma("sp", c[:], I["c_cos"][:, t0:t0 + TT], writes=[cb])
                    s_, sb_ = s_r.next()
                    S.dma("sp", s_[:], I["c_sin"][:, t0:t0 + TT], writes=[sb_])
                    o, ob = o_r.next()
                    S.op("dve", lambda: nc.vector.tensor_tensor(out=o[:], in0=x[:], in1=c[:], op=ALU.mult), reads=[xb, cb], writes=[ob])
                    S.op("pool", lambda: nc.gpsimd.tensor_tensor(out=xs[:], in0=xs[:], in1=s_[:], op=ALU.mult), reads=[sb_], writes=[xsb])
                    obf, obfb = ob_r.next()
                    S.op("dve", lambda: nc.vector.tensor_tensor(out=obf[:], in0=o[:], in1=xs[:], op=ALU.add), reads=[ob, xsb], writes=[obfb])
                    S.dma("pool", dst[h * 128:(h + 1) * 128, t0:t0 + TT], obf[:], reads=[obfb], writes=[db("o_%sr" % which)])
                    sq, sqb = sq_r.next()
                    S.op("act", lambda: nc.scalar.activation(out=sq[:], in_=x[:], func=AF.Square), reads=[xb], writes=[sqb])
                    pn, pnb = C.aux.next()
                    S.op("pe", lambda: nc.tensor.matmul(pn[0:1, :TT], lhsT=C.ones_f[:, 0:1], rhs=sq[:], start=True, stop=True),
                         reads=[C.b_ones, sqb], writes=[pnb])
                    nr, nrb = n_r.next()
                    S.op("act", lambda: nc.scalar.activation(out=nr[:], in_=pn[0:1, :TT], func=AF.Sqrt), reads=[pnb], writes=[nrb])
                    if which == "k":
                        mx, mxb = n_r.next()
                        S.op("dve", lambda: nc.vector.tensor_reduce(out=mx[:, 0:1], in_=nr[:], axis=AX.X, op=ALU.max), reads=[nrb], writes=[mxb])
                        S.op("dve", lambda: nc.vector.tensor_tensor(out=kmax[:, h:h + 1], in0=kmax[:, h:h + 1], in1=mx[:, 0:1], op=ALU.max),
                             reads=[mxb], writes=[kmaxb])
                    else:
                        ng, ngb = ng_r.next()
                        S.op("dve", lambda: nc.vector.tensor_scalar(out=ng[:], in0=nr[:], scalar1=kmax[:, h:h + 1], scalar2=-1.0,
                                                                    op0=ALU.mult, op1=ALU.mult), reads=[nrb, kmaxb], writes=[ngb])
                        S.dma("pool", OZ["negc"][h:h + 1, t0:t0 + TT], ng[:], reads=[ngb], writes=[db("o_negc")])


def dil_attn(C):
    nc, S, L = C.nc, C.S, C.L
    I, db, OZ = C.I, C.db, C.OZ
    sb, ring = C.sb, C.ring
    SCALE = 128 ** -0.5
    VB = 32
    with C.phase():
        mask = sb([128, 256], BF16, "dmask"); maskb = Buf()
        mask_f = sb([128, 256], F32, "dmaskf"); maskfb = Buf()
        S.dma("sp", mask_f[:], I["c_dmask"], writes=[maskfb])
        S.op("dve", lambda: nc.vector.tensor_copy(out=mask[:], in_=mask_f[:]), reads=[maskfb], writes=[maskb])
        onesr = sb([1, 128], BF16, "onesr"); onesrb = Buf()
        S.op("pool", lambda: nc.gpsimd.memset(onesr[:], 1.0), writes=[onesrb])
        q_r = ring(1, [128, L], BF16, "dq")
        k_r = ring(1, [128, L], BF16, "dkk")
        nc_b = ring(1, [1, L], BF16, "ncb")
        vf_r = ring(2, [128, VB, 128], F32, "dvf")
        vb_r = ring(2, [128, VB, 129], BF16, "dvb")
        p_r = ring(2, [128, 256], BF16, "dp")
        pm_r = ring(2, [128, 256], BF16, "dpm")
        o_r = ring(3, [128, 129], F32, "dout")
        ps_r = Ring(C.acc.tiles)
        po_r = Ring(C.aux.tiles + C.aux2.tiles)
        for h in range(8):
            qs, qsb = q_r.next()
            S.dma("sp", qs[:], OZ["qr"][h * 128:(h + 1) * 128, :], reads=[db("o_qr")], writes=[qsb])
            ks, ksb = k_r.next()
            S.dma("sp", ks[:], OZ["kr"][h * 128:(h + 1) * 128, :], reads=[db("o_kr")], writes=[ksb])
            ncb, ncbb = nc_b.next()
            S.dma("sp", ncb[:], OZ["negc"][h:h + 1, :], reads=[db("o_negc")], writes=[ncbb])
            for br, dil in enumerate((1, 4, 16)):
                nblk = L // dil // 128
                for r in range(dil):
                    pend = None
                    for m in range(nblk):
                        if m % VB == 0:
                            nb_ = min(VB, nblk - m)
                            vf, vfb = vf_r.next()
                            src = OZ["dv"][:, h * 128:(h + 1) * 128].rearrange("(m i d) c -> d i m c", d=dil, i=128)[r]
                            S.dma("sp", vf[:, :nb_, :], src[:, m:m + nb_, :], reads=[db("o_dv")], writes=[vfb])
                            vb, vbb = vb_r.next()
                            S.op("pool", lambda: nc.gpsimd.memset(vb[:, :nb_, 128:129], 1.0), writes=[vbb])
                            S.op("pool", lambda: nc.gpsimd.tensor_copy(out=vb[:, :nb_, 0:128], in_=vf[:, :nb_, :]), reads=[vfb], writes=[vbb])
                        last = (m == nblk - 1)
                        nq = 128 if last else 256
                        base = r + dil * 128 * m
                        kv = ks[:, base: base + dil * 127 + 1: dil]
                        qv = qs[:, base: base + dil * (nq - 1) + 1: dil]
                        cv = ncb[:, base: base + dil * (nq - 1) + 1: dil]
                        psc, pscb = ps_r.next()
                        S.op("pe", lambda: nc.tensor.matmul(psc[:, :nq], lhsT=kv, rhs=qv, start=True, stop=False),
                             reads=[ksb, qsb], writes=[pscb])
                        S.op("pe", lambda: nc.tensor.matmul(psc[:, :nq], lhsT=onesr[:, :], rhs=cv, start=False, stop=True),
                             reads=[onesrb, ncbb], writes=[pscb])
                        p, pb = p_r.next()
                        S.op("act", lambda: nc.scalar.activation(out=p[:, :nq], in_=psc[:, :nq], func=AF.Exp, scale=SCALE),
                             reads=[pscb], writes=[pb])
                        pm, pmb = pm_r.next()
                        S.op("dve", lambda: nc.vector.tensor_tensor(out=pm[:, :nq], in0=p[:, :nq], in1=mask[:, :nq], op=ALU.mult),
                             reads=[pb, maskb], writes=[pmb])
                        vmm = vb[:, m % VB, :]
                        if pend is None:
                            pcur, pcurb = po_r.next()
                            S.op("pe", lambda: nc.tensor.matmul(pcur[:, :129], lhsT=pm[:, 0:128], rhs=vmm, start=True, stop=True),
                                 reads=[pmb, vbb], writes=[pcurb])
                        else:
                            pcur, pcurb = pend
                            S.op("pe", lambda: nc.tensor.matmul(pcur[:, :129], lhsT=pm[:, 0:128], rhs=vmm, start=False, stop=True),
                                 reads=[pmb, vbb], writes=[pcurb])
                        if not last:
                            pnx, pnxb = po_r.next()
                            S.op("pe", lambda: nc.tensor.matmul(pnx[:, :129], lhsT=pm[:, 128:256], rhs=vmm, start=True, stop=False),
                                 reads=[pmb, vbb], writes=[pnxb])
                            pend = (pnx, pnxb)
                        o, ob = o_r.next()
                        S.op("act", lambda: nc.scalar.copy(out=o[:], in_=pcur[:, :129]), reads=[pcurb], writes=[ob])
                        dst = OZ["acc"][br].rearrange("(m i d) hh c -> d i m hh c", d=dil, i=128)[r][:, m, h, 0:129]
                        S.dma("pool", dst, o[:], reads=[ob], writes=[db("o_acc")])


def dil_combine(C):
    nc, S, L = C.nc, C.S, C.L
    I, db, OZ = C.I, C.db, C.OZ
    sb, ring = C.sb, C.ring
    with C.phase():
        a_r = [ring(2, [128, 8, 132], F32, "ca%d" % i) for i in range(3)]
        den_r = ring(2, [128, 8], F32, "cden")
        out_r = ring(2, [128, 1024], F32, "cout")
        ob_r = ring(2, [128, 8, 128], BF16, "cob")
        for t0 in range(0, L, 128):
            A = []
            for br in range(3):
                a, ab = a_r[br].next()
                S.dma("sp", a[:], OZ["acc"][br][t0:t0 + 128], reads=[db("o_acc")], writes=[ab])
                A.append((a, ab))
            a0, a0b = A[0]
            S.op("dve", lambda: nc.vector.tensor_tensor(out=a0[:], in0=a0[:], in1=A[1][0][:], op=ALU.add), reads=[A[1][1]], writes=[a0b])
            S.op("dve", lambda: nc.vector.tensor_tensor(out=a0[:], in0=a0[:], in1=A[2][0][:], op=ALU.add), reads=[A[2][1]], writes=[a0b])
            den, denb = den_r.next()
            S.op("dve", lambda: nc.vector.reciprocal(out=den[:], in_=a0[:, :, 128]), reads=[a0b], writes=[denb])
            o, ob = out_r.next()
            S.op("dve", lambda: nc.vector.tensor_tensor(out=o[:].rearrange("p (h c) -> p h c", c=128), in0=a0[:, :, 0:128],
                                                        in1=den[:].unsqueeze(2).to_broadcast([128, 8, 128]), op=ALU.mult),
                 reads=[a0b, denb], writes=[ob])
            tok_to_mixT(C, o, ob, 1024, t0, C.aux2, ob_r)


def _prep_common(inputs, L):
    def g16(a):
        return np.ascontiguousarray(a.reshape(a.shape[0], 16, 128).transpose(0, 2, 1))
    m = {}
    for n in ("norm_mix_pre", "norm_mix_post", "norm_ffn_pre", "norm_ffn_post", "ple_norm"):
        m[n] = g16(np.asarray(inputs[n], np.float32))
    for n in ("ev_w_in", "ev_w_out", "pool_w", "od_w_in", "od_w_out", "ffn_up", "ffn_down", "ple_proj", "ple_gate"):
        m[n] = np.ascontiguousarray(np.asarray(inputs[n], np.float32))
    ps_ = np.asarray(inputs["pool_scale"], np.float32)
    m["pool_scale"] = np.ascontiguousarray(ps_.reshape(-1, 4, 128).transpose(0, 2, 1))
    gw = np.asarray(inputs["gla_gate_w2"], np.float32)
    gb = np.asarray(inputs["gla_gate_b"], np.float32)
    m["gla_gate_wb"] = np.ascontiguousarray(np.concatenate([gw, gb[:, None, :]], axis=1))
    gn = np.asarray(inputs["gla_norm"], np.float32)
    m["gla_norm"] = np.ascontiguousarray(gn.reshape(-1, 3, 128).transpose(0, 2, 1))
    mu = np.asarray(inputs["rwkv_mu"], np.float32)
    m["rwkv_mu"] = np.ascontiguousarray(mu[:, None, :])
    m["rwkv_mu_lr"] = np.ascontiguousarray(mu[:, 3072:3360, None])
    m["rwkv_w2a"] = np.ascontiguousarray(np.concatenate([np.asarray(inputs["rwkv_w2"], np.float32),
                                                         np.asarray(inputs["rwkv_w0"], np.float32)[:, None, :]], axis=1))
    m["rwkv_a2a"] = np.ascontiguousarray(np.concatenate([np.asarray(inputs["rwkv_a2"], np.float32),
                                                         np.asarray(inputs["rwkv_a0"], np.float32)[:, None, :]], axis=1))
    m["rwkv_g2"] = np.ascontiguousarray(np.asarray(inputs["rwkv_g2"], np.float32))
    for n in ("rwkv_k_k", "rwkv_k_a", "rwkv_r_k", "rwkv_ln_w", "rwkv_ln_b"):
        a = np.asarray(inputs[n], np.float32)
        m[n] = np.ascontiguousarray(a.reshape(a.shape[0], 1, 1024))
    sel = np.zeros((2, 128), np.float32)
    sel[0, :64] = 1.0
    sel[1, 64:] = 1.0
    m["c_sel"] = sel
    m["c_ident"] = np.eye(128, dtype=np.float32)
    m["c_su"] = np.triu(np.ones((64, 64), np.float32), 1)
    m["c_sl"] = np.tril(np.ones((64, 64), np.float32), -1)
    m["c_i64"] = np.eye(64, dtype=np.float32)
    half = 64
    inv = (np.float32(10000.0) ** (-np.arange(half, dtype=np.float32) / np.float32(half))).astype(np.float32)
    ang = (np.arange(L, dtype=np.float32)[:, None] * inv[None, :]).astype(np.float32)
    cos = np.cos(ang).astype(np.float32).T
    sin = np.sin(ang).astype(np.float32).T
    m["c_cos"] = np.ascontiguousarray(np.concatenate([cos, cos], axis=0))
    m["c_sin"] = np.ascontiguousarray(np.concatenate([-sin, sin], axis=0))
    kidx = np.arange(128)[:, None]
    qidx = np.arange(256)[None, :]
    m["c_dmask"] = ((qidx - kidx >= 0) & (qidx - kidx <= 128)).astype(np.float32)
    tri = np.triu(np.ones((64, 64), np.float32))
    m["c_tri"] = tri
    m["c_trimask"] = tri.copy()
    cnt = np.zeros((4, 128, 16), np.float32)
    for gi, w in enumerate((2, 4, 8, 16)):
        cnt[gi, :, :] = 1.0 / np.minimum(np.arange(1, 17), w).astype(np.float32)[None, :]
    m["c_poolcnt"] = cnt
    return m


LAYER_KEYS = ("norm_mix_pre", "norm_mix_post", "norm_ffn_pre", "norm_ffn_post", "ple_norm",
              "ffn_up", "ffn_down", "ple_proj", "ple_gate")
EVEN_KEYS = ("ev_w_in", "ev_w_out", "pool_w", "pool_scale", "gla_gate_wb", "gla_norm")
ODD_KEYS = ("od_w_in", "od_w_out", "rwkv_mu", "rwkv_mu_lr", "rwkv_w2a", "rwkv_a2a", "rwkv_g2",
            "rwkv_k_k", "rwkv_k_a", "rwkv_r_k", "rwkv_ln_w", "rwkv_ln_b")


def make_in_maps(inputs, L, B, layer=None, xT_list=None, common=None):
    if common is None:
        common = _prep_common(inputs, L)
    p = np.asarray(inputs["p"], np.float32)
    maps = []
    for b in range(B):
        if layer is None:
            m = dict(common)
            m["pT"] = np.ascontiguousarray(p[:, b].transpose(0, 2, 1))
        else:
            m = {}
            for k, v in common.items():
                if k in LAYER_KEYS:
                    m[k] = np.ascontiguousarray(v[layer:layer + 1])
                elif k in EVEN_KEYS:
                    if layer % 2 == 0:
                        m[k] = np.ascontiguousarray(v[layer // 2:layer // 2 + 1])
                elif k in ODD_KEYS:
                    if layer % 2 == 1:
                        m[k] = np.ascontiguousarray(v[layer // 2:layer // 2 + 1])
                else:
                    m[k] = v
            m["pT"] = np.ascontiguousarray(p[layer:layer + 1, b].transpose(0, 2, 1))
        if xT_list is not None:
            m["xT"] = xT_list[b]
        else:
            m["xT"] = np.ascontiguousarray(np.asarray(inputs["x"], np.float32)[b].T)
        maps.append(m)
    return maps


N_LAUNCH_MODE = 1


def kernel(**inputs):
    x = np.asarray(inputs["x"])
    B, L, _ = x.shape
    if N_LAUNCH_MODE == 1:
        nc, C = build(L)
        maps = make_in_maps(inputs, L, B)
        res = run_bass_kernel_spmd(nc, maps, core_ids=list(range(B)))
        xT = [res.results[b]["yT"] for b in range(B)]
    else:
        common = _prep_common(inputs, L)
        xT = None
        for li in range(4):
            nc, C = build(L, layers=(li,), single=True)
            maps = make_in_maps(inputs, L, B, layer=li, xT_list=xT, common=common)
            res = run_bass_kernel_spmd(nc, maps, core_ids=list(range(B)))
            xT = [np.ascontiguousarray(res.results[b]["yT"]) for b in range(B)]
            del maps, res
    out = np.stack([np.ascontiguousarray(xT[b].T) for b in range(B)], axis=0)
    return out.astype(np.float32)
```
